# Optimizing a Trainium2 kernel written in Bass

```python
import math
import jax, jax.numpy as jnp
from jax import lax
import numpy as np

D_MODEL = 2048
BATCH = 2
SEQ = 8192
DEPTH = 1
DEC_BATCH = 32
DEC_SEQ = 32
PAST_LEN = 4096

CHUNK = 64
N_HEADS = 8
D_QK = 64
D_V = 2 * D_QK
ATTN_WIDTH = N_HEADS * D_V
CONV_CH = 1024
CONV_WIDTH = 31
CONV_STATE = CONV_WIDTH - 1
ROPE_DIM = D_QK // 4
ROPE_THETA = 500000.0
D_FF = -(-8 * D_MODEL // (3 * 256)) * 256
Q_COLS = N_HEADS * 2 * D_QK
K_COLS = N_HEADS * 2 * D_QK
V_COLS = N_HEADS * D_V
CONV_COLS = 2 * CONV_CH
GATE_COLS = 2 * D_MODEL
IN_COLS = Q_COLS + K_COLS + V_COLS + CONV_COLS + GATE_COLS
Q_BLOCK = 128
EPS = 1e-6

kernel_name = "diff_attn_conformer_conv_hybrid_step"


def lambda_init(layer):
    return 0.8 - 0.6 * math.exp(-0.3 * layer)


def rms_norm(x, g):
    xf = x.astype(jnp.float32)
    y = xf * lax.rsqrt(jnp.mean(xf * xf, axis=-1, keepdims=True) + EPS)
    return (y * g.astype(jnp.float32)).astype(x.dtype)


def layer_norm(x, g, b):
    xf = x.astype(jnp.float32)
    mu = jnp.mean(xf, axis=-1, keepdims=True)
    var = jnp.mean(jnp.square(xf - mu), axis=-1, keepdims=True)
    y = (xf - mu) * lax.rsqrt(var + EPS)
    return (y * g.astype(jnp.float32) + b.astype(jnp.float32)).astype(x.dtype)


def partial_rope(x, pos):
    half = ROPE_DIM // 2
    inv_freq = ROPE_THETA ** (-jnp.arange(0, ROPE_DIM, 2, dtype=jnp.float32) / ROPE_DIM)
    ang = pos.astype(jnp.float32)[:, None] * inv_freq[None, :]
    cos = jnp.cos(ang)[:, None, None, :]
    sin = jnp.sin(ang)[:, None, None, :]
    xf = x.astype(jnp.float32)
    x1 = xf[..., :half]
    x2 = xf[..., half:ROPE_DIM]
    rot = jnp.concatenate([x1 * cos - x2 * sin, x2 * cos + x1 * sin, xf[..., ROPE_DIM:]], axis=-1)
    return rot.astype(x.dtype)


def diff_attention(q, k, v, q_pos, k_pos, lam):
    scores = jnp.einsum('bqhmd,bkhmd->bhmqk', q, k).astype(jnp.float32) * (D_QK ** -0.5)
    visible = (k_pos // CHUNK)[None, :] <= (q_pos // CHUNK)[:, None]
    scores = jnp.where(visible, scores, -jnp.inf)
    p = jax.nn.softmax(scores, axis=-1)
    a = p[:, :, 0] - lam * p[:, :, 1]
    return jnp.einsum('bhqk,bkhe->bqhe', a.astype(v.dtype), v)


def prompt_diff_attention(q, k, v, lam):
    b, t = q.shape[0], q.shape[1]
    nb = t // Q_BLOCK
    qb = q.reshape(b, nb, Q_BLOCK, N_HEADS, 2, D_QK).swapaxes(0, 1)
    pb = jnp.arange(t).reshape(nb, Q_BLOCK)
    k_pos = jnp.arange(t)

    def one_block(args):
        qi, pi = args
        return diff_attention(qi, k, v, pi, k_pos, lam)

    out = lax.map(one_block, (qb, pb))
    return out.swapaxes(0, 1).reshape(b, t, N_HEADS, D_V)


def causal_depthwise_conv(xp, w, b):
    y = lax.conv_general_dilated(xp, w[:, None, :].astype(xp.dtype), window_strides=(1,), padding='VALID',
                                 dimension_numbers=('NWC', 'WIO', 'NWC'), feature_group_count=CONV_CH)
    return y + b


def hybrid_layer(x, pos, past_k, past_v, past_conv, lam_init,
                 g_pre_mix, w_in, lq1, lk1, lq2, lk2, g_subln, w_attn_out,
                 w_dw, b_dw, g_cn, b_cn, w_conv_out, b_conv_out, w_o, g_post_mix,
                 g_pre_ffn, w_ffn_gate, w_ffn_up, w_ffn_down, g_post_ffn):
    b, t, _ = x.shape
    h = rms_norm(x, g_pre_mix)
    z = h @ w_in
    o1 = Q_COLS
    o2 = o1 + K_COLS
    o3 = o2 + V_COLS
    o4 = o3 + CONV_COLS
    q = partial_rope(z[..., :o1].reshape(b, t, N_HEADS, 2, D_QK), pos)
    k = partial_rope(z[..., o1:o2].reshape(b, t, N_HEADS, 2, D_QK), pos)
    v = z[..., o2:o3].reshape(b, t, N_HEADS, D_V)
    u = z[..., o3:o4]
    gate = z[..., o4:]

    lam = (jnp.exp(jnp.sum(lq1.astype(jnp.float32) * lk1.astype(jnp.float32)))
           - jnp.exp(jnp.sum(lq2.astype(jnp.float32) * lk2.astype(jnp.float32))) + lam_init)
    if past_k is None:
        o = prompt_diff_attention(q, k, v, lam)
    else:
        k_all = jnp.concatenate([past_k, k], axis=1)
        v_all = jnp.concatenate([past_v, v], axis=1)
        o = diff_attention(q, k_all, v_all, pos, jnp.arange(k_all.shape[1]), lam)
    o = rms_norm(o, g_subln) * (1.0 - lam_init)
    a_branch = o.reshape(b, t, ATTN_WIDTH) @ w_attn_out

    glu = u[..., :CONV_CH] * jax.nn.sigmoid(u[..., CONV_CH:])
    if past_conv is None:
        past_conv = jnp.zeros((b, CONV_STATE, CONV_CH), glu.dtype)
    cp = jnp.concatenate([past_conv.astype(glu.dtype), glu], axis=1)
    c = causal_depthwise_conv(cp, w_dw, b_dw)
    c = jax.nn.silu(layer_norm(c, g_cn, b_cn))
    c_branch = c @ w_conv_out + b_conv_out
    new_conv = cp[:, -CONV_STATE:]

    merged = jax.nn.sigmoid(gate[..., :D_MODEL]) * a_branch + jax.nn.sigmoid(gate[..., D_MODEL:]) * c_branch
    x = x + rms_norm(merged @ w_o, g_post_mix)

    h2 = rms_norm(x, g_pre_ffn)
    f = (jax.nn.silu(h2 @ w_ffn_gate) * (h2 @ w_ffn_up)) @ w_ffn_down
    x = x + rms_norm(f, g_post_ffn)
    return x, k, v, new_conv


def setup_inputs(seed: int = 0) -> dict:
    key = jax.random.key(seed)
    ks = jax.random.split(key, 32)
    f32 = jnp.float32

    def nrm(k, shape, scale):
        return jax.random.normal(k, shape, f32) * scale

    def gain(k, shape):
        return 1.0 + 0.02 * jax.random.normal(k, shape, f32)

    L = DEPTH
    return {
        "x_prompt": nrm(ks[0], (BATCH, SEQ, D_MODEL), 1.0),
        "x_sample": nrm(ks[1], (DEC_BATCH, DEC_SEQ, D_MODEL), 1.0),
        "cache_k": nrm(ks[2], (L, DEC_BATCH, PAST_LEN, N_HEADS, 2, D_QK), 1.0),
        "cache_v": nrm(ks[3], (L, DEC_BATCH, PAST_LEN, N_HEADS, D_V), 1.0),
        "state_conv": nrm(ks[4], (L, DEC_BATCH, CONV_STATE, CONV_CH), 0.5),
        "g_pre_mix": gain(ks[5], (L, D_MODEL)),
        "w_in": nrm(ks[6], (L, D_MODEL, IN_COLS), D_MODEL ** -0.5),
        "lambda_q1": nrm(ks[7], (L, D_QK), 0.1),
        "lambda_k1": nrm(ks[8], (L, D_QK), 0.1),
        "lambda_q2": nrm(ks[9], (L, D_QK), 0.1),
        "lambda_k2": nrm(ks[10], (L, D_QK), 0.1),
        "g_subln": gain(ks[11], (L, D_V)),
        "w_attn_out": nrm(ks[12], (L, ATTN_WIDTH, D_MODEL), ATTN_WIDTH ** -0.5),
        "w_dw": nrm(ks[13], (L, CONV_WIDTH, CONV_CH), CONV_WIDTH ** -0.5),
        "b_dw": nrm(ks[14], (L, CONV_CH), 0.02),
        "g_conv_norm": gain(ks[15], (L, CONV_CH)),
        "b_conv_norm": nrm(ks[16], (L, CONV_CH), 0.02),
        "w_conv_out": nrm(ks[17], (L, CONV_CH, D_MODEL), CONV_CH ** -0.5),
        "b_conv_out": nrm(ks[18], (L, D_MODEL), 0.02),
        "w_o": nrm(ks[19], (L, D_MODEL, D_MODEL), D_MODEL ** -0.5),
        "g_post_mix": gain(ks[20], (L, D_MODEL)),
        "g_pre_ffn": gain(ks[21], (L, D_MODEL)),
        "w_ffn_gate": nrm(ks[22], (L, D_MODEL, D_FF), D_MODEL ** -0.5),
        "w_ffn_up": nrm(ks[23], (L, D_MODEL, D_FF), D_MODEL ** -0.5),
        "w_ffn_down": nrm(ks[24], (L, D_FF, D_MODEL), D_FF ** -0.5),
        "g_post_ffn": gain(ks[25], (L, D_MODEL)),
    }


def reference(x_prompt, x_sample, cache_k, cache_v, state_conv,
              g_pre_mix, w_in, lambda_q1, lambda_k1, lambda_q2, lambda_k2, g_subln, w_attn_out,
              w_dw, b_dw, g_conv_norm, b_conv_norm, w_conv_out, b_conv_out, w_o, g_post_mix,
              g_pre_ffn, w_ffn_gate, w_ffn_up, w_ffn_down, g_post_ffn):
    pos_p = jnp.arange(x_prompt.shape[1])
    pos_s = cache_k.shape[2] + jnp.arange(x_sample.shape[1])
    yp = x_prompt
    ys = x_sample
    kp_l, vp_l, cp_l, ks_l, vs_l, cs_l = [], [], [], [], [], []
    for l in range(DEPTH):
        w = (g_pre_mix[l], w_in[l], lambda_q1[l], lambda_k1[l], lambda_q2[l], lambda_k2[l], g_subln[l],
             w_attn_out[l], w_dw[l], b_dw[l], g_conv_norm[l], b_conv_norm[l], w_conv_out[l], b_conv_out[l],
             w_o[l], g_post_mix[l], g_pre_ffn[l], w_ffn_gate[l], w_ffn_up[l], w_ffn_down[l], g_post_ffn[l])
        li = lambda_init(l)
        yp, kp, vp, cp = hybrid_layer(yp, pos_p, None, None, None, li, *w)
        ys, kn, vn, cn = hybrid_layer(ys, pos_s, cache_k[l], cache_v[l], state_conv[l], li, *w)
        kp_l.append(kp)
        vp_l.append(vp)
        cp_l.append(cp)
        ks_l.append(kn)
        vs_l.append(vn)
        cs_l.append(cn)
    k_prompt = jnp.stack(kp_l)
    v_prompt = jnp.stack(vp_l)
    conv_prompt = jnp.stack(cp_l)
    k_sample = jnp.stack(ks_l)
    v_sample = jnp.stack(vs_l)
    conv_sample = jnp.stack(cs_l)
    return (yp, ys, k_prompt, v_prompt, conv_prompt, k_sample, v_sample, conv_sample)
```

```python
import math
import numpy as np
import ml_dtypes
import concourse.bass as bass
import concourse.mybir as mybir
from concourse.bass_utils import run_bass_kernel_spmd

F32 = mybir.dt.float32
BF16 = mybir.dt.bfloat16
ALU = mybir.AluOpType
AF = mybir.ActivationFunctionType
AX = mybir.AxisListType

D = 2048
NCH = 16
SEQ = 8192
NH = 8
CC = 1024
DFF = 5632
NFC = 44
INC = 9216
EPS = 1e-6
TT = 256
NSB = TT // 128
NOWN = 2048 // TT
NOTH = 3 * NOWN
NTL = NOTH + NOWN
WS = 256
KC = 8
LAM_INIT = 0.8 - 0.6 * math.exp(0.0)
SKIP_STREAM_SYNC = False

P_GPRE, P_GPOST, P_GFFN, P_GPFF, P_BCO = 0, 16, 32, 48, 64
P_GSUB = 80
P_WDW = 81
P_BDW = P_WDW + 248
P_GCN = P_BDW + 8
P_BCN = P_GCN + 8
P_LAM = P_BCN + 8
NPAR = P_LAM + 256


class Prog:
    ENGS = ["pe", "act", "dve", "pool", "sp"]

    def __init__(self):
        self.ops = {e: [] for e in self.ENGS}
        self.res = {}
        self.dma_cnt = {}
        self.dma_ep = {}

    def op(self, eng, fn, reads=(), writes=(), dma=None, stream=False, join=False):
        o = {"eng": eng, "fn": fn, "deps": [], "needed": False, "dma": dma, "stream": stream}
        deps = []
        for r in reads:
            st = self.res.setdefault(r, {"w": None, "r": []})
            if st["w"] is not None:
                deps.append(st["w"])
        for w in writes:
            st = self.res.setdefault(w, {"w": None, "r": []})
            if st["w"] is not None and not (join and st["w"].get("dma_base") == dma):
                deps.append(st["w"])
            deps.extend(st["r"])
        seen = set()
        for a in deps:
            if id(a) in seen or a is o:
                continue
            seen.add(id(a))
            if a["dma"] is None and a["eng"] == eng:
                if eng in ("pe", "sp"):
                    continue
                if SKIP_STREAM_SYNC and a["stream"] and stream:
                    continue
            if a["dma"] is None:
                a["needed"] = True
            o["deps"].append(a)
        for r in reads:
            self.res[r]["r"].append(o)
        for w in writes:
            self.res[w] = {"w": o, "r": []}
        if dma is not None:
            ep = self.dma_ep.get(dma, 0)
            if not join and self.dma_cnt.get("%s_%d" % (dma, ep), 0) >= 1000:
                ep += 1
                self.dma_ep[dma] = ep
            dname = "%s_%d" % (dma, ep)
            o["dma"] = dname
            o["dma_base"] = dma
            self.dma_cnt[dname] = self.dma_cnt.get(dname, 0) + 1
            o["dmaval"] = 16 * self.dma_cnt[dname]
        self.ops[eng].append(o)
        return o

    def emit(self, nc, block):
        sems = {}
        allsems = []

        def getsem(name):
            if name not in sems:
                cm = nc.semaphore("s_" + name)
                s = cm.__enter__()
                allsems.append(cm)
                sems[name] = s
            return sems[name]

        for e in self.ENGS:
            n = 0
            ep = 0
            for o in self.ops[e]:
                if o["dma"] is None and o["needed"]:
                    if n >= 12000:
                        n = 0
                        ep += 1
                    n += 1
                    o["semval"] = n
                    o["semkey"] = "e_%s_%d" % (e, ep)
                    getsem(o["semkey"])
        for d in self.dma_cnt:
            getsem("d_" + d)

        def run(e, engobj):
            waited = {}
            for o in self.ops[e]:
                need = {}
                for a in o["deps"]:
                    if a["dma"] is not None:
                        key, val = "d_" + a["dma"], a["dmaval"]
                    else:
                        key, val = a["semkey"], a["semval"]
                    need[key] = max(need.get(key, 0), val)
                for key, val in need.items():
                    if waited.get(key, 0) >= val:
                        continue
                    waited[key] = val
                    engobj.wait_ge(sems[key], val)
                ins = o["fn"](engobj)
                if o["dma"] is not None:
                    ins.then_inc(sems["d_" + o["dma"]], 16)
                elif o["needed"]:
                    ins.then_inc(sems[o["semkey"]], 1)
            if e == "sp":
                for d, c in self.dma_cnt.items():
                    engobj.wait_ge(sems["d_" + d], 16 * c)

        block.tensor(lambda t: run("pe", t))
        block.scalar(lambda t: run("act", t))
        block.vector(lambda t: run("dve", t))
        block.gpsimd(lambda t: run("pool", t))
        block.sync(lambda t: run("sp", t))
        return allsems


def build_nc(cfg=None):
    cfg = cfg or {}
    C_HALO = cfg.get('halo', True)
    C_NOTH = cfg.get('noth', NOTH)
    C_NOWN = cfg.get('nown', NOWN)
    C_SMP = cfg.get('sample', True)
    C_TAIL = cfg.get('tail', True)
    C_SCR = cfg.get('scr', True)
    C_ROPE = cfg.get('rope', True)
    C_TR = cfg.get('tr', True)
    nc = bass.Bass("TRN2", target_bir_lowering=False)
    dt_in = lambda n, s, d=F32: nc.dram_tensor(n, list(s), d, kind="ExternalInput").ap()
    dt_out = lambda n, s: nc.dram_tensor(n, list(s), F32, kind="ExternalOutput").ap()
    x_oth = dt_in("x_oth", [NOTH * TT, D])
    x_own = dt_in("x_own", [NOWN * TT, D])
    x_halo = dt_in("x_halo", [NOWN * 32, D])
    x_smp = dt_in("x_smp", [128, D])
    rope_d = dt_in("rope", [128, NTL * NSB + 1, 16])
    flags_d = dt_in("flags", [128, NOTH])
    params_d = dt_in("params", [128, NPAR])
    consts_d = dt_in("consts", [128, 256])
    cache_k = dt_in("cache_k", [4, 4096, 1024])
    cache_v = dt_in("cache_v", [4, 4096, 1024])
    state_conv = dt_in("state_conv", [4, 30, CC])
    w_in = dt_in("w_in", [D, INC])
    w_ao = dt_in("w_ao", [CC, D])
    w_co = dt_in("w_co", [CC, D])
    w_o = dt_in("w_o", [D, D])
    w_fg = dt_in("w_fg", [D, DFF])
    w_fu = dt_in("w_fu", [D, DFF])
    w_fd = dt_in("w_fd", [DFF, D])
    y_own = dt_out("y_own", [NOWN * TT, D])
    k_own = dt_out("k_own", [NOWN * TT, 1024])
    v_own = dt_out("v_own", [NOWN * TT, 1024])
    conv_p = dt_out("conv_p", [30, CC])
    y_s = dt_out("y_s", [128, D])
    k_s = dt_out("k_s", [128, 1024])
    v_s = dt_out("v_s", [128, 1024])
    conv_s = dt_out("conv_s", [4, 30, CC])
    kT_scr = nc.dram_tensor("kT_scr", [NH, 128, NTL, TT], BF16, kind="Internal").ap()
    v_scr = nc.dram_tensor("v_scr", [NH, 128, NTL, NSB, 128], BF16, kind="Internal").ap()
    vm_scr = nc.dram_tensor("vm_scr", [NH, 128, NOTH, NSB, 128], BF16, kind="Internal").ap()

    P = Prog()
    cms = []

    def sb(name, shape, dt):
        cm = nc.sbuf_tensor(name, list(shape), dt)
        t = cm.__enter__()
        cms.append(cm)
        return t

    ps = []
    for i in range(8):
        cm = nc.psum_tensor("ps%d" % i, [128, 512], F32)
        ps.append(cm.__enter__())
        cms.append(cm)
    PSN = ["ps%d" % i for i in range(8)]

    TM = 128 + TT
    xT = sb("xT", [128, NCH, TT], F32)
    hT = sb("hT", [128, NCH, TT], BF16)
    bufA = sb("bufA", [128, NCH, TT], F32)
    xin = [sb("xin0", [128, D], F32)] * 2
    kvst = [sb("kvst0", [128, 2048], F32)] * 2
    NWSL = 3
    NSTG = 3
    wsl = [sb("wsl%d" % i, [128, NCH, WS], BF16) for i in range(NWSL)]
    stg = [sb("stg%d" % i, [128, 1024], F32) for i in range(NSTG)]
    QT = sb("QT", [128, 2, NH, TT], BF16)
    oT = sb("oT", [128, NH, TT], BF16)
    mrgT = sb("mrgT", [128, NCH, TT], BF16)
    arena = sb("arena", [128, NFC * TT], BF16)
    actT = arena[:, :].rearrange("p (c t) -> p c t", t=TT)
    _e0 = 8 * (32 + TT) * 2
    ext = arena[:, 0:_e0].bitcast(F32).rearrange("p (c t) -> p c t", t=32 + TT)
    _a0 = _e0 + 8 * TT * 2
    acc = arena[:, _e0:_a0].bitcast(F32).rearrange("p (c t) -> p c t", t=TT)
    cnT = arena[:, _a0:_a0 + 8 * TT].rearrange("p (c t) -> p c t", t=TT)
    assert _a0 + 8 * TT <= NFC * TT
    KTc = [sb("KTc%d" % i, [128, KC, TT], BF16) for i in range(2)]
    Vc = [sb("Vc%d" % i, [128, KC * NSB, 128], BF16) for i in range(2)]
    Pt = [sb("Pt%d" % i, [128, 512], BF16) for i in range(2)]
    tmpf = [sb("tmpf%d" % i, [128, 512], F32) for i in range(4)]
    qtmp = sb("qtmp", [128, 1024], F32)
    qbf = sb("qbf", [128, 1024], BF16)
    kbf = sb("kbf", [128, 1024], BF16)
    vbf = sb("vbf", [128, 1024], BF16)
    vmf = sb("vmf", [128, 1024], BF16)
    ktr = sb("ktr", [128, NH, 128], BF16)
    rtmp = [sb("rtmp%d" % i, [128, 16, 8], F32) for i in range(4)]
    tA = sb("tA", [128, 2, TT], F32)
    tB = sb("tB", [128, 2, TT], F32)
    rstd = sb("rstd", [128, TT], F32)
    mean = sb("mean", [128, TT], F32)
    par = sb("par", [128, NPAR], F32)
    rope = sb("rope_sb", [128, NTL * NSB + 1, 16], F32)
    flg = sb("flg", [128, NOTH], F32)
    cst = sb("cst_sb", [128, 256], F32)
    identb = sb("identb", [128, 128], BF16)
    onesb = sb("onesb", [128, 128], BF16)
    onesf = sb("onesf", [128, NOTH, 128], BF16)
    halo = sb("halo", [128, 8, NOWN, 32], F32)
    sc = sb("sc", [128, 16], F32)
    zrhs = sb("zrhs", [128, 512], BF16)
    ckb = KTc[0][:, :, :].rearrange("p a b -> p (a b)")[:, 0:1024]
    cvb = [Vc[i][:, :, :].rearrange("p a b -> p (a b)")[:, 0:1024] for i in range(2)]
    ckT = KTc[1][:, :, 0:128]
    stc = kvst[0][0:32, 0:CC]
    identf = cst[:, 0:128]
    onesF = cst[:, 128:256]

    state = {"ps": 0, "ws": 0, "xin": 0, "kv": 0, "ev": 0, "stg": 0, "ce": 0}

    def nps():
        i = state["ps"]
        state["ps"] = (i + 1) % 6
        return i + 2

    def evac_eng():
        state["ev"] ^= 1
        return "act" if state["ev"] else "dve"

    def copy_op(eng, out, in_, reads, writes, stream=True):
        if eng == "act":
            P.op("act", lambda e: e.copy(out=out, in_=in_), reads, writes, stream=stream)
        else:
            P.op(eng, lambda e: e.tensor_copy(out=out, in_=in_), reads, writes, stream=stream)

    ACCN = ["acc%d" % cb for cb in range(8)]

    def col(c):
        return par[:, c:c + 1]

    P.op("sp", lambda e: e.dma_start(out=par[:], in_=params_d), writes=["par"], dma="c0")
    P.op("sp", lambda e: e.dma_start(out=rope[:], in_=rope_d), writes=["rope"], dma="c1")
    P.op("sp", lambda e: e.dma_start(out=flg[:], in_=flags_d), writes=["flg"], dma="c2")
    P.op("sp", lambda e: e.dma_start(out=cst[:], in_=consts_d), writes=["cst"], dma="c3")
    copy_op("dve", identb[:], identf, ["cst"], ["identb"], stream=False)
    P.op("dve", lambda e: e.memset(zrhs[:, :], 0.0), [], ["zrhs"], stream=False)
    P.op("dve", lambda e: e.memset(QT[:, :, :, :].rearrange("p m h t -> p (m h t)"), 0.0), [], ["QT"], stream=False)
    copy_op("dve", onesb[:], onesF, ["cst"], ["onesb"], stream=False)
    for o in range(NOTH):
        P.op("dve", lambda e, o=o: e.tensor_scalar(out=onesf[:, o, :], in0=onesF, scalar1=flg[:, o:o + 1],
                                                   scalar2=None, op0=ALU.mult),
             ["cst", "flg"], ["onesf"], stream=False)
    lp = tmpf[0]
    P.op("dve", lambda e: e.tensor_tensor(out=lp[:, 0:64], in0=par[:, P_LAM:P_LAM + 64], in1=par[:, P_LAM + 64:P_LAM + 128], op=ALU.mult),
         ["par"], ["tmpf0"], stream=False)
    P.op("dve", lambda e: e.tensor_tensor(out=lp[:, 64:128], in0=par[:, P_LAM + 128:P_LAM + 192], in1=par[:, P_LAM + 192:P_LAM + 256], op=ALU.mult),
         ["par", "tmpf0"], ["tmpf0"], stream=False)
    P.op("dve", lambda e: e.tensor_reduce(out=sc[:, 3:4], in_=lp[:, 0:64], axis=AX.X, op=ALU.add), ["tmpf0"], ["sc"], stream=False)
    P.op("dve", lambda e: e.tensor_reduce(out=sc[:, 4:5], in_=lp[:, 64:128], axis=AX.X, op=ALU.add), ["tmpf0", "sc"], ["sc"], stream=False)
    P.op("act", lambda e: e.activation(out=sc[:, 5:7], in_=sc[:, 3:5], func=AF.Exp), ["sc"], ["sc"], stream=False)
    P.op("dve", lambda e: e.tensor_tensor(out=sc[:, 0:1], in0=sc[:, 5:6], in1=sc[:, 6:7], op=ALU.subtract), ["sc"], ["sc"], stream=False)
    P.op("dve", lambda e: e.tensor_scalar(out=sc[:, 1:2], in0=sc[:, 0:1], scalar1=LAM_INIT, scalar2=-1.0, op0=ALU.add, op1=ALU.mult),
         ["sc"], ["sc"], stream=False)
    P.op("dve", lambda e: e.tensor_scalar(out=sc[:, 2:3], in0=col(P_GSUB), scalar1=1.0 - LAM_INIT, scalar2=None, op0=ALU.mult),
         ["sc", "par"], ["sc"], stream=False)
    P.op("dve", lambda e: e.memset(sc[:, 8:9], EPS), ["sc"], ["sc"], stream=False)
    epsc = sc[:, 8:9]
    neglam = sc[:, 1:2]
    gsub = sc[:, 2:3]

    def rsqrt_op(dst, src, reads, dstname, scale):
        P.op("act", lambda e: e.activation(out=dst, in_=src, func=AF.Sqrt, bias=epsc, scale=scale), list(reads) + ["sc"], [dstname], stream=False)
        P.op("dve", lambda e: e.reciprocal(out=dst, in_=dst), [dstname], [dstname], stream=False)

    def wres(slot):
        return ["wsl%dq%d" % (slot, q) for q in range(4)]

    CAST_ENGS = ["pool", "pool", "pool"]

    def load_cast(dst, src, dstname, n_inner=None):
        st = state["stg"]
        state["stg"] = (st + 1) % NSTG
        sv = stg[st][:, :]
        if n_inner is not None:
            sv = stg[st][:, 0:dst.shape[1] * n_inner].rearrange("p (c n) -> p c n", n=n_inner)
        else:
            sv = stg[st][:, 0:dst.shape[1]]
        P.op("sp", lambda e: e.dma_start(out=sv, in_=src), writes=["stg%d" % st], dma="g%d" % st)
        eng = CAST_ENGS[state["ce"] % 3]
        state["ce"] += 1
        copy_op(eng, dst, sv, ["stg%d" % st], [dstname])

    def load_w(src, k0, nk, c0, ncols=WS):
        s = state["ws"]
        state["ws"] = (s + 1) % NWSL
        for qi, q0 in enumerate(range(0, nk, 4)):
            q1 = min(nk, q0 + 4)
            load_cast(wsl[s][:, q0:q1, 0:ncols],
                      src[(k0 + q0) * 128:(k0 + q1) * 128, c0:c0 + ncols].rearrange("(c p) n -> p c n", p=128),
                      "wsl%dq%d" % (s, qi), n_inner=ncols)
        return s

    def norm_stats(srcT, srcname, nch, ntok, inv_n, tagps):
        b = nps()
        for c in range(nch):
            t = c % 2
            P.op("act", lambda e, c=c, t=t: e.activation(out=tmpf[t][:, 0:ntok], in_=srcT[:, c, 0:ntok], func=AF.Square),
                 [srcname], ["tmpf%d" % t], stream=True)
            P.op("pe", lambda e, c=c, t=t, b=b: e.matmul(ps[b][:, 0:ntok], lhsT=onesF, rhs=tmpf[t][:, 0:ntok],
                                                         start=(c == 0), stop=(c == nch - 1)),
                 ["tmpf%d" % t, "cst"], [PSN[b]])
        rsqrt_op(rstd[:, 0:ntok], ps[b][:, 0:ntok], [PSN[b]], "rstd", inv_n)

    def load_xT(xsrc, row0, nsub, ntok):
        for s in range(nsub):
            xi = 0
            P.op("sp", lambda e, xi=xi, s=s: e.dma_start(out=xin[xi][:], in_=xsrc[row0 + s * 128: row0 + (s + 1) * 128, :]),
                 writes=["xin%d" % xi], dma="x%d" % xi)
            for c4 in range(4):
                b = nps()
                for k in range(4):
                    c = c4 * 4 + k
                    P.op("pe", lambda e, xi=xi, c=c, k=k, b=b: e.transpose(out=ps[b][:, k * 128:(k + 1) * 128],
                                                                           in_=xin[xi][:, c * 128:(c + 1) * 128], identity=identf),
                         ["xin%d" % xi, "cst"], [PSN[b]])
                copy_op(evac_eng(), xT[:, c4 * 4:c4 * 4 + 4, s * 128:(s + 1) * 128],
                        ps[b][:, :].rearrange("p (c t) -> p c t", t=128), [PSN[b]], ["xT"])

    def make_hT(gbase, ntok):
        norm_stats(xT, "xT", NCH, ntok, 1.0 / D, None)
        for c in range(NCH):
            P.op("dve", lambda e, c=c: e.scalar_tensor_tensor(out=hT[:, c, 0:ntok], in0=xT[:, c, 0:ntok], scalar=col(gbase + c),
                                                              in1=rstd[:, 0:ntok], op0=ALU.mult, op1=ALU.mult),
                 ["xT", "rstd", "par"], ["hT"], stream=True)

    def tokmajor_mm(s, slot, ncols=WS):
        b = nps()
        for c in range(NCH):
            P.op("pe", lambda e, c=c, b=b: e.matmul(ps[b][:, 0:ncols], lhsT=hT[:, c, s * 128:(s + 1) * 128], rhs=wsl[slot][:, c, 0:ncols],
                                                    start=(c == 0), stop=(c == NCH - 1)),
                 (["hT"] + wres(slot)), [PSN[b]])
        return b

    def featmajor_mm(srcT, srcname, nk, slot, blk, ntok, wchunk0=0):
        b = nps()
        for c in range(nk):
            P.op("pe", lambda e, c=c, b=b: e.matmul(ps[b][:, 0:ntok], lhsT=wsl[slot][:, c, blk * 128:(blk + 1) * 128],
                                                    rhs=srcT[:, wchunk0 + c, 0:ntok], start=(c == 0), stop=(c == nk - 1)),
                 ([srcname] + wres(slot)), [PSN[b]])
        return b

    def rope_evac(b, dst, dstname, rsub, nblk):
        w = nblk * 64
        copy_op("act", dst, ps[b][:, 0:w], [PSN[b]], [dstname])
        if not C_ROPE:
            return
        dv = dst.rearrange("p (b d) -> p b d", d=64)
        pv = dv
        cosb = rope[:, rsub, 0:8].unsqueeze(1).to_broadcast([128, nblk, 8])
        sinb = rope[:, rsub, 8:16].unsqueeze(1).to_broadcast([128, nblk, 8])
        x1, x2 = pv[:, :, 0:8], pv[:, :, 8:16]
        r = [t[:, 0:nblk, :] for t in rtmp]
        rn = ["rtmp%d" % i for i in range(4)]
        P.op("dve", lambda e: e.tensor_tensor(out=r[0], in0=x1, in1=cosb, op=ALU.mult), [dstname, "rope"], [rn[0]], stream=False)
        P.op("dve", lambda e: e.tensor_tensor(out=r[1], in0=x2, in1=sinb, op=ALU.mult), [dstname, "rope"], [rn[1]], stream=False)
        P.op("dve", lambda e: e.tensor_tensor(out=r[2], in0=x2, in1=cosb, op=ALU.mult), [dstname, "rope"], [rn[2]], stream=False)
        P.op("dve", lambda e: e.tensor_tensor(out=r[3], in0=x1, in1=sinb, op=ALU.mult), [dstname, "rope"], [rn[3]], stream=False)
        P.op("dve", lambda e: e.tensor_tensor(out=dv[:, :, 0:8], in0=r[0], in1=r[1], op=ALU.subtract), [rn[0], rn[1], dstname], [dstname], stream=False)
        P.op("dve", lambda e: e.tensor_tensor(out=dv[:, :, 8:16], in0=r[2], in1=r[3], op=ALU.add), [rn[2], rn[3], dstname], [dstname], stream=False)

    def transpose8(srcbf, srcname, dst, dstname, nrows=128):
        b = nps()
        pb = ps[b][:, :].bitcast(BF16)
        for h in range(NH):
            P.op("pe", lambda e, h=h: e.transpose(out=pb[:, h * 128:h * 128 + nrows], in_=srcbf[0:nrows, h * 128:(h + 1) * 128],
                                                  identity=identb[0:nrows, 0:nrows]),
                 [srcname, "identb"], [PSN[b]])
        pb3 = pb.rearrange("p (h t) -> p h t", t=128)[:, :, 0:nrows]
        if isinstance(dst, tuple):
            eng = evac_eng()
            copy_op(eng, dst[0], pb3[0:64], [PSN[b]], [dstname])
            copy_op(eng, dst[1], pb3[64:128], [PSN[b]], [dstname])
        else:
            copy_op(evac_eng(), dst, pb3, [PSN[b]], [dstname])

    def qkv_phase(tslot, rsub0, nsub, kout, vout, orow0, do_q, flag_o=None, ntok_rows=128):
        groups = ([("q", 0), ("q", 1), ("q", 2), ("q", 3)] if do_q else []) + [("k", 0), ("k", 1), ("k", 2), ("k", 3)] + [("v", 0), ("v", 1), ("v", 2), ("v", 3)]
        groups = groups[:cfg.get('qg', 99)]
        slabs = {}
        for s in range(nsub):
            kv = state["kv"]
            state["kv"] ^= 1
            kst = kvst[kv][:, 0:1024]
            vst = kvst[kv][:, 1024:2048]
            kvn = "kvst0"
            for (kind, g) in groups:
                c0 = {"q": 0, "k": 1024, "v": 2048}[kind] + g * WS
                slot = load_w(w_in, 0, NCH, c0)
                b = tokmajor_mm(s, slot)
                if kind == "q":
                    rope_evac(b, qtmp[:, g * WS:(g + 1) * WS], "qtmp", rsub0 + s, WS // 64)
                elif kind == "k":
                    rope_evac(b, kst[:, g * WS:(g + 1) * WS], kvn, rsub0 + s, WS // 64)
                else:
                    copy_op("act", vst[:, g * WS:(g + 1) * WS], ps[b][:, 0:WS], [PSN[b]], [kvn])
                    copy_op("dve", vbf[:, g * WS:(g + 1) * WS], vst[:, g * WS:(g + 1) * WS], [kvn], ["vbf"])
            if do_q:
                copy_op("dve", qbf[:], qtmp[:], ["qtmp"], ["qbf"])
                transpose8(qbf, "qbf", (QT[0:64, 0, :, s * 128:(s + 1) * 128], QT[64:128, 1, :, s * 128:(s + 1) * 128]), "QT")
            copy_op("act", kbf[:], kst, [kvn], ["kbf"])
            if kout is not None:
                P.op("act", lambda e, s=s, kst=kst: e.dma_start(out=kout[orow0 + s * 128: orow0 + (s + 1) * 128, :], in_=kst),
                     reads=[kvn], dma="ok")
                P.op("act", lambda e, s=s, vst=vst: e.dma_start(out=vout[orow0 + s * 128: orow0 + (s + 1) * 128, :], in_=vst),
                     reads=[kvn], dma="ov")
            if tslot is not None and C_TR:
                transpose8(kbf, "kbf", ktr[:, :, :], "ktr")
            if tslot is not None and C_SCR:
                for hh in (0, 4):
                    P.op("act", lambda e, s=s, hh=hh: e.dma_start(out=kT_scr[hh:hh + 4, :, tslot, s * 128:(s + 1) * 128].rearrange("h p t -> p h t"), in_=ktr[:, hh:hh + 4, :]),
                         reads=["ktr"], writes=["kTs%d" % tslot], dma="sk", join=True)
                    P.op("act", lambda e, s=s, hh=hh: e.dma_start(out=v_scr[hh:hh + 4, :, tslot, s, :].rearrange("h p e -> p h e"),
                                                                 in_=vbf[:, hh * 128:(hh + 4) * 128].rearrange("p (h e) -> p h e", e=128)),
                         reads=["vbf"], writes=["vs%d" % tslot], dma="sv", join=True)
                if flag_o is not None:
                    P.op("dve", lambda e: e.tensor_scalar(out=vmf[:], in0=vbf[:], scalar1=flg[:, flag_o:flag_o + 1], scalar2=None, op0=ALU.mult),
                         ["vbf", "flg"], ["vmf"], stream=True)
                    for hh in (0, 4):
                        P.op("act", lambda e, s=s, hh=hh: e.dma_start(out=vm_scr[hh:hh + 4, :, flag_o, s, :].rearrange("h p e -> p h e"),
                                                                     in_=vmf[:, hh * 128:(hh + 4) * 128].rearrange("p (h e) -> p h e", e=128)),
                             reads=["vmf"], writes=["vms%d" % flag_o], dma="sm", join=True)

    def glu_phase(ntok, dstfn, dstname, inview=lambda a: a):
        for g in range(CC // WS):
            sa = load_w(w_in, 0, NCH, 3072 + g * WS)
            sbb = load_w(w_in, 0, NCH, 4096 + g * WS)
            GL = cfg.get('glu', 4)
            for blk in range(WS // 128):
                cb = g * (WS // 128) + blk
                if GL < 2:
                    continue
                ba = featmajor_mm(hT, "hT", NCH, sa, blk, ntok)
                bb = featmajor_mm(hT, "hT", NCH, sbb, blk, ntok)
                if GL < 3:
                    continue
                P.op("act", lambda e, bb=bb: e.activation(out=tmpf[2][:, 0:ntok], in_=ps[bb][:, 0:ntok], func=AF.Sigmoid),
                     [PSN[bb]], ["tmpf2"], stream=True)
                if GL < 4:
                    continue
                if cfg.get('gdst') == 'tmp':
                    P.op("dve", lambda e, ba=ba, cb=cb: e.tensor_tensor(out=tmpf[3][:, 0:ntok], in0=ps[ba][:, 0:ntok], in1=tmpf[2][:, 0:ntok], op=ALU.mult),
                         [PSN[ba], "tmpf2"], ["tmpf3"], stream=True)
                    continue
                if cfg.get('gdst') == 'sb':
                    copy_op("act", tmpf[3][:, 0:ntok], ps[ba][:, 0:ntok], [PSN[ba]], ["tmpf3"])
                    P.op("dve", lambda e, ba=ba, cb=cb: e.tensor_tensor(out=dstfn(cb), in0=inview(tmpf[3][:, 0:ntok]), in1=inview(tmpf[2][:, 0:ntok]), op=ALU.mult),
                         ["tmpf3", "tmpf2"], [dstname], stream=True)
                    continue
                P.op("dve", lambda e, ba=ba, cb=cb: e.tensor_tensor(out=dstfn(cb), in0=inview(ps[ba][:, 0:ntok]), in1=inview(tmpf[2][:, 0:ntok]), op=ALU.mult),
                     [PSN[ba], "tmpf2"], [dstname], stream=True)

    def conv_phase(extv, accv, ntok_shape):
        for cb in range(8):
            P.op("dve", lambda e, cb=cb: e.tensor_scalar(out=accv(cb), in0=extv(cb, 0), scalar1=col(P_WDW + cb), scalar2=col(P_BDW + cb),
                                                          op0=ALU.mult, op1=ALU.add), ["ext", "par"], ["acc%d" % cb], stream=True)
        for k in range(1, 31):
            for cb in range(8):
                P.op("dve", lambda e, cb=cb, k=k: e.scalar_tensor_tensor(out=accv(cb), in0=extv(cb, k), scalar=col(P_WDW + k * 8 + cb),
                                                                          in1=accv(cb), op0=ALU.mult, op1=ALU.add),
                     ["ext", "acc%d" % cb, "par"], ["acc%d" % cb], stream=True)

    def conv_norm(ntok):
        b1 = nps()
        b2 = nps()
        for cb in range(8):
            P.op("pe", lambda e, cb=cb: e.matmul(ps[b1][:, 0:ntok], lhsT=onesF, rhs=acc[:, cb, 0:ntok], start=(cb == 0), stop=(cb == 7)),
                 ["acc%d" % cb, "cst"], [PSN[b1]])
        for cb in range(8):
            t = cb % 2
            P.op("act", lambda e, cb=cb, t=t: e.activation(out=tmpf[t][:, 0:ntok], in_=acc[:, cb, 0:ntok], func=AF.Square),
                 ["acc%d" % cb], ["tmpf%d" % t], stream=True)
            P.op("pe", lambda e, cb=cb, t=t: e.matmul(ps[b2][:, 0:ntok], lhsT=onesF, rhs=tmpf[t][:, 0:ntok], start=(cb == 0), stop=(cb == 7)),
                 ["tmpf%d" % t, "cst"], [PSN[b2]])
        P.op("dve", lambda e: e.tensor_scalar(out=mean[:, 0:ntok], in0=ps[b1][:, 0:ntok], scalar1=1.0 / CC, scalar2=None, op0=ALU.mult),
             [PSN[b1]], ["mean"], stream=False)
        P.op("dve", lambda e: e.tensor_tensor(out=tmpf[3][:, 0:ntok], in0=mean[:, 0:ntok], in1=mean[:, 0:ntok], op=ALU.mult),
             ["mean"], ["tmpf3"], stream=False)
        P.op("dve", lambda e: e.scalar_tensor_tensor(out=rstd[:, 0:ntok], in0=ps[b2][:, 0:ntok], scalar=1.0 / CC, in1=tmpf[3][:, 0:ntok],
                                                     op0=ALU.mult, op1=ALU.subtract), [PSN[b2], "tmpf3"], ["rstd"], stream=False)
        rsqrt_op(rstd[:, 0:ntok], rstd[:, 0:ntok], ["rstd"], "rstd", 1.0)
        for cb in range(8):
            if cfg.get('convl', 3) < 3:
                break
            t = 2 + cb % 2
            P.op("dve", lambda e, cb=cb, t=t: e.tensor_tensor(out=tmpf[t][:, 0:ntok], in0=acc[:, cb, 0:ntok], in1=mean[:, 0:ntok], op=ALU.subtract),
                 ["acc%d" % cb, "mean"], ["tmpf%d" % t], stream=False)
            P.op("dve", lambda e, cb=cb, t=t: e.tensor_tensor(out=tmpf[t][:, 0:ntok], in0=tmpf[t][:, 0:ntok], in1=rstd[:, 0:ntok], op=ALU.mult),
                 ["tmpf%d" % t, "rstd"], ["tmpf%d" % t], stream=False)
            if cfg.get('silu', 'split') == 'fused':
                P.op("act", lambda e, cb=cb, t=t: e.activation(out=cnT[:, cb, 0:ntok], in_=tmpf[t][:, 0:ntok], func=AF.Silu,
                                                               bias=col(P_BCN + cb), scale=col(P_GCN + cb)),
                     ["tmpf%d" % t, "par"], ["cnT"], stream=False)
            else:
                P.op("act", lambda e, cb=cb, t=t: e.activation(out=tmpf[t][:, 0:ntok], in_=tmpf[t][:, 0:ntok], func=AF.Identity,
                                                               bias=col(P_BCN + cb), scale=col(P_GCN + cb)),
                     ["tmpf%d" % t, "par"], ["tmpf%d" % t], stream=False)
                P.op("act", lambda e, cb=cb, t=t: e.activation(out=tmpf[t - 2][:, 0:ntok], in_=tmpf[t][:, 0:ntok], func=AF.Sigmoid),
                     ["tmpf%d" % t], ["tmpf%d" % (t - 2)], stream=False)
                P.op("dve", lambda e, cb=cb, t=t: e.tensor_tensor(out=cnT[:, cb, 0:ntok], in0=tmpf[t][:, 0:ntok], in1=tmpf[t - 2][:, 0:ntok], op=ALU.mult),
                     ["tmpf%d" % t, "tmpf%d" % (t - 2)], ["cnT"], stream=False)

    def head_finish(o_dst, w):
        P.op("dve", lambda e: e.reciprocal(out=tmpf[0][:, 0:2 * w], in_=ps[1][:, 0:2 * w]), [PSN[1]], ["tmpf0"], stream=False)
        P.op("dve", lambda e: e.tensor_tensor(out=tmpf[1][:, 0:2 * w], in0=ps[0][:, 0:2 * w], in1=tmpf[0][:, 0:2 * w], op=ALU.mult),
             [PSN[0], "tmpf0"], ["tmpf1"], stream=False)
        P.op("dve", lambda e: e.scalar_tensor_tensor(out=tmpf[2][:, 0:w], in0=tmpf[1][:, w:2 * w], scalar=neglam, in1=tmpf[1][:, 0:w],
                                                     op0=ALU.mult, op1=ALU.add), ["tmpf1", "sc"], ["tmpf2"], stream=False)
        P.op("act", lambda e: e.activation(out=tmpf[3][:, 0:w], in_=tmpf[2][:, 0:w], func=AF.Square), ["tmpf2"], ["tmpf3"], stream=False)
        b = nps()
        P.op("pe", lambda e: e.matmul(ps[b][:, 0:w], lhsT=onesF, rhs=tmpf[3][:, 0:w], start=True, stop=True), ["tmpf3", "cst"], [PSN[b]])
        rsqrt_op(tmpf[0][:, 0:w], ps[b][:, 0:w], [PSN[b]], "tmpf0", 1.0 / 128)
        P.op("dve", lambda e: e.scalar_tensor_tensor(out=o_dst, in0=tmpf[2][:, 0:w], scalar=gsub, in1=tmpf[0][:, 0:w], op0=ALU.mult, op1=ALU.mult),
             ["tmpf2", "tmpf0", "sc"], ["oT"], stream=False)

    def prompt_attention(i):
        ktiles = [("o", o) for o in range(3 * i + 3)] + [("w", s) for s in range(i + 1)]
        ATL = cfg.get('attl', 4)
        nkt = len(ktiles)
        for h in range(NH):
            first = True
            sbi = 0
            for c0 in range(0, nkt, KC):
                chunk = ktiles[c0:c0 + KC]
                cb_ = (c0 // KC) % 2
                kres, vres = "KTc%d" % cb_, "Vc%d" % cb_
                runs = []
                for idx, (kind, n) in enumerate(chunk):
                    ksl = n if kind == "o" else NOTH + n
                    vsrc = ("vm", n) if (kind == "o" and n >= 3 * i) else ("v", ksl)
                    runs.append((idx, ksl, vsrc))
                j = 0
                while j < len(runs):
                    j2 = j
                    while j2 + 1 < len(runs) and runs[j2 + 1][1] == runs[j2][1] + 1:
                        j2 += 1
                    a, bnd = runs[j][1], runs[j2][1] + 1
                    P.op("sp", lambda e, j=j, a=a, bnd=bnd, h=h, cb_=cb_: e.dma_start(out=KTc[cb_][:, j:j + bnd - a, :], in_=kT_scr[h, :, a:bnd, :]),
                         reads=["kTs%d" % t for t in range(a, bnd)], writes=[kres], dma="lk%d" % cb_)
                    j = j2 + 1
                j = 0
                while j < len(runs):
                    j2 = j
                    while j2 + 1 < len(runs) and runs[j2 + 1][2][0] == runs[j2][2][0] and runs[j2 + 1][2][1] == runs[j2][2][1] + 1:
                        j2 += 1
                    kindv, a = runs[j][2]
                    bnd = runs[j2][2][1] + 1
                    srcv = vm_scr if kindv == "vm" else v_scr
                    rn = [("vms%d" if kindv == "vm" else "vs%d") % t for t in range(a, bnd)]
                    P.op("sp", lambda e, j=j, a=a, bnd=bnd, h=h, cb_=cb_, srcv=srcv: e.dma_start(
                        out=Vc[cb_][:, j * NSB:(j + bnd - a) * NSB, :], in_=srcv[h, :, a:bnd, :, :].rearrange("p t s e -> p (t s) e")),
                         reads=rn, writes=[vres], dma="lv%d" % cb_)
                    j = j2 + 1
                for idx, (kind, n) in enumerate(chunk):
                    diag = (kind == "w" and n == i)
                    lones = onesf[:, n, :] if (kind == "o" and n >= 3 * i) else onesb[:, :]
                    lon = "onesf" if (kind == "o" and n >= 3 * i) else "onesb"
                    for sbk in range(NSB):
                        last = (c0 + idx == nkt - 1) and (sbk == NSB - 1)
                        q0 = sbk * 128 if diag else 0
                        wq = TT - q0
                        sbank = 2 + (sbi % 2)
                        pslot = sbi % 2
                        sbi += 1
                        sv = ps[sbank][:, :].rearrange("p (m q) -> p m q", m=2)
                        for m in range((cfg.get('nm', 2)) if ATL >= 2 else 0):
                            P.op("pe", lambda e, m=m, idx=idx, sbk=sbk, q0=q0, cb_=cb_, sbank=sbank, h=h: e.matmul(
                                ps[sbank][:, m * TT + q0:(m + 1) * TT],
                                lhsT=KTc[cb_][:, idx, sbk * 128:(sbk + 1) * 128],
                                rhs=QT[:, m, h, q0:TT], start=True, stop=True),
                                 [kres, "QT"], [PSN[sbank]])
                        if ATL < 2:
                            continue
                        pv = Pt[pslot][:, 0:2 * TT].rearrange("p (m q) -> p m q", m=2)
                        P.op("act", lambda e, q0=q0, pv=pv, sv=sv: e.activation(out=pv[:, :, q0:TT], in_=sv[:, :, q0:TT], func=AF.Exp, scale=0.125),
                             [PSN[sbank]], ["Pt%d" % pslot], stream=True)
                        if diag:
                            P.op("dve", lambda e, q0=q0, pv=pv: e.memset(pv[64:128, :, q0:q0 + 64], 0.0), ["Pt%d" % pslot], ["Pt%d" % pslot], stream=False)
                        if ATL < 3:
                            continue
                        if q0 == 0:
                            P.op("pe", lambda e, idx=idx, sbk=sbk, cb_=cb_, pslot=pslot, first=first, last=last: e.matmul(
                                ps[0][:, 0:2 * TT], lhsT=Vc[cb_][:, idx * NSB + sbk, :], rhs=Pt[pslot][:, 0:2 * TT], start=first, stop=last),
                                 [vres, "Pt%d" % pslot], [PSN[0]])
                            P.op("pe", lambda e, pslot=pslot, first=first, last=last, lones=lones: e.matmul(
                                ps[1][:, 0:2 * TT], lhsT=lones, rhs=Pt[pslot][:, 0:2 * TT], start=first, stop=last),
                                 [lon, "Pt%d" % pslot], [PSN[1]])
                        else:
                            for m in range(2):
                                lm = last and m == 1
                                P.op("pe", lambda e, m=m, idx=idx, sbk=sbk, cb_=cb_, pslot=pslot, q0=q0, lm=lm: e.matmul(
                                    ps[0][:, m * TT + q0:(m + 1) * TT], lhsT=Vc[cb_][:, idx * NSB + sbk, :],
                                    rhs=Pt[pslot][:, m * TT + q0:(m + 1) * TT], start=False, stop=lm),
                                     [vres, "Pt%d" % pslot], [PSN[0]])
                                P.op("pe", lambda e, m=m, pslot=pslot, q0=q0, lm=lm, lones=lones: e.matmul(
                                    ps[1][:, m * TT + q0:(m + 1) * TT], lhsT=lones, rhs=Pt[pslot][:, m * TT + q0:(m + 1) * TT], start=False, stop=lm),
                                     [lon, "Pt%d" % pslot], [PSN[1]])
                        first = False
            if ATL >= 4:
                head_finish(oT[:, h, 0:TT], TT)

    def sample_attention():
        for bt in range(4):
            first = True
            for kb in range(33):
                nk = 128 if kb < 32 else 32
                cv_ = kb % 2
                if kb < 32:
                    load_cast(ckb, cache_k[bt, kb * 128:(kb + 1) * 128, :], "KTc0")
                    load_cast(cvb[cv_], cache_v[bt, kb * 128:(kb + 1) * 128, :], "Vc%d" % cv_)
                    transpose8(ckb, "KTc0", ckT, "KTc1")
                    kTsrc, kTname, kc0 = ckT, "KTc1", 0
                    vsrc, vname = cvb[cv_], "Vc%d" % cv_
                else:
                    kTsrc, kTname, kc0 = ktr, "ktr", bt * 32
                    P.op("sp", lambda e, bt=bt, cv_=cv_: e.dma_start(out=cvb[cv_][0:32, :], in_=vbf[bt * 32:(bt + 1) * 32, :]),
                         reads=["vbf"], writes=["Vc%d" % cv_], dma="mv%d" % cv_)
                    vsrc, vname = cvb[cv_], "Vc%d" % cv_
                sbank = 2 + (kb % 2)
                pslot = kb % 2
                for h in range(NH):
                    for m in range(2):
                        cidx = (h * 2 + m) * 32
                        P.op("pe", lambda e, h=h, m=m, cidx=cidx, nk=nk, sbank=sbank, bt=bt, kTsrc=kTsrc, kc0=kc0: e.matmul(
                            ps[sbank][0:nk, cidx:cidx + 32], lhsT=kTsrc[:, h, kc0:kc0 + nk],
                            rhs=QT[:, m, h, bt * 32:(bt + 1) * 32], start=True, stop=True),
                             [kTname, "QT"], [PSN[sbank]])
                P.op("act", lambda e, nk=nk, sbank=sbank, pslot=pslot: e.activation(out=Pt[pslot][0:nk, :], in_=ps[sbank][0:nk, :], func=AF.Exp, scale=0.125),
                     [PSN[sbank]], ["Pt%d" % pslot], stream=True)
                last = kb == 32
                if first:
                    P.op("pe", lambda e: e.matmul(ps[0][:, :], lhsT=onesb[:, :], rhs=zrhs[:, :], start=True, stop=False), ["onesb", "zrhs"], [PSN[0]])
                for h in range(NH):
                    cidx = h * 64
                    P.op("pe", lambda e, h=h, cidx=cidx, nk=nk, pslot=pslot, first=first, last=last, vsrc=vsrc: e.matmul(
                        ps[0][:, cidx:cidx + 64], lhsT=vsrc[0:nk, h * 128:(h + 1) * 128], rhs=Pt[pslot][0:nk, cidx:cidx + 64],
                        start=False, stop=(last and h == NH - 1)), [vname, "Pt%d" % pslot], [PSN[0]])
                P.op("pe", lambda e, nk=nk, pslot=pslot, first=first, last=last: e.matmul(
                    ps[1][:, :], lhsT=onesb[0:nk, :], rhs=Pt[pslot][0:nk, :], start=first, stop=last), ["onesb", "Pt%d" % pslot], [PSN[1]])
                first = False
            P.op("dve", lambda e: e.reciprocal(out=tmpf[0][:, :], in_=ps[1][:, :]), [PSN[1]], ["tmpf0"], stream=False)
            P.op("dve", lambda e: e.tensor_tensor(out=tmpf[1][:, :], in0=ps[0][:, :], in1=tmpf[0][:, :], op=ALU.mult), [PSN[0], "tmpf0"], ["tmpf1"], stream=False)
            t1 = tmpf[1][:, :].rearrange("p (h m q) -> p h m q", m=2, q=32)
            o2 = tmpf[2][:, 0:256].rearrange("p (h q) -> p h q", q=32)
            P.op("dve", lambda e, t1=t1, o2=o2: e.scalar_tensor_tensor(out=o2, in0=t1[:, :, 1, :], scalar=neglam, in1=t1[:, :, 0, :], op0=ALU.mult, op1=ALU.add),
                 ["tmpf1", "sc"], ["tmpf2"], stream=False)
            P.op("act", lambda e: e.activation(out=tmpf[3][:, 0:256], in_=tmpf[2][:, 0:256], func=AF.Square), ["tmpf2"], ["tmpf3"], stream=False)
            b = nps()
            P.op("pe", lambda e, b=b: e.matmul(ps[b][:, 0:256], lhsT=onesF, rhs=tmpf[3][:, 0:256], start=True, stop=True), ["tmpf3", "cst"], [PSN[b]])
            rsqrt_op(tmpf[0][:, 0:256], ps[b][:, 0:256], [PSN[b]], "tmpf0", 1.0 / 128)
            r0 = tmpf[0][:, 0:256].rearrange("p (h q) -> p h q", q=32)
            P.op("dve", lambda e, bt=bt, o2=o2, r0=r0: e.scalar_tensor_tensor(out=oT[:, :, bt * 32:(bt + 1) * 32], in0=o2, scalar=gsub, in1=r0, op0=ALU.mult, op1=ALU.mult),
                 ["tmpf2", "tmpf0", "sc"], ["oT"], stream=False)

    def merge_phase(ntok):
        for jg in range(D // WS):
            s_ao = load_w(w_ao, 0, 8, jg * WS)
            for blk in range(2):
                b = featmajor_mm(oT, "oT", 8, s_ao, blk, ntok)
                copy_op("act", tA[:, blk, 0:ntok], ps[b][:, 0:ntok], [PSN[b]], ["tA"])
            s_ga = load_w(w_in, 0, NCH, 5120 + jg * WS)
            for blk in range(2):
                b = featmajor_mm(hT, "hT", NCH, s_ga, blk, ntok)
                P.op("act", lambda e, b=b: e.activation(out=tmpf[2][:, 0:ntok], in_=ps[b][:, 0:ntok], func=AF.Sigmoid), [PSN[b]], ["tmpf2"], stream=True)
                P.op("dve", lambda e, blk=blk: e.tensor_tensor(out=tA[:, blk, 0:ntok], in0=tA[:, blk, 0:ntok], in1=tmpf[2][:, 0:ntok], op=ALU.mult),
                     ["tA", "tmpf2"], ["tA"], stream=True)
            s_co = load_w(w_co, 0, 8, jg * WS)
            for blk in range(2):
                j = jg * 2 + blk
                b = featmajor_mm(cnT, "cnT", 8, s_co, blk, ntok)
                P.op("dve", lambda e, b=b, blk=blk, j=j: e.tensor_scalar(out=tB[:, blk, 0:ntok], in0=ps[b][:, 0:ntok], scalar1=col(P_BCO + j), scalar2=None, op0=ALU.add),
                     [PSN[b], "par"], ["tB"], stream=True)
            s_gb = load_w(w_in, 0, NCH, 7168 + jg * WS)
            for blk in range(2):
                j = jg * 2 + blk
                b = featmajor_mm(hT, "hT", NCH, s_gb, blk, ntok)
                P.op("act", lambda e, b=b: e.activation(out=tmpf[3][:, 0:ntok], in_=ps[b][:, 0:ntok], func=AF.Sigmoid), [PSN[b]], ["tmpf3"], stream=True)
                P.op("dve", lambda e, blk=blk: e.tensor_tensor(out=tB[:, blk, 0:ntok], in0=tB[:, blk, 0:ntok], in1=tmpf[3][:, 0:ntok], op=ALU.mult),
                     ["tB", "tmpf3"], ["tB"], stream=True)
                P.op("dve", lambda e, blk=blk, j=j: e.tensor_tensor(out=mrgT[:, j, 0:ntok], in0=tA[:, blk, 0:ntok], in1=tB[:, blk, 0:ntok], op=ALU.add),
                     ["tA", "tB"], ["mrgT"], stream=True)

    def proj_norm_residual(srcT, srcname, nk_total, wsrc, gbase, ntok):
        for jg in range(D // WS):
            kgs = [(k0, min(NCH, nk_total - k0)) for k0 in range(0, nk_total, NCH)]
            bs = [nps(), nps()]
            for gi, (k0, nk) in enumerate(kgs):
                slot = load_w(wsrc, k0, nk, jg * WS)
                for blk in range(2):
                    b = bs[blk]
                    for c in range(nk):
                        P.op("pe", lambda e, c=c, b=b, blk=blk, slot=slot, k0=k0, gi=gi, nk=nk: e.matmul(
                            ps[b][:, 0:ntok], lhsT=wsl[slot][:, c, blk * 128:(blk + 1) * 128], rhs=srcT[:, k0 + c, 0:ntok],
                            start=(gi == 0 and c == 0), stop=(gi == len(kgs) - 1 and c == nk - 1)),
                             (([srcname] + wres(slot)) + (["ext", "cnT"] + ACCN if srcname == "actT" else [])), [PSN[b]])
            for blk in range(2):
                copy_op(evac_eng(), bufA[:, jg * 2 + blk, 0:ntok], ps[bs[blk]][:, 0:ntok], [PSN[bs[blk]]], ["bufA"])
        norm_stats(bufA, "bufA", NCH, ntok, 1.0 / D, None)
        for c in range(NCH):
            P.op("dve", lambda e, c=c: e.scalar_tensor_tensor(out=bufA[:, c, 0:ntok], in0=bufA[:, c, 0:ntok], scalar=col(gbase + c), in1=rstd[:, 0:ntok],
                                                              op0=ALU.mult, op1=ALU.mult), ["bufA", "rstd", "par"], ["bufA"], stream=True)
            P.op("dve", lambda e, c=c: e.tensor_tensor(out=xT[:, c, 0:ntok], in0=xT[:, c, 0:ntok], in1=bufA[:, c, 0:ntok], op=ALU.add),
                 ["bufA", "xT"], ["xT"], stream=True)

    def ffn_act(ntok):
        for fg in range(DFF // WS):
            sg = load_w(w_fg, 0, NCH, fg * WS)
            su = load_w(w_fu, 0, NCH, fg * WS)
            for blk in range(2):
                bg = featmajor_mm(hT, "hT", NCH, sg, blk, ntok)
                bu = featmajor_mm(hT, "hT", NCH, su, blk, ntok)
                P.op("act", lambda e, bg=bg: e.activation(out=tmpf[2][:, 0:ntok], in_=ps[bg][:, 0:ntok], func=AF.Sigmoid), [PSN[bg]], ["tmpf2"], stream=True)
                P.op("dve", lambda e, bg=bg: e.tensor_tensor(out=tmpf[2][:, 0:ntok], in0=ps[bg][:, 0:ntok], in1=tmpf[2][:, 0:ntok], op=ALU.mult),
                     [PSN[bg], "tmpf2"], ["tmpf2"], stream=True)
                P.op("dve", lambda e, bu=bu, fg=fg, blk=blk: e.tensor_tensor(out=actT[:, fg * 2 + blk, 0:ntok], in0=ps[bu][:, 0:ntok], in1=tmpf[2][:, 0:ntok], op=ALU.mult),
                     [PSN[bu], "tmpf2"], ["actT", "ext", "cnT"] + ACCN, stream=True)

    def store_y(ydst, row0, nsub):
        for s in range(nsub):
            xi = 0
            for c4 in range(4):
                b = nps()
                for k in range(4):
                    c = c4 * 4 + k
                    P.op("pe", lambda e, c=c, k=k, b=b, s=s: e.transpose(out=ps[b][:, k * 128:(k + 1) * 128], in_=xT[:, c, s * 128:(s + 1) * 128], identity=identf),
                         ["xT", "cst"], [PSN[b]])
                copy_op(evac_eng(), xin[xi][:, c4 * 512:(c4 + 1) * 512], ps[b][:, :], [PSN[b]], ["xin%d" % xi])
            P.op("act", lambda e, xi=xi, s=s: e.dma_start(out=ydst[row0 + s * 128: row0 + (s + 1) * 128, :], in_=xin[xi][:]),
                 reads=["xin%d" % xi], dma="y%d" % xi)

    def layer_tail(ntok):
        merge_phase(ntok)
        proj_norm_residual(mrgT, "mrgT", NCH, w_o, P_GPOST, ntok)
        norm_stats(xT, "xT", NCH, ntok, 1.0 / D, None)
        for c in range(NCH):
            P.op("dve", lambda e, c=c: e.scalar_tensor_tensor(out=hT[:, c, 0:ntok], in0=xT[:, c, 0:ntok], scalar=col(P_GFFN + c), in1=rstd[:, 0:ntok],
                                                              op0=ALU.mult, op1=ALU.mult), ["xT", "rstd", "par"], ["hT"], stream=True)
        ffn_act(ntok)
        proj_norm_residual(actT, "actT", NFC, w_fd, P_GPFF, ntok)

    NHS = (NOWN * 32) // 128
    halov = halo[:, :, :, :].rearrange("p c i t -> p c (i t)")
    C_ST = cfg.get('stage', 3)
    if C_HALO:
        if C_ST >= 1:
            load_xT(x_halo, 0, NHS, NHS * 128)
        if C_ST >= 2:
            make_hT(P_GPRE, NHS * 128)
        if C_ST >= 3:
            glu_phase(NHS * 128, lambda cb: halov[:, cb, :], "halo")

    for o in range(C_NOTH):
        load_xT(x_oth, o * TT, NSB, TT)
        make_hT(P_GPRE, TT)
        qkv_phase(o, o * NSB, NSB, None, None, 0, False, flag_o=o)

    for i in range(C_NOWN):
        load_xT(x_own, i * TT, NSB, TT)
        make_hT(P_GPRE, TT)
        qkv_phase(NOTH + i, (NOTH + i) * NSB, NSB, k_own, v_own, i * TT, True)
        for cb in range(8):
            copy_op("act", ext[:, cb, 0:32], halo[:, cb, i, :], ["halo"], ["ext"], stream=False)
        glu_phase(TT, lambda cb: ext[:, cb, 32:32 + TT], "ext")
        if i == NOWN - 1:
            b = nps()
            for cb in range(8):
                P.op("pe", lambda e, cb=cb, b=b: e.transpose(out=ps[b][0:32, (cb % 4) * 128:(cb % 4 + 1) * 128],
                                                             in_=ext[:, cb, TT:TT + 32], identity=identf), ["ext", "cst"], [PSN[b]])
                if cb % 4 == 3:
                    copy_op("dve", stc[0:32, (cb - 3) * 128:(cb + 1) * 128], ps[b][0:32, :], [PSN[b]], ["kvst0"], stream=False)
                    if cb == 3:
                        b = nps()
            P.op("act", lambda e: e.dma_start(out=conv_p[:, :], in_=stc[2:32, :]), reads=["kvst0"], dma="oc")
        if cfg.get('conv', True):
            conv_phase(lambda cb, k: ext[:, cb, 2 + k:2 + k + TT], lambda cb: acc[:, cb, 0:TT], None)
            if cfg.get('convl', 3) >= 2:
                conv_norm(TT)
        if cfg.get('attn', True):
            prompt_attention(i)
        if C_TAIL:
            layer_tail(TT)
        store_y(y_own, i * TT, NSB)

    if C_SMP:
        load_xT(x_smp, 0, 1, 128)
        make_hT(P_GPRE, 128)
        qkv_phase(None, NTL * NSB, 1, k_s, v_s, 0, True)
        transpose8(kbf, "kbf", ktr[:, :, :], "ktr")
        extS = ext[:, :, 0:248].rearrange("p c (b t) -> p c b t", t=62)
        for bt in range(4):
            P.op("sp", lambda e, bt=bt: e.dma_start(out=stc[0:30, :], in_=state_conv[bt, :, :]), writes=["kvst0"], dma="sc")
            b = nps()
            for cb in range(8):
                P.op("pe", lambda e, cb=cb, b=b: e.transpose(out=ps[b][:, cb * 32:cb * 32 + 30], in_=stc[0:30, cb * 128:(cb + 1) * 128], identity=identf[0:30, 0:30]),
                     ["kvst0", "cst"], [PSN[b]])
            copy_op("dve", extS[:, :, bt, 0:30], ps[b][:, 0:256].rearrange("p (c t) -> p c t", t=32)[:, :, 0:30], [PSN[b]], ["ext"], stream=False)
        glu_phase(128, lambda cb: extS[:, cb, :, 30:62], "ext", inview=lambda a: a.rearrange("p (b t) -> p b t", t=32))
        for bt in range(4):
            b = nps()
            for cb in range(8):
                P.op("pe", lambda e, cb=cb, b=b, bt=bt: e.transpose(out=ps[b][0:32, (cb % 4) * 128:(cb % 4 + 1) * 128], in_=extS[:, cb, bt, 30:62], identity=identf),
                     ["ext", "cst"], [PSN[b]])
                if cb % 4 == 3:
                    copy_op("dve", stc[0:32, (cb - 3) * 128:(cb + 1) * 128], ps[b][0:32, :], [PSN[b]], ["kvst0"], stream=False)
                    if cb == 3:
                        b = nps()
            P.op("act", lambda e, bt=bt: e.dma_start(out=conv_s[bt, :, :], in_=stc[2:32, :]), reads=["kvst0"], dma="oc")
        conv_phase(lambda cb, k: extS[:, cb, :, k:k + 32], lambda cb: acc[:, cb, 0:128].rearrange("p (b t) -> p b t", t=32), None)
        conv_norm(128)
        sample_attention()
        layer_tail(128)
        store_y(y_s, 0, 1)

    print("ops:", {e: len(v) for e, v in P.ops.items()}, flush=True)
    with nc.Block() as block:
        semcms = P.emit(nc, block)
    return nc


_NC_CACHE = {}


def _rope_tab(pos):
    inv = (500000.0 ** (-np.arange(0, 16, 2, dtype=np.float32) / 16.0)).astype(np.float32)
    ang = pos.astype(np.float32)[:, None] * inv[None, :]
    return np.concatenate([np.cos(ang), np.sin(ang)], axis=1).astype(np.float32)


def prepare(inputs):
    f = lambda k: np.ascontiguousarray(np.asarray(inputs[k], dtype=np.float32))
    xp, xs = f("x_prompt"), f("x_sample")
    ck, cv, scv = f("cache_k")[0], f("cache_v")[0], f("state_conv")[0]
    par = np.zeros((128, NPAR), np.float32)
    colmaj = lambda v, n: np.ascontiguousarray(v.reshape(n, 128).T)
    par[:, P_GPRE:P_GPRE + 16] = colmaj(f("g_pre_mix")[0], 16)
    par[:, P_GPOST:P_GPOST + 16] = colmaj(f("g_post_mix")[0], 16)
    par[:, P_GFFN:P_GFFN + 16] = colmaj(f("g_pre_ffn")[0], 16)
    par[:, P_GPFF:P_GPFF + 16] = colmaj(f("g_post_ffn")[0], 16)
    par[:, P_BCO:P_BCO + 16] = colmaj(f("b_conv_out")[0], 16)
    par[:, P_GSUB] = f("g_subln")[0]
    wdw = f("w_dw")[0]
    par[:, P_WDW:P_WDW + 248] = wdw.reshape(31, 8, 128).transpose(2, 0, 1).reshape(128, 248)
    par[:, P_BDW:P_BDW + 8] = colmaj(f("b_dw")[0], 8)
    par[:, P_GCN:P_GCN + 8] = colmaj(f("g_conv_norm")[0], 8)
    par[:, P_BCN:P_BCN + 8] = colmaj(f("b_conv_norm")[0], 8)
    for n, key in enumerate(["lambda_q1", "lambda_k1", "lambda_q2", "lambda_k2"]):
        par[:, P_LAM + 64 * n:P_LAM + 64 * (n + 1)] = f(key)[0][None, :]
    consts = np.concatenate([np.eye(128, dtype=np.float32), np.ones((128, 128), np.float32)], axis=1)
    w = {"w_in": f("w_in")[0], "w_ao": f("w_attn_out")[0], "w_co": f("w_conv_out")[0], "w_o": f("w_o")[0],
         "w_fg": f("w_ffn_gate")[0], "w_fu": f("w_ffn_up")[0], "w_fd": f("w_ffn_down")[0]}
    in_maps = []
    own_tiles = {}
    for core in range(8):
        b, j = core // 4, core % 4
        own = [4 * i + j for i in range(NOWN)]
        oth = [4 * wd + k for wd in range(NOWN) for k in range(4) if k != j]
        own_tiles[core] = own
        xb = xp[b]
        x_oth = np.concatenate([xb[t * TT:(t + 1) * TT] for t in oth], axis=0)
        x_own = np.concatenate([xb[t * TT:(t + 1) * TT] for t in own], axis=0)
        x_halo = np.zeros((NOWN * 32, D), np.float32)
        for i, t in enumerate(own):
            if t > 0:
                x_halo[i * 32:(i + 1) * 32] = xb[t * TT - 32:t * TT]
        pos = np.concatenate([np.arange(t * TT, (t + 1) * TT) for t in oth + own] + [4096 + (np.arange(128) % 32)])
        rt = _rope_tab(pos).reshape(NTL * NSB + 1, 128, 16).transpose(1, 0, 2)
        flags = np.zeros((128, NOTH), np.float32)
        for o, t in enumerate(oth):
            flags[:, o] = 1.0 if (t % 4) < j else 0.0
        m = {"x_oth": x_oth, "x_own": x_own, "x_halo": x_halo,
             "x_smp": np.ascontiguousarray(xs[4 * core:4 * core + 4].reshape(128, D)),
             "rope": np.ascontiguousarray(rt), "flags": flags, "params": par, "consts": consts,
             "cache_k": np.ascontiguousarray(ck[4 * core:4 * core + 4].reshape(4, 4096, 1024)),
             "cache_v": np.ascontiguousarray(cv[4 * core:4 * core + 4].reshape(4, 4096, 1024)),
             "state_conv": np.ascontiguousarray(scv[4 * core:4 * core + 4])}
        m.update(w)
        in_maps.append(m)
    return in_maps, own_tiles


def kernel(**inputs):
    in_maps, own_tiles = prepare(inputs)
    if "nc" not in _NC_CACHE:
        _NC_CACHE["nc"] = build_nc()
    res = run_bass_kernel_spmd(_NC_CACHE["nc"], in_maps, core_ids=list(range(8)))
    R = res.results
    yp = np.zeros((2, SEQ, D), np.float32)
    kp = np.zeros((1, 2, SEQ, 1024), np.float32)
    vp = np.zeros((1, 2, SEQ, 1024), np.float32)
    cp = np.zeros((1, 2, 30, CC), np.float32)
    ys = np.zeros((32, 32, D), np.float32)
    ks = np.zeros((1, 32, 32, 1024), np.float32)
    vs = np.zeros((1, 32, 32, 1024), np.float32)
    cs = np.zeros((1, 32, 30, CC), np.float32)
    for core in range(8):
        b, j = core // 4, core % 4
        r = R[core]
        for i, t in enumerate(own_tiles[core]):
            yp[b, t * TT:(t + 1) * TT] = r["y_own"][i * TT:(i + 1) * TT]
            kp[0, b, t * TT:(t + 1) * TT] = r["k_own"][i * TT:(i + 1) * TT]
            vp[0, b, t * TT:(t + 1) * TT] = r["v_own"][i * TT:(i + 1) * TT]
        if j == 3:
            cp[0, b] = r["conv_p"]
        ys[4 * core:4 * core + 4] = r["y_s"].reshape(4, 32, D)
        ks[0, 4 * core:4 * core + 4] = r["k_s"].reshape(4, 32, 1024)
        vs[0, 4 * core:4 * core + 4] = r["v_s"].reshape(4, 32, 1024)
        cs[0, 4 * core:4 * core + 4] = r["conv_s"]
    return (yp, ys, kp.reshape(1, 2, SEQ, 8, 2, 64), vp.reshape(1, 2, SEQ, 8, 128), cp,
            ks.reshape(1, 32, 32, 8, 2, 64), vs.reshape(1, 32, 32, 8, 128), cs)
```

```python
import math
import numpy as np
import ml_dtypes
import concourse.bass as bass
import concourse.mybir as mybir
from concourse.bass_utils import run_bass_kernel_spmd

F32 = mybir.dt.float32
BF16 = mybir.dt.bfloat16
ALU = mybir.AluOpType
AF = mybir.ActivationFunctionType
AX = mybir.AxisListType

D = 2048
NCH = 16
SEQ = 8192
NH = 8
CC = 1024
DFF = 5632
NFC = 44
INC = 9216
EPS = 1e-6
TT = 256
NSB = TT // 128
NOWN = 2048 // TT
NOTH = 3 * NOWN
NTL = NOTH + NOWN
WS = 256
KC = 8
LAM_INIT = 0.8 - 0.6 * math.exp(0.0)
SKIP_STREAM_SYNC = False

P_GPRE, P_GPOST, P_GFFN, P_GPFF, P_BCO = 0, 16, 32, 48, 64
P_GSUB = 80
P_WDW = 81
P_BDW = P_WDW + 248
P_GCN = P_BDW + 8
P_BCN = P_GCN + 8
P_LAM = P_BCN + 8
NPAR = P_LAM + 256


class Prog:
    ENGS = ["pe", "act", "dve", "pool", "sp"]

    def __init__(self):
        self.ops = {e: [] for e in self.ENGS}
        self.res = {}
        self.dma_cnt = {}
        self.dma_ep = {}

    def op(self, eng, fn, reads=(), writes=(), dma=None, stream=False, join=False):
        o = {"eng": eng, "fn": fn, "deps": [], "needed": False, "dma": dma, "stream": stream}
        deps = []
        for r in reads:
            st = self.res.setdefault(r, {"w": None, "r": []})
            if st["w"] is not None:
                deps.append(st["w"])
        for w in writes:
            st = self.res.setdefault(w, {"w": None, "r": []})
            if st["w"] is not None and not (join and st["w"].get("dma_base") == dma):
                deps.append(st["w"])
            deps.extend(st["r"])
        seen = set()
        for a in deps:
            if id(a) in seen or a is o:
                continue
            seen.add(id(a))
            if a["dma"] is None and a["eng"] == eng:
                if eng in ("pe", "sp"):
                    continue
                if SKIP_STREAM_SYNC and a["stream"] and stream:
                    continue
            if a["dma"] is None:
                a["needed"] = True
            o["deps"].append(a)
        for r in reads:
            self.res[r]["r"].append(o)
        for w in writes:
            self.res[w] = {"w": o, "r": []}
        if dma is not None:
            ep = self.dma_ep.get(dma, 0)
            if not join and self.dma_cnt.get("%s_%d" % (dma, ep), 0) >= 1000:
                ep += 1
                self.dma_ep[dma] = ep
            dname = "%s_%d" % (dma, ep)
            o["dma"] = dname
            o["dma_base"] = dma
            self.dma_cnt[dname] = self.dma_cnt.get(dname, 0) + 1
            o["dmaval"] = 16 * self.dma_cnt[dname]
        self.ops[eng].append(o)
        return o

    def emit(self, nc, block):
        sems = {}
        allsems = []

        def getsem(name):
            if name not in sems:
                cm = nc.semaphore("s_" + name)
                s = cm.__enter__()
                allsems.append(cm)
                sems[name] = s
            return sems[name]

        for e in self.ENGS:
            n = 0
            ep = 0
            for o in self.ops[e]:
                if o["dma"] is None and o["needed"]:
                    if n >= 12000:
                        n = 0
                        ep += 1
                    n += 1
                    o["semval"] = n
                    o["semkey"] = "e_%s_%d" % (e, ep)
                    getsem(o["semkey"])
        for d in self.dma_cnt:
            getsem("d_" + d)

        def run(e, engobj):
            waited = {}
            for o in self.ops[e]:
                need = {}
                for a in o["deps"]:
                    if a["dma"] is not None:
                        key, val = "d_" + a["dma"], a["dmaval"]
                    else:
                        key, val = a["semkey"], a["semval"]
                    need[key] = max(need.get(key, 0), val)
                for key, val in need.items():
                    if waited.get(key, 0) >= val:
                        continue
                    waited[key] = val
                    engobj.wait_ge(sems[key], val)
                ins = o["fn"](engobj)
                if o["dma"] is not None:
                    ins.then_inc(sems["d_" + o["dma"]], 16)
                elif o["needed"]:
                    ins.then_inc(sems[o["semkey"]], 1)
            if e == "sp":
                for d, c in self.dma_cnt.items():
                    engobj.wait_ge(sems["d_" + d], 16 * c)

        block.tensor(lambda t: run("pe", t))
        block.scalar(lambda t: run("act", t))
        block.vector(lambda t: run("dve", t))
        block.gpsimd(lambda t: run("pool", t))
        block.sync(lambda t: run("sp", t))
        return allsems


def build_nc(cfg=None):
    cfg = cfg or {}
    C_HALO = cfg.get('halo', True)
    C_NOTH = cfg.get('noth', NOTH)
    C_NOWN = cfg.get('nown', NOWN)
    C_SMP = cfg.get('sample', True)
    C_TAIL = cfg.get('tail', True)
    C_SCR = cfg.get('scr', True)
    C_ROPE = cfg.get('rope', True)
    C_TR = cfg.get('tr', True)
    nc = bass.Bass("TRN2", target_bir_lowering=False)
    dt_in = lambda n, s, d=F32: nc.dram_tensor(n, list(s), d, kind="ExternalInput").ap()
    dt_out = lambda n, s: nc.dram_tensor(n, list(s), F32, kind="ExternalOutput").ap()
    x_oth = dt_in("x_oth", [NOTH * TT, D])
    x_own = dt_in("x_own", [NOWN * TT, D])
    x_halo = dt_in("x_halo", [NOWN * 32, D])
    x_smp = dt_in("x_smp", [128, D])
    rope_d = dt_in("rope", [128, NTL * NSB + 1, 16])
    flags_d = dt_in("flags", [128, NOTH])
    params_d = dt_in("params", [128, NPAR])
    consts_d = dt_in("consts", [128, 256])
    cache_k = dt_in("cache_k", [4, 4096, 1024])
    cache_v = dt_in("cache_v", [4, 4096, 1024])
    state_conv = dt_in("state_conv", [4, 30, CC])
    w_in = dt_in("w_in", [D, INC])
    w_ao = dt_in("w_ao", [CC, D])
    w_co = dt_in("w_co", [CC, D])
    w_o = dt_in("w_o", [D, D])
    w_fg = dt_in("w_fg", [D, DFF])
    w_fu = dt_in("w_fu", [D, DFF])
    w_fd = dt_in("w_fd", [DFF, D])
    y_own = dt_out("y_own", [NOWN * TT, D])
    k_own = dt_out("k_own", [NOWN * TT, 1024])
    v_own = dt_out("v_own", [NOWN * TT, 1024])
    conv_p = dt_out("conv_p", [30, CC])
    y_s = dt_out("y_s", [128, D])
    k_s = dt_out("k_s", [128, 1024])
    v_s = dt_out("v_s", [128, 1024])
    conv_s = dt_out("conv_s", [4, 30, CC])
    kT_scr = nc.dram_tensor("kT_scr", [NH, 128, NTL, TT], BF16, kind="Internal").ap()
    v_scr = nc.dram_tensor("v_scr", [NH, 128, NTL, NSB, 128], BF16, kind="Internal").ap()
    vm_scr = nc.dram_tensor("vm_scr", [NH, 128, NOTH, NSB, 128], BF16, kind="Internal").ap()

    P = Prog()
    cms = []

    def sb(name, shape, dt):
        cm = nc.sbuf_tensor(name, list(shape), dt)
        t = cm.__enter__()
        cms.append(cm)
        return t

    ps = []
    for i in range(8):
        cm = nc.psum_tensor("ps%d" % i, [128, 512], F32)
        ps.append(cm.__enter__())
        cms.append(cm)
    PSN = ["ps%d" % i for i in range(8)]

    TM = 128 + TT
    xT = sb("xT", [128, NCH, TT], F32)
    hT = sb("hT", [128, NCH, TT], BF16)
    bufA = sb("bufA", [128, NCH, TT], F32)
    xin = [sb("xin0", [128, D], F32)] * 2
    kvst = [sb("kvst%d" % i, [128, 2048], F32) for i in range(2)]
    NWSL = 2
    NSTG = 3
    wsl = [sb("wsl%d" % i, [128, NCH, WS], BF16) for i in range(NWSL)]
    stg = [sb("stg%d" % i, [128, 1024], F32) for i in range(NSTG)]
    QT = sb("QT", [128, 2, NH, TT], BF16)
    oT = sb("oT", [128, NH, TT], BF16)
    arena = sb("arena", [128, NFC * TT], BF16)
    actT = arena[:, :].rearrange("p (c t) -> p c t", t=TT)
    _e0 = 8 * (32 + TT) * 2
    ext = arena[:, 0:_e0].bitcast(F32).rearrange("p (c t) -> p c t", t=32 + TT)
    _a0 = _e0 + 8 * TT * 2
    acc = arena[:, _e0:_a0].bitcast(F32).rearrange("p (c t) -> p c t", t=TT)
    cnT = arena[:, _a0:_a0 + 8 * TT].rearrange("p (c t) -> p c t", t=TT)
    assert _a0 + 8 * TT <= NFC * TT
    kvbuf = sb("kvbuf", [128, 4 * KC * TT], BF16)
    KTc = [kvbuf[:, i * KC * TT:(i + 1) * KC * TT].rearrange("p (a b) -> p a b", b=TT) for i in range(2)]
    Vc = [kvbuf[:, (2 + i) * KC * TT:(3 + i) * KC * TT].rearrange("p (a b) -> p a b", b=128) for i in range(2)]
    mrgT = kvbuf[:, 0:NCH * TT].rearrange("p (a b) -> p a b", b=TT)
    assert NCH * TT <= 2 * KC * TT
    Pt = [sb("Pt%d" % i, [128, 512], BF16) for i in range(2)]
    tmpf = [sb("tmpf%d" % i, [128, 512], F32) for i in range(4)]
    qtmps = [sb("qtmp%d" % i, [128, 1024], F32) for i in range(2)]
    qbf = sb("qbf", [128, 1024], BF16)
    kbf = sb("kbf", [128, 1024], BF16)
    vbfs = [sb("vbf%d" % i, [128, 1024], BF16) for i in range(2)]
    vbf = vbfs[0]
    vmf = sb("vmf", [128, 1024], BF16)
    ktr = sb("ktr", [128, NH, 128], BF16)
    rtmp = [sb("rtmp%d" % i, [128, 16, 8], F32) for i in range(4)]
    tA = sb("tA", [128, 2, TT], F32)
    tB = sb("tB", [128, 2, TT], F32)
    rstd = sb("rstd", [128, TT], F32)
    mean = sb("mean", [128, TT], F32)
    par = sb("par", [128, NPAR], F32)
    rope = sb("rope_sb", [128, NTL * NSB + 1, 16], F32)
    flg = sb("flg", [128, NOTH], F32)
    cst = sb("cst_sb", [128, 256], F32)
    identb = sb("identb", [128, 128], BF16)
    onesb = sb("onesb", [128, 128], BF16)
    onesf = sb("onesf", [128, NOTH, 128], BF16)
    halo = sb("halo", [128, 8, NOWN, 32], F32)
    sc = sb("sc", [128, 16], F32)
    zrhs = sb("zrhs", [128, 512], BF16)
    ckb = KTc[0][:, :, :].rearrange("p a b -> p (a b)")[:, 0:1024]
    cvb = [Vc[i][:, :, :].rearrange("p a b -> p (a b)")[:, 0:1024] for i in range(2)]
    ckT = KTc[1][:, :, 0:128]
    stc = kvst[0][0:32, 0:CC]
    identf = cst[:, 0:128]
    onesF = cst[:, 128:256]

    state = {"ps": 0, "ws": 0, "xin": 0, "kv": 0, "ev": 0, "stg": 0, "ce": 0}

    def nps():
        i = state["ps"]
        state["ps"] = (i + 1) % 6
        return i + 2

    def evac_eng():
        state["ev"] ^= 1
        return "act" if state["ev"] else "dve"

    def copy_op(eng, out, in_, reads, writes, stream=True):
        if eng == "act":
            P.op("act", lambda e: e.copy(out=out, in_=in_), reads, writes, stream=stream)
        else:
            P.op(eng, lambda e: e.tensor_copy(out=out, in_=in_), reads, writes, stream=stream)

    ACCN = ["acc%d" % cb for cb in range(8)]

    def col(c):
        return par[:, c:c + 1]

    P.op("sp", lambda e: e.dma_start(out=par[:], in_=params_d), writes=["par"], dma="c0")
    P.op("sp", lambda e: e.dma_start(out=rope[:], in_=rope_d), writes=["rope"], dma="c1")
    P.op("sp", lambda e: e.dma_start(out=flg[:], in_=flags_d), writes=["flg"], dma="c2")
    P.op("sp", lambda e: e.dma_start(out=cst[:], in_=consts_d), writes=["cst"], dma="c3")
    copy_op("dve", identb[:], identf, ["cst"], ["identb"], stream=False)
    P.op("dve", lambda e: e.memset(zrhs[:, :], 0.0), [], ["zrhs"], stream=False)
    P.op("dve", lambda e: e.memset(QT[:, :, :, :].rearrange("p m h t -> p (m h t)"), 0.0), [], ["QT"], stream=False)
    copy_op("dve", onesb[:], onesF, ["cst"], ["onesb"], stream=False)
    for o in range(NOTH):
        P.op("dve", lambda e, o=o: e.tensor_scalar(out=onesf[:, o, :], in0=onesF, scalar1=flg[:, o:o + 1],
                                                   scalar2=None, op0=ALU.mult),
             ["cst", "flg"], ["onesf"], stream=False)
    lp = tmpf[0]
    P.op("dve", lambda e: e.tensor_tensor(out=lp[:, 0:64], in0=par[:, P_LAM:P_LAM + 64], in1=par[:, P_LAM + 64:P_LAM + 128], op=ALU.mult),
         ["par"], ["tmpf0"], stream=False)
    P.op("dve", lambda e: e.tensor_tensor(out=lp[:, 64:128], in0=par[:, P_LAM + 128:P_LAM + 192], in1=par[:, P_LAM + 192:P_LAM + 256], op=ALU.mult),
         ["par", "tmpf0"], ["tmpf0"], stream=False)
    P.op("dve", lambda e: e.tensor_reduce(out=sc[:, 3:4], in_=lp[:, 0:64], axis=AX.X, op=ALU.add), ["tmpf0"], ["sc"], stream=False)
    P.op("dve", lambda e: e.tensor_reduce(out=sc[:, 4:5], in_=lp[:, 64:128], axis=AX.X, op=ALU.add), ["tmpf0", "sc"], ["sc"], stream=False)
    P.op("act", lambda e: e.activation(out=sc[:, 5:7], in_=sc[:, 3:5], func=AF.Exp), ["sc"], ["sc"], stream=False)
    P.op("dve", lambda e: e.tensor_tensor(out=sc[:, 0:1], in0=sc[:, 5:6], in1=sc[:, 6:7], op=ALU.subtract), ["sc"], ["sc"], stream=False)
    P.op("dve", lambda e: e.tensor_scalar(out=sc[:, 1:2], in0=sc[:, 0:1], scalar1=LAM_INIT, scalar2=-1.0, op0=ALU.add, op1=ALU.mult),
         ["sc"], ["sc"], stream=False)
    P.op("dve", lambda e: e.tensor_scalar(out=sc[:, 2:3], in0=col(P_GSUB), scalar1=1.0 - LAM_INIT, scalar2=None, op0=ALU.mult),
         ["sc", "par"], ["sc"], stream=False)
    P.op("dve", lambda e: e.memset(sc[:, 8:9], EPS), ["sc"], ["sc"], stream=False)
    epsc = sc[:, 8:9]
    neglam = sc[:, 1:2]
    gsub = sc[:, 2:3]

    def rsqrt_op(dst, src, reads, dstname, scale):
        P.op("act", lambda e: e.activation(out=dst, in_=src, func=AF.Sqrt, bias=epsc, scale=scale), list(reads) + ["sc"], [dstname], stream=False)
        P.op("dve", lambda e: e.reciprocal(out=dst, in_=dst), [dstname], [dstname], stream=False)

    def wres(slot):
        return ["wsl%dq%d" % (slot, q) for q in range(4)]

    CAST_ENGS = ["dve", "pool", "act"]

    def load_cast(dst, src, dstname, n_inner=None):
        st = state["stg"]
        state["stg"] = (st + 1) % NSTG
        sv = stg[st][:, :]
        if n_inner is not None:
            sv = stg[st][:, 0:dst.shape[1] * n_inner].rearrange("p (c n) -> p c n", n=n_inner)
        else:
            sv = stg[st][:, 0:dst.shape[1]]
        P.op("sp", lambda e: e.dma_start(out=sv, in_=src), writes=["stg%d" % st], dma="g%d" % st)
        eng = CAST_ENGS[state["ce"] % 3]
        state["ce"] += 1
        copy_op(eng, dst, sv, ["stg%d" % st], [dstname])

    def load_w(src, k0, nk, c0, ncols=WS):
        s = state["ws"]
        state["ws"] = (s + 1) % NWSL
        for qi, q0 in enumerate(range(0, nk, 4)):
            q1 = min(nk, q0 + 4)
            load_cast(wsl[s][:, q0:q1, 0:ncols],
                      src[(k0 + q0) * 128:(k0 + q1) * 128, c0:c0 + ncols].rearrange("(c p) n -> p c n", p=128),
                      "wsl%dq%d" % (s, qi), n_inner=ncols)
        return s

    def norm_stats(srcT, srcname, nch, ntok, inv_n, tagps):
        b = nps()
        for c in range(nch):
            t = c % 2
            P.op("act", lambda e, c=c, t=t: e.activation(out=tmpf[t][:, 0:ntok], in_=srcT[:, c, 0:ntok], func=AF.Square),
                 [srcname], ["tmpf%d" % t], stream=True)
            P.op("pe", lambda e, c=c, t=t, b=b: e.matmul(ps[b][:, 0:ntok], lhsT=onesF, rhs=tmpf[t][:, 0:ntok],
                                                         start=(c == 0), stop=(c == nch - 1)),
                 ["tmpf%d" % t, "cst"], [PSN[b]])
        rsqrt_op(rstd[:, 0:ntok], ps[b][:, 0:ntok], [PSN[b]], "rstd", inv_n)

    def load_xT(xsrc, row0, nsub, ntok):
        for s in range(nsub):
            xi = 0
            P.op("sp", lambda e, xi=xi, s=s: e.dma_start(out=xin[xi][:], in_=xsrc[row0 + s * 128: row0 + (s + 1) * 128, :]),
                 writes=["xin%d" % xi], dma="x%d" % xi)
            for c4 in range(4):
                b = nps()
                for k in range(4):
                    c = c4 * 4 + k
                    P.op("pe", lambda e, xi=xi, c=c, k=k, b=b: e.transpose(out=ps[b][:, k * 128:(k + 1) * 128],
                                                                           in_=xin[xi][:, c * 128:(c + 1) * 128], identity=identf),
                         ["xin%d" % xi, "cst"], [PSN[b]])
                copy_op(evac_eng(), xT[:, c4 * 4:c4 * 4 + 4, s * 128:(s + 1) * 128],
                        ps[b][:, :].rearrange("p (c t) -> p c t", t=128), [PSN[b]], ["xT"])

    def make_hT(gbase, ntok):
        norm_stats(xT, "xT", NCH, ntok, 1.0 / D, None)
        for c in range(NCH):
            P.op("dve", lambda e, c=c: e.scalar_tensor_tensor(out=hT[:, c, 0:ntok], in0=xT[:, c, 0:ntok], scalar=col(gbase + c),
                                                              in1=rstd[:, 0:ntok], op0=ALU.mult, op1=ALU.mult),
                 ["xT", "rstd", "par"], ["hT"], stream=True)

    def tokmajor_mm(s, slot, ncols=WS):
        b = nps()
        for c in range(NCH):
            P.op("pe", lambda e, c=c, b=b: e.matmul(ps[b][:, 0:ncols], lhsT=hT[:, c, s * 128:(s + 1) * 128], rhs=wsl[slot][:, c, 0:ncols],
                                                    start=(c == 0), stop=(c == NCH - 1)),
                 (["hT"] + wres(slot)), [PSN[b]])
        return b

    def featmajor_mm(srcT, srcname, nk, slot, blk, ntok, wchunk0=0):
        b = nps()
        for c in range(nk):
            P.op("pe", lambda e, c=c, b=b: e.matmul(ps[b][:, 0:ntok], lhsT=wsl[slot][:, c, blk * 128:(blk + 1) * 128],
                                                    rhs=srcT[:, wchunk0 + c, 0:ntok], start=(c == 0), stop=(c == nk - 1)),
                 ([srcname] + wres(slot)), [PSN[b]])
        return b

    def rope_evac(b, dst, dstname, rsub, nblk):
        w = nblk * 64
        copy_op("act", dst, ps[b][:, 0:w], [PSN[b]], [dstname])
        if not C_ROPE:
            return
        dv = dst.rearrange("p (b d) -> p b d", d=64)
        pv = dv
        cosb = rope[:, rsub, 0:8].unsqueeze(1).to_broadcast([128, nblk, 8])
        sinb = rope[:, rsub, 8:16].unsqueeze(1).to_broadcast([128, nblk, 8])
        x1, x2 = pv[:, :, 0:8], pv[:, :, 8:16]
        r = [t[:, 0:nblk, :] for t in rtmp]
        rn = ["rtmp%d" % i for i in range(4)]
        P.op("dve", lambda e: e.tensor_tensor(out=r[0], in0=x1, in1=cosb, op=ALU.mult), [dstname, "rope"], [rn[0]], stream=False)
        P.op("dve", lambda e: e.tensor_tensor(out=r[1], in0=x2, in1=sinb, op=ALU.mult), [dstname, "rope"], [rn[1]], stream=False)
        P.op("dve", lambda e: e.tensor_tensor(out=r[2], in0=x2, in1=cosb, op=ALU.mult), [dstname, "rope"], [rn[2]], stream=False)
        P.op("dve", lambda e: e.tensor_tensor(out=r[3], in0=x1, in1=sinb, op=ALU.mult), [dstname, "rope"], [rn[3]], stream=False)
        P.op("dve", lambda e: e.tensor_tensor(out=dv[:, :, 0:8], in0=r[0], in1=r[1], op=ALU.subtract), [rn[0], rn[1], dstname], [dstname], stream=False)
        P.op("dve", lambda e: e.tensor_tensor(out=dv[:, :, 8:16], in0=r[2], in1=r[3], op=ALU.add), [rn[2], rn[3], dstname], [dstname], stream=False)

    def transpose8(srcbf, srcname, dst, dstname, nrows=128):
        b = nps()
        pb = ps[b][:, :].bitcast(BF16)
        for h in range(NH):
            P.op("pe", lambda e, h=h: e.transpose(out=pb[:, h * 128:h * 128 + nrows], in_=srcbf[0:nrows, h * 128:(h + 1) * 128],
                                                  identity=identb[0:nrows, 0:nrows]),
                 [srcname, "identb"], [PSN[b]])
        pb3 = pb.rearrange("p (h t) -> p h t", t=128)[:, :, 0:nrows]
        if isinstance(dst, tuple):
            eng = evac_eng()
            copy_op(eng, dst[0], pb3[0:64], [PSN[b]], [dstname])
            copy_op(eng, dst[1], pb3[64:128], [PSN[b]], [dstname])
        else:
            copy_op(evac_eng(), dst, pb3, [PSN[b]], [dstname])

    def qkv_phase(tslot, rsub0, nsub, kout, vout, orow0, do_q, flag_o=None, ntok_rows=128):
        groups = ([("q", 0), ("q", 1), ("q", 2), ("q", 3)] if do_q else []) + [("k", 0), ("k", 1), ("k", 2), ("k", 3)] + [("v", 0), ("v", 1), ("v", 2), ("v", 3)]
        groups = groups[:cfg.get('qg', 99)]
        for (kind, g) in groups:
            c0 = {"q": 0, "k": 1024, "v": 2048}[kind] + g * WS
            slot = load_w(w_in, 0, NCH, c0)
            for s in range(nsub):
                kst = kvst[s][:, 0:1024]
                vst = kvst[s][:, 1024:2048]
                kvn = "kvst%d" % s
                b = tokmajor_mm(s, slot)
                if kind == "q":
                    rope_evac(b, qtmps[s][:, g * WS:(g + 1) * WS], "qtmp%d" % s, rsub0 + s, WS // 64)
                elif kind == "k":
                    rope_evac(b, kst[:, g * WS:(g + 1) * WS], kvn, rsub0 + s, WS // 64)
                else:
                    copy_op("act", vst[:, g * WS:(g + 1) * WS], ps[b][:, 0:WS], [PSN[b]], [kvn])
                    copy_op("dve", vbfs[s][:, g * WS:(g + 1) * WS], vst[:, g * WS:(g + 1) * WS], [kvn], ["vbf%d" % s])
        for s in range(nsub):
            kv = s
            kst = kvst[s][:, 0:1024]
            vst = kvst[s][:, 1024:2048]
            kvn = "kvst%d" % s
            qtmp = qtmps[s]
            vbf = vbfs[s]
            vbn = "vbf%d" % s
            if do_q:
                copy_op("dve", qbf[:], qtmp[:], ["qtmp%d" % s], ["qbf"])
                transpose8(qbf, "qbf", (QT[0:64, 0, :, s * 128:(s + 1) * 128], QT[64:128, 1, :, s * 128:(s + 1) * 128]), "QT")
            copy_op("act", kbf[:], kst, [kvn], ["kbf"])
            if kout is not None:
                P.op("act", lambda e, s=s, kst=kst: e.dma_start(out=kout[orow0 + s * 128: orow0 + (s + 1) * 128, :], in_=kst),
                     reads=[kvn], dma="ok%d" % s)
                P.op("act", lambda e, s=s, vst=vst: e.dma_start(out=vout[orow0 + s * 128: orow0 + (s + 1) * 128, :], in_=vst),
                     reads=[kvn], dma="ov%d" % s)
            if tslot is not None and C_TR:
                transpose8(kbf, "kbf", ktr[:, :, :], "ktr")
            if tslot is not None and C_SCR:
                for hh in (0, 4):
                    P.op("act", lambda e, s=s, hh=hh: e.dma_start(out=kT_scr[hh:hh + 4, :, tslot, s * 128:(s + 1) * 128].rearrange("h p t -> p h t"), in_=ktr[:, hh:hh + 4, :]),
                         reads=["ktr"], writes=["kTs%d" % tslot], dma="sk", join=True)
                    P.op("act", lambda e, s=s, hh=hh, vbf=vbf: e.dma_start(out=v_scr[hh:hh + 4, :, tslot, s, :].rearrange("h p e -> p h e"),
                                                                 in_=vbf[:, hh * 128:(hh + 4) * 128].rearrange("p (h e) -> p h e", e=128)),
                         reads=[vbn], writes=["vs%d" % tslot], dma="sv%d" % s, join=True)
                if flag_o is not None:
                    P.op("dve", lambda e, vbf=vbf: e.tensor_scalar(out=vmf[:], in0=vbf[:], scalar1=flg[:, flag_o:flag_o + 1], scalar2=None, op0=ALU.mult),
                         [vbn, "flg"], ["vmf"], stream=True)
                    for hh in (0, 4):
                        P.op("act", lambda e, s=s, hh=hh: e.dma_start(out=vm_scr[hh:hh + 4, :, flag_o, s, :].rearrange("h p e -> p h e"),
                                                                     in_=vmf[:, hh * 128:(hh + 4) * 128].rearrange("p (h e) -> p h e", e=128)),
                             reads=["vmf"], writes=["vms%d" % flag_o], dma="sm", join=True)

    def glu_phase(ntok, dstfn, dstname, inview=lambda a: a):
        for g in range(CC // WS):
            sa = load_w(w_in, 0, NCH, 3072 + g * WS)
            sbb = load_w(w_in, 0, NCH, 4096 + g * WS)
            GL = cfg.get('glu', 4)
            for blk in range(WS // 128):
                cb = g * (WS // 128) + blk
                if GL < 2:
                    continue
                ba = featmajor_mm(hT, "hT", NCH, sa, blk, ntok)
                bb = featmajor_mm(hT, "hT", NCH, sbb, blk, ntok)
                if GL < 3:
                    continue
                P.op("act", lambda e, bb=bb: e.activation(out=tmpf[2][:, 0:ntok], in_=ps[bb][:, 0:ntok], func=AF.Sigmoid),
                     [PSN[bb]], ["tmpf2"], stream=True)
                if GL < 4:
                    continue
                if cfg.get('gdst') == 'tmp':
                    P.op("dve", lambda e, ba=ba, cb=cb: e.tensor_tensor(out=tmpf[3][:, 0:ntok], in0=ps[ba][:, 0:ntok], in1=tmpf[2][:, 0:ntok], op=ALU.mult),
                         [PSN[ba], "tmpf2"], ["tmpf3"], stream=True)
                    continue
                if cfg.get('gdst') == 'sb':
                    copy_op("act", tmpf[3][:, 0:ntok], ps[ba][:, 0:ntok], [PSN[ba]], ["tmpf3"])
                    P.op("dve", lambda e, ba=ba, cb=cb: e.tensor_tensor(out=dstfn(cb), in0=inview(tmpf[3][:, 0:ntok]), in1=inview(tmpf[2][:, 0:ntok]), op=ALU.mult),
                         ["tmpf3", "tmpf2"], [dstname], stream=True)
                    continue
                P.op("dve", lambda e, ba=ba, cb=cb: e.tensor_tensor(out=dstfn(cb), in0=inview(ps[ba][:, 0:ntok]), in1=inview(tmpf[2][:, 0:ntok]), op=ALU.mult),
                     [PSN[ba], "tmpf2"], [dstname], stream=True)

    def conv_phase(extv, accv, ntok_shape):
        for cb in range(8):
            P.op("dve", lambda e, cb=cb: e.tensor_scalar(out=accv(cb), in0=extv(cb, 0), scalar1=col(P_WDW + cb), scalar2=col(P_BDW + cb),
                                                          op0=ALU.mult, op1=ALU.add), ["ext", "par"], ["acc%d" % cb], stream=True)
        for k in range(1, 31):
            for cb in range(8):
                P.op("dve", lambda e, cb=cb, k=k: e.scalar_tensor_tensor(out=accv(cb), in0=extv(cb, k), scalar=col(P_WDW + k * 8 + cb),
                                                                          in1=accv(cb), op0=ALU.mult, op1=ALU.add),
                     ["ext", "acc%d" % cb, "par"], ["acc%d" % cb], stream=True)

    def conv_norm(ntok):
        b1 = nps()
        b2 = nps()
        for cb in range(8):
            P.op("pe", lambda e, cb=cb: e.matmul(ps[b1][:, 0:ntok], lhsT=onesF, rhs=acc[:, cb, 0:ntok], start=(cb == 0), stop=(cb == 7)),
                 ["acc%d" % cb, "cst"], [PSN[b1]])
        for cb in range(8):
            t = cb % 2
            P.op("act", lambda e, cb=cb, t=t: e.activation(out=tmpf[t][:, 0:ntok], in_=acc[:, cb, 0:ntok], func=AF.Square),
                 ["acc%d" % cb], ["tmpf%d" % t], stream=True)
            P.op("pe", lambda e, cb=cb, t=t: e.matmul(ps[b2][:, 0:ntok], lhsT=onesF, rhs=tmpf[t][:, 0:ntok], start=(cb == 0), stop=(cb == 7)),
                 ["tmpf%d" % t, "cst"], [PSN[b2]])
        P.op("dve", lambda e: e.tensor_scalar(out=mean[:, 0:ntok], in0=ps[b1][:, 0:ntok], scalar1=1.0 / CC, scalar2=None, op0=ALU.mult),
             [PSN[b1]], ["mean"], stream=False)
        P.op("dve", lambda e: e.tensor_tensor(out=tmpf[3][:, 0:ntok], in0=mean[:, 0:ntok], in1=mean[:, 0:ntok], op=ALU.mult),
             ["mean"], ["tmpf3"], stream=False)
        P.op("dve", lambda e: e.scalar_tensor_tensor(out=rstd[:, 0:ntok], in0=ps[b2][:, 0:ntok], scalar=1.0 / CC, in1=tmpf[3][:, 0:ntok],
                                                     op0=ALU.mult, op1=ALU.subtract), [PSN[b2], "tmpf3"], ["rstd"], stream=False)
        rsqrt_op(rstd[:, 0:ntok], rstd[:, 0:ntok], ["rstd"], "rstd", 1.0)
        for cb in range(8):
            if cfg.get('convl', 3) < 3:
                break
            t = 2 + cb % 2
            P.op("dve", lambda e, cb=cb, t=t: e.tensor_tensor(out=tmpf[t][:, 0:ntok], in0=acc[:, cb, 0:ntok], in1=mean[:, 0:ntok], op=ALU.subtract),
                 ["acc%d" % cb, "mean"], ["tmpf%d" % t], stream=False)
            P.op("dve", lambda e, cb=cb, t=t: e.tensor_tensor(out=tmpf[t][:, 0:ntok], in0=tmpf[t][:, 0:ntok], in1=rstd[:, 0:ntok], op=ALU.mult),
                 ["tmpf%d" % t, "rstd"], ["tmpf%d" % t], stream=False)
            if cfg.get('silu', 'split') == 'fused':
                P.op("act", lambda e, cb=cb, t=t: e.activation(out=cnT[:, cb, 0:ntok], in_=tmpf[t][:, 0:ntok], func=AF.Silu,
                                                               bias=col(P_BCN + cb), scale=col(P_GCN + cb)),
                     ["tmpf%d" % t, "par"], ["cnT"], stream=False)
            else:
                P.op("act", lambda e, cb=cb, t=t: e.activation(out=tmpf[t][:, 0:ntok], in_=tmpf[t][:, 0:ntok], func=AF.Identity,
                                                               bias=col(P_BCN + cb), scale=col(P_GCN + cb)),
                     ["tmpf%d" % t, "par"], ["tmpf%d" % t], stream=False)
                P.op("act", lambda e, cb=cb, t=t: e.activation(out=tmpf[t - 2][:, 0:ntok], in_=tmpf[t][:, 0:ntok], func=AF.Sigmoid),
                     ["tmpf%d" % t], ["tmpf%d" % (t - 2)], stream=False)
                P.op("dve", lambda e, cb=cb, t=t: e.tensor_tensor(out=cnT[:, cb, 0:ntok], in0=tmpf[t][:, 0:ntok], in1=tmpf[t - 2][:, 0:ntok], op=ALU.mult),
                     ["tmpf%d" % t, "tmpf%d" % (t - 2)], ["cnT"], stream=False)

    def head_finish(o_dst, w):
        P.op("dve", lambda e: e.reciprocal(out=tmpf[0][:, 0:2 * w], in_=ps[1][:, 0:2 * w]), [PSN[1]], ["tmpf0"], stream=False)
        P.op("dve", lambda e: e.tensor_tensor(out=tmpf[1][:, 0:2 * w], in0=ps[0][:, 0:2 * w], in1=tmpf[0][:, 0:2 * w], op=ALU.mult),
             [PSN[0], "tmpf0"], ["tmpf1"], stream=False)
        P.op("dve", lambda e: e.scalar_tensor_tensor(out=tmpf[2][:, 0:w], in0=tmpf[1][:, w:2 * w], scalar=neglam, in1=tmpf[1][:, 0:w],
                                                     op0=ALU.mult, op1=ALU.add), ["tmpf1", "sc"], ["tmpf2"], stream=False)
        P.op("act", lambda e: e.activation(out=tmpf[3][:, 0:w], in_=tmpf[2][:, 0:w], func=AF.Square), ["tmpf2"], ["tmpf3"], stream=False)
        b = nps()
        P.op("pe", lambda e: e.matmul(ps[b][:, 0:w], lhsT=onesF, rhs=tmpf[3][:, 0:w], start=True, stop=True), ["tmpf3", "cst"], [PSN[b]])
        rsqrt_op(tmpf[0][:, 0:w], ps[b][:, 0:w], [PSN[b]], "tmpf0", 1.0 / 128)
        P.op("dve", lambda e: e.scalar_tensor_tensor(out=o_dst, in0=tmpf[2][:, 0:w], scalar=gsub, in1=tmpf[0][:, 0:w], op0=ALU.mult, op1=ALU.mult),
             ["tmpf2", "tmpf0", "sc"], ["oT"], stream=False)

    def prompt_attention(i):
        ktiles = [("o", o) for o in range(3 * i + 3)] + [("w", s) for s in range(i + 1)]
        ATL = cfg.get('attl', 4)
        nkt = len(ktiles)
        for h in range(NH):
            first = True
            sbi = 0
            for c0 in range(0, nkt, KC):
                chunk = ktiles[c0:c0 + KC]
                cb_ = (c0 // KC) % 2
                kres, vres = "KTc%d" % cb_, "Vc%d" % cb_
                runs = []
                for idx, (kind, n) in enumerate(chunk):
                    ksl = n if kind == "o" else NOTH + n
                    vsrc = ("vm", n) if (kind == "o" and n >= 3 * i) else ("v", ksl)
                    runs.append((idx, ksl, vsrc))
                j = 0
                while j < len(runs):
                    j2 = j
                    while j2 + 1 < len(runs) and runs[j2 + 1][1] == runs[j2][1] + 1:
                        j2 += 1
                    a, bnd = runs[j][1], runs[j2][1] + 1
                    P.op("sp", lambda e, j=j, a=a, bnd=bnd, h=h, cb_=cb_: e.dma_start(out=KTc[cb_][:, j:j + bnd - a, :], in_=kT_scr[h, :, a:bnd, :]),
                         reads=["kTs%d" % t for t in range(a, bnd)], writes=[kres], dma="lk%d" % cb_)
                    j = j2 + 1
                j = 0
                while j < len(runs):
                    j2 = j
                    while j2 + 1 < len(runs) and runs[j2 + 1][2][0] == runs[j2][2][0] and runs[j2 + 1][2][1] == runs[j2][2][1] + 1:
                        j2 += 1
                    kindv, a = runs[j][2]
                    bnd = runs[j2][2][1] + 1
                    srcv = vm_scr if kindv == "vm" else v_scr
                    rn = [("vms%d" if kindv == "vm" else "vs%d") % t for t in range(a, bnd)]
                    P.op("sp", lambda e, j=j, a=a, bnd=bnd, h=h, cb_=cb_, srcv=srcv: e.dma_start(
                        out=Vc[cb_][:, j * NSB:(j + bnd - a) * NSB, :], in_=srcv[h, :, a:bnd, :, :].rearrange("p t s e -> p (t s) e")),
                         reads=rn, writes=[vres], dma="lv%d" % cb_)
                    j = j2 + 1
                for idx, (kind, n) in enumerate(chunk):
                    diag = (kind == "w" and n == i)
                    lones = onesf[:, n, :] if (kind == "o" and n >= 3 * i) else onesb[:, :]
                    lon = "onesf" if (kind == "o" and n >= 3 * i) else "onesb"
                    for sbk in range(NSB):
                        last = (c0 + idx == nkt - 1) and (sbk == NSB - 1)
                        q0 = sbk * 128 if diag else 0
                        wq = TT - q0
                        sbank = 2 + (sbi % 2)
                        pslot = sbi % 2
                        sbi += 1
                        sv = ps[sbank][:, :].rearrange("p (m q) -> p m q", m=2)
                        for m in range((cfg.get('nm', 2)) if ATL >= 2 else 0):
                            P.op("pe", lambda e, m=m, idx=idx, sbk=sbk, q0=q0, cb_=cb_, sbank=sbank, h=h: e.matmul(
                                ps[sbank][:, m * TT + q0:(m + 1) * TT],
                                lhsT=KTc[cb_][:, idx, sbk * 128:(sbk + 1) * 128],
                                rhs=QT[:, m, h, q0:TT], start=True, stop=True),
                                 [kres, "QT"], [PSN[sbank]])
                        if ATL < 2:
                            continue
                        pv = Pt[pslot][:, 0:2 * TT].rearrange("p (m q) -> p m q", m=2)
                        P.op("act", lambda e, q0=q0, pv=pv, sv=sv: e.activation(out=pv[:, :, q0:TT], in_=sv[:, :, q0:TT], func=AF.Exp, scale=0.125),
                             [PSN[sbank]], ["Pt%d" % pslot], stream=True)
                        if diag:
                            P.op("dve", lambda e, q0=q0, pv=pv: e.memset(pv[64:128, :, q0:q0 + 64], 0.0), ["Pt%d" % pslot], ["Pt%d" % pslot], stream=False)
                        if ATL < 3:
                            continue
                        if q0 == 0:
                            P.op("pe", lambda e, idx=idx, sbk=sbk, cb_=cb_, pslot=pslot, first=first, last=last: e.matmul(
                                ps[0][:, 0:2 * TT], lhsT=Vc[cb_][:, idx * NSB + sbk, :], rhs=Pt[pslot][:, 0:2 * TT], start=first, stop=last),
                                 [vres, "Pt%d" % pslot], [PSN[0]])
                            P.op("pe", lambda e, pslot=pslot, first=first, last=last, lones=lones: e.matmul(
                                ps[1][:, 0:2 * TT], lhsT=lones, rhs=Pt[pslot][:, 0:2 * TT], start=first, stop=last),
                                 [lon, "Pt%d" % pslot], [PSN[1]])
                        else:
                            for m in range(2):
                                lm = last and m == 1
                                P.op("pe", lambda e, m=m, idx=idx, sbk=sbk, cb_=cb_, pslot=pslot, q0=q0, lm=lm: e.matmul(
                                    ps[0][:, m * TT + q0:(m + 1) * TT], lhsT=Vc[cb_][:, idx * NSB + sbk, :],
                                    rhs=Pt[pslot][:, m * TT + q0:(m + 1) * TT], start=False, stop=lm),
                                     [vres, "Pt%d" % pslot], [PSN[0]])
                                P.op("pe", lambda e, m=m, pslot=pslot, q0=q0, lm=lm, lones=lones: e.matmul(
                                    ps[1][:, m * TT + q0:(m + 1) * TT], lhsT=lones, rhs=Pt[pslot][:, m * TT + q0:(m + 1) * TT], start=False, stop=lm),
                                     [lon, "Pt%d" % pslot], [PSN[1]])
                        first = False
            if ATL >= 4:
                head_finish(oT[:, h, 0:TT], TT)

    def sample_attention():
        for bt in range(4):
            first = True
            for kb in range(33):
                nk = 128 if kb < 32 else 32
                cv_ = kb % 2
                if kb < 32:
                    load_cast(ckb, cache_k[bt, kb * 128:(kb + 1) * 128, :], "KTc0")
                    load_cast(cvb[cv_], cache_v[bt, kb * 128:(kb + 1) * 128, :], "Vc%d" % cv_)
                    transpose8(ckb, "KTc0", ckT, "KTc1")
                    kTsrc, kTname, kc0 = ckT, "KTc1", 0
                    vsrc, vname = cvb[cv_], "Vc%d" % cv_
                else:
                    kTsrc, kTname, kc0 = ktr, "ktr", bt * 32
                    P.op("sp", lambda e, bt=bt, cv_=cv_: e.dma_start(out=cvb[cv_][0:32, :], in_=vbf[bt * 32:(bt + 1) * 32, :]),
                         reads=["vbf0"], writes=["Vc%d" % cv_], dma="mv%d" % cv_)
                    vsrc, vname = cvb[cv_], "Vc%d" % cv_
                sbank = 2 + (kb % 2)
                pslot = kb % 2
                for h in range(NH):
                    for m in range(2):
                        cidx = (h * 2 + m) * 32
                        P.op("pe", lambda e, h=h, m=m, cidx=cidx, nk=nk, sbank=sbank, bt=bt, kTsrc=kTsrc, kc0=kc0: e.matmul(
                            ps[sbank][0:nk, cidx:cidx + 32], lhsT=kTsrc[:, h, kc0:kc0 + nk],
                            rhs=QT[:, m, h, bt * 32:(bt + 1) * 32], start=True, stop=True),
                             [kTname, "QT"], [PSN[sbank]])
                P.op("act", lambda e, nk=nk, sbank=sbank, pslot=pslot: e.activation(out=Pt[pslot][0:nk, :], in_=ps[sbank][0:nk, :], func=AF.Exp, scale=0.125),
                     [PSN[sbank]], ["Pt%d" % pslot], stream=True)
                last = kb == 32
                if first:
                    P.op("pe", lambda e: e.matmul(ps[0][:, :], lhsT=onesb[:, :], rhs=zrhs[:, :], start=True, stop=False), ["onesb", "zrhs"], [PSN[0]])
                for h in range(NH):
                    cidx = h * 64
                    P.op("pe", lambda e, h=h, cidx=cidx, nk=nk, pslot=pslot, first=first, last=last, vsrc=vsrc: e.matmul(
                        ps[0][:, cidx:cidx + 64], lhsT=vsrc[0:nk, h * 128:(h + 1) * 128], rhs=Pt[pslot][0:nk, cidx:cidx + 64],
                        start=False, stop=(last and h == NH - 1)), [vname, "Pt%d" % pslot], [PSN[0]])
                P.op("pe", lambda e, nk=nk, pslot=pslot, first=first, last=last: e.matmul(
                    ps[1][:, :], lhsT=onesb[0:nk, :], rhs=Pt[pslot][0:nk, :], start=first, stop=last), ["onesb", "Pt%d" % pslot], [PSN[1]])
                first = False
            P.op("dve", lambda e: e.reciprocal(out=tmpf[0][:, :], in_=ps[1][:, :]), [PSN[1]], ["tmpf0"], stream=False)
            P.op("dve", lambda e: e.tensor_tensor(out=tmpf[1][:, :], in0=ps[0][:, :], in1=tmpf[0][:, :], op=ALU.mult), [PSN[0], "tmpf0"], ["tmpf1"], stream=False)
            t1 = tmpf[1][:, :].rearrange("p (h m q) -> p h m q", m=2, q=32)
            o2 = tmpf[2][:, 0:256].rearrange("p (h q) -> p h q", q=32)
            P.op("dve", lambda e, t1=t1, o2=o2: e.scalar_tensor_tensor(out=o2, in0=t1[:, :, 1, :], scalar=neglam, in1=t1[:, :, 0, :], op0=ALU.mult, op1=ALU.add),
                 ["tmpf1", "sc"], ["tmpf2"], stream=False)
            P.op("act", lambda e: e.activation(out=tmpf[3][:, 0:256], in_=tmpf[2][:, 0:256], func=AF.Square), ["tmpf2"], ["tmpf3"], stream=False)
            b = nps()
            P.op("pe", lambda e, b=b: e.matmul(ps[b][:, 0:256], lhsT=onesF, rhs=tmpf[3][:, 0:256], start=True, stop=True), ["tmpf3", "cst"], [PSN[b]])
            rsqrt_op(tmpf[0][:, 0:256], ps[b][:, 0:256], [PSN[b]], "tmpf0", 1.0 / 128)
            r0 = tmpf[0][:, 0:256].rearrange("p (h q) -> p h q", q=32)
            P.op("dve", lambda e, bt=bt, o2=o2, r0=r0: e.scalar_tensor_tensor(out=oT[:, :, bt * 32:(bt + 1) * 32], in0=o2, scalar=gsub, in1=r0, op0=ALU.mult, op1=ALU.mult),
                 ["tmpf2", "tmpf0", "sc"], ["oT"], stream=False)

    def merge_phase(ntok):
        for jg in range(D // WS):
            s_ao = load_w(w_ao, 0, 8, jg * WS)
            for blk in range(2):
                b = featmajor_mm(oT, "oT", 8, s_ao, blk, ntok)
                copy_op("act", tA[:, blk, 0:ntok], ps[b][:, 0:ntok], [PSN[b]], ["tA"])
            s_ga = load_w(w_in, 0, NCH, 5120 + jg * WS)
            for blk in range(2):
                b = featmajor_mm(hT, "hT", NCH, s_ga, blk, ntok)
                P.op("act", lambda e, b=b: e.activation(out=tmpf[2][:, 0:ntok], in_=ps[b][:, 0:ntok], func=AF.Sigmoid), [PSN[b]], ["tmpf2"], stream=True)
                P.op("dve", lambda e, blk=blk: e.tensor_tensor(out=tA[:, blk, 0:ntok], in0=tA[:, blk, 0:ntok], in1=tmpf[2][:, 0:ntok], op=ALU.mult),
                     ["tA", "tmpf2"], ["tA"], stream=True)
            s_co = load_w(w_co, 0, 8, jg * WS)
            for blk in range(2):
                j = jg * 2 + blk
                b = featmajor_mm(cnT, "cnT", 8, s_co, blk, ntok)
                P.op("dve", lambda e, b=b, blk=blk, j=j: e.tensor_scalar(out=tB[:, blk, 0:ntok], in0=ps[b][:, 0:ntok], scalar1=col(P_BCO + j), scalar2=None, op0=ALU.add),
                     [PSN[b], "par"], ["tB"], stream=True)
            s_gb = load_w(w_in, 0, NCH, 7168 + jg * WS)
            for blk in range(2):
                j = jg * 2 + blk
                b = featmajor_mm(hT, "hT", NCH, s_gb, blk, ntok)
                P.op("act", lambda e, b=b: e.activation(out=tmpf[3][:, 0:ntok], in_=ps[b][:, 0:ntok], func=AF.Sigmoid), [PSN[b]], ["tmpf3"], stream=True)
                P.op("dve", lambda e, blk=blk: e.tensor_tensor(out=tB[:, blk, 0:ntok], in0=tB[:, blk, 0:ntok], in1=tmpf[3][:, 0:ntok], op=ALU.mult),
                     ["tB", "tmpf3"], ["tB"], stream=True)
                P.op("dve", lambda e, blk=blk, j=j: e.tensor_tensor(out=mrgT[:, j, 0:ntok], in0=tA[:, blk, 0:ntok], in1=tB[:, blk, 0:ntok], op=ALU.add),
                     ["tA", "tB"], ["mrgT", "KTc0", "KTc1"], stream=True)

    def proj_norm_residual(srcT, srcname, nk_total, wsrc, gbase, ntok):
        for jg in range(D // WS):
            kgs = [(k0, min(NCH, nk_total - k0)) for k0 in range(0, nk_total, NCH)]
            bs = [nps(), nps()]
            for gi, (k0, nk) in enumerate(kgs):
                slot = load_w(wsrc, k0, nk, jg * WS)
                for blk in range(2):
                    b = bs[blk]
                    for c in range(nk):
                        P.op("pe", lambda e, c=c, b=b, blk=blk, slot=slot, k0=k0, gi=gi, nk=nk: e.matmul(
                            ps[b][:, 0:ntok], lhsT=wsl[slot][:, c, blk * 128:(blk + 1) * 128], rhs=srcT[:, k0 + c, 0:ntok],
                            start=(gi == 0 and c == 0), stop=(gi == len(kgs) - 1 and c == nk - 1)),
                             (([srcname] + wres(slot)) + (["ext", "cnT"] + ACCN if srcname == "actT" else ["KTc0", "KTc1"] if srcname == "mrgT" else [])), [PSN[b]])
            for blk in range(2):
                copy_op(evac_eng(), bufA[:, jg * 2 + blk, 0:ntok], ps[bs[blk]][:, 0:ntok], [PSN[bs[blk]]], ["bufA"])
        norm_stats(bufA, "bufA", NCH, ntok, 1.0 / D, None)
        for c in range(NCH):
            P.op("dve", lambda e, c=c: e.scalar_tensor_tensor(out=bufA[:, c, 0:ntok], in0=bufA[:, c, 0:ntok], scalar=col(gbase + c), in1=rstd[:, 0:ntok],
                                                              op0=ALU.mult, op1=ALU.mult), ["bufA", "rstd", "par"], ["bufA"], stream=True)
            P.op("dve", lambda e, c=c: e.tensor_tensor(out=xT[:, c, 0:ntok], in0=xT[:, c, 0:ntok], in1=bufA[:, c, 0:ntok], op=ALU.add),
                 ["bufA", "xT"], ["xT"], stream=True)

    def ffn_act(ntok):
        for fg in range(DFF // WS):
            sg = load_w(w_fg, 0, NCH, fg * WS)
            su = load_w(w_fu, 0, NCH, fg * WS)
            for blk in range(2):
                bg = featmajor_mm(hT, "hT", NCH, sg, blk, ntok)
                bu = featmajor_mm(hT, "hT", NCH, su, blk, ntok)
                P.op("act", lambda e, bg=bg: e.activation(out=tmpf[2][:, 0:ntok], in_=ps[bg][:, 0:ntok], func=AF.Sigmoid), [PSN[bg]], ["tmpf2"], stream=True)
                P.op("dve", lambda e, bg=bg: e.tensor_tensor(out=tmpf[2][:, 0:ntok], in0=ps[bg][:, 0:ntok], in1=tmpf[2][:, 0:ntok], op=ALU.mult),
                     [PSN[bg], "tmpf2"], ["tmpf2"], stream=True)
                P.op("dve", lambda e, bu=bu, fg=fg, blk=blk: e.tensor_tensor(out=actT[:, fg * 2 + blk, 0:ntok], in0=ps[bu][:, 0:ntok], in1=tmpf[2][:, 0:ntok], op=ALU.mult),
                     [PSN[bu], "tmpf2"], ["actT", "ext", "cnT"] + ACCN, stream=True)

    def store_y(ydst, row0, nsub):
        for s in range(nsub):
            xi = 0
            for c4 in range(4):
                b = nps()
                for k in range(4):
                    c = c4 * 4 + k
                    P.op("pe", lambda e, c=c, k=k, b=b, s=s: e.transpose(out=ps[b][:, k * 128:(k + 1) * 128], in_=xT[:, c, s * 128:(s + 1) * 128], identity=identf),
                         ["xT", "cst"], [PSN[b]])
                copy_op(evac_eng(), xin[xi][:, c4 * 512:(c4 + 1) * 512], ps[b][:, :], [PSN[b]], ["xin%d" % xi])
            P.op("act", lambda e, xi=xi, s=s: e.dma_start(out=ydst[row0 + s * 128: row0 + (s + 1) * 128, :], in_=xin[xi][:]),
                 reads=["xin%d" % xi], dma="y%d" % xi)

    def layer_tail(ntok):
        merge_phase(ntok)
        proj_norm_residual(mrgT, "mrgT", NCH, w_o, P_GPOST, ntok)
        norm_stats(xT, "xT", NCH, ntok, 1.0 / D, None)
        for c in range(NCH):
            P.op("dve", lambda e, c=c: e.scalar_tensor_tensor(out=hT[:, c, 0:ntok], in0=xT[:, c, 0:ntok], scalar=col(P_GFFN + c), in1=rstd[:, 0:ntok],
                                                              op0=ALU.mult, op1=ALU.mult), ["xT", "rstd", "par"], ["hT"], stream=True)
        ffn_act(ntok)
        proj_norm_residual(actT, "actT", NFC, w_fd, P_GPFF, ntok)

    NHS = (NOWN * 32) // 128
    halov = halo[:, :, :, :].rearrange("p c i t -> p c (i t)")
    C_ST = cfg.get('stage', 3)
    if C_HALO:
        if C_ST >= 1:
            load_xT(x_halo, 0, NHS, NHS * 128)
        if C_ST >= 2:
            make_hT(P_GPRE, NHS * 128)
        if C_ST >= 3:
            glu_phase(NHS * 128, lambda cb: halov[:, cb, :], "halo")

    for o in range(C_NOTH):
        load_xT(x_oth, o * TT, NSB, TT)
        make_hT(P_GPRE, TT)
        qkv_phase(o, o * NSB, NSB, None, None, 0, False, flag_o=o)

    for i in range(C_NOWN):
        load_xT(x_own, i * TT, NSB, TT)
        make_hT(P_GPRE, TT)
        qkv_phase(NOTH + i, (NOTH + i) * NSB, NSB, k_own, v_own, i * TT, True)
        for cb in range(8):
            copy_op("act", ext[:, cb, 0:32], halo[:, cb, i, :], ["halo"], ["ext"], stream=False)
        glu_phase(TT, lambda cb: ext[:, cb, 32:32 + TT], "ext")
        if i == NOWN - 1:
            b = nps()
            for cb in range(8):
                P.op("pe", lambda e, cb=cb, b=b: e.transpose(out=ps[b][0:32, (cb % 4) * 128:(cb % 4 + 1) * 128],
                                                             in_=ext[:, cb, TT:TT + 32], identity=identf), ["ext", "cst"], [PSN[b]])
                if cb % 4 == 3:
                    copy_op("dve", stc[0:32, (cb - 3) * 128:(cb + 1) * 128], ps[b][0:32, :], [PSN[b]], ["kvst0"], stream=False)
                    if cb == 3:
                        b = nps()
            P.op("act", lambda e: e.dma_start(out=conv_p[:, :], in_=stc[2:32, :]), reads=["kvst0"], dma="oc")
        if cfg.get('conv', True):
            conv_phase(lambda cb, k: ext[:, cb, 2 + k:2 + k + TT], lambda cb: acc[:, cb, 0:TT], None)
            if cfg.get('convl', 3) >= 2:
                conv_norm(TT)
        if cfg.get('attn', True):
            prompt_attention(i)
        if C_TAIL:
            layer_tail(TT)
        store_y(y_own, i * TT, NSB)

    if C_SMP:
        load_xT(x_smp, 0, 1, 128)
        make_hT(P_GPRE, 128)
        qkv_phase(None, NTL * NSB, 1, k_s, v_s, 0, True)
        transpose8(kbf, "kbf", ktr[:, :, :], "ktr")
        extS = ext[:, :, 0:248].rearrange("p c (b t) -> p c b t", t=62)
        for bt in range(4):
            P.op("sp", lambda e, bt=bt: e.dma_start(out=stc[0:30, :], in_=state_conv[bt, :, :]), writes=["kvst0"], dma="sc")
            b = nps()
            for cb in range(8):
                P.op("pe", lambda e, cb=cb, b=b: e.transpose(out=ps[b][:, cb * 32:cb * 32 + 30], in_=stc[0:30, cb * 128:(cb + 1) * 128], identity=identf[0:30, 0:30]),
                     ["kvst0", "cst"], [PSN[b]])
            copy_op("dve", extS[:, :, bt, 0:30], ps[b][:, 0:256].rearrange("p (c t) -> p c t", t=32)[:, :, 0:30], [PSN[b]], ["ext"], stream=False)
        glu_phase(128, lambda cb: extS[:, cb, :, 30:62], "ext", inview=lambda a: a.rearrange("p (b t) -> p b t", t=32))
        for bt in range(4):
            b = nps()
            for cb in range(8):
                P.op("pe", lambda e, cb=cb, b=b, bt=bt: e.transpose(out=ps[b][0:32, (cb % 4) * 128:(cb % 4 + 1) * 128], in_=extS[:, cb, bt, 30:62], identity=identf),
                     ["ext", "cst"], [PSN[b]])
                if cb % 4 == 3:
                    copy_op("dve", stc[0:32, (cb - 3) * 128:(cb + 1) * 128], ps[b][0:32, :], [PSN[b]], ["kvst0"], stream=False)
                    if cb == 3:
                        b = nps()
            P.op("act", lambda e, bt=bt: e.dma_start(out=conv_s[bt, :, :], in_=stc[2:32, :]), reads=["kvst0"], dma="oc")
        conv_phase(lambda cb, k: extS[:, cb, :, k:k + 32], lambda cb: acc[:, cb, 0:128].rearrange("p (b t) -> p b t", t=32), None)
        conv_norm(128)
        sample_attention()
        layer_tail(128)
        store_y(y_s, 0, 1)

    print("ops:", {e: len(v) for e, v in P.ops.items()}, flush=True)
    with nc.Block() as block:
        semcms = P.emit(nc, block)
    return nc


_NC_CACHE = {}


def _rope_tab(pos):
    inv = (500000.0 ** (-np.arange(0, 16, 2, dtype=np.float32) / 16.0)).astype(np.float32)
    ang = pos.astype(np.float32)[:, None] * inv[None, :]
    return np.concatenate([np.cos(ang), np.sin(ang)], axis=1).astype(np.float32)


def prepare(inputs):
    f = lambda k: np.ascontiguousarray(np.asarray(inputs[k], dtype=np.float32))
    xp, xs = f("x_prompt"), f("x_sample")
    ck, cv, scv = f("cache_k")[0], f("cache_v")[0], f("state_conv")[0]
    par = np.zeros((128, NPAR), np.float32)
    colmaj = lambda v, n: np.ascontiguousarray(v.reshape(n, 128).T)
    par[:, P_GPRE:P_GPRE + 16] = colmaj(f("g_pre_mix")[0], 16)
    par[:, P_GPOST:P_GPOST + 16] = colmaj(f("g_post_mix")[0], 16)
    par[:, P_GFFN:P_GFFN + 16] = colmaj(f("g_pre_ffn")[0], 16)
    par[:, P_GPFF:P_GPFF + 16] = colmaj(f("g_post_ffn")[0], 16)
    par[:, P_BCO:P_BCO + 16] = colmaj(f("b_conv_out")[0], 16)
    par[:, P_GSUB] = f("g_subln")[0]
    wdw = f("w_dw")[0]
    par[:, P_WDW:P_WDW + 248] = wdw.reshape(31, 8, 128).transpose(2, 0, 1).reshape(128, 248)
    par[:, P_BDW:P_BDW + 8] = colmaj(f("b_dw")[0], 8)
    par[:, P_GCN:P_GCN + 8] = colmaj(f("g_conv_norm")[0], 8)
    par[:, P_BCN:P_BCN + 8] = colmaj(f("b_conv_norm")[0], 8)
    for n, key in enumerate(["lambda_q1", "lambda_k1", "lambda_q2", "lambda_k2"]):
        par[:, P_LAM + 64 * n:P_LAM + 64 * (n + 1)] = f(key)[0][None, :]
    consts = np.concatenate([np.eye(128, dtype=np.float32), np.ones((128, 128), np.float32)], axis=1)
    w = {"w_in": f("w_in")[0], "w_ao": f("w_attn_out")[0], "w_co": f("w_conv_out")[0], "w_o": f("w_o")[0],
         "w_fg": f("w_ffn_gate")[0], "w_fu": f("w_ffn_up")[0], "w_fd": f("w_ffn_down")[0]}
    in_maps = []
    own_tiles = {}
    for core in range(8):
        b, j = core // 4, core % 4
        own = [4 * i + j for i in range(NOWN)]
        oth = [4 * wd + k for wd in range(NOWN) for k in range(4) if k != j]
        own_tiles[core] = own
        xb = xp[b]
        x_oth = np.concatenate([xb[t * TT:(t + 1) * TT] for t in oth], axis=0)
        x_own = np.concatenate([xb[t * TT:(t + 1) * TT] for t in own], axis=0)
        x_halo = np.zeros((NOWN * 32, D), np.float32)
        for i, t in enumerate(own):
            if t > 0:
                x_halo[i * 32:(i + 1) * 32] = xb[t * TT - 32:t * TT]
        pos = np.concatenate([np.arange(t * TT, (t + 1) * TT) for t in oth + own] + [4096 + (np.arange(128) % 32)])
        rt = _rope_tab(pos).reshape(NTL * NSB + 1, 128, 16).transpose(1, 0, 2)
        flags = np.zeros((128, NOTH), np.float32)
        for o, t in enumerate(oth):
            flags[:, o] = 1.0 if (t % 4) < j else 0.0
        m = {"x_oth": x_oth, "x_own": x_own, "x_halo": x_halo,
             "x_smp": np.ascontiguousarray(xs[4 * core:4 * core + 4].reshape(128, D)),
             "rope": np.ascontiguousarray(rt), "flags": flags, "params": par, "consts": consts,
             "cache_k": np.ascontiguousarray(ck[4 * core:4 * core + 4].reshape(4, 4096, 1024)),
             "cache_v": np.ascontiguousarray(cv[4 * core:4 * core + 4].reshape(4, 4096, 1024)),
             "state_conv": np.ascontiguousarray(scv[4 * core:4 * core + 4])}
        m.update(w)
        in_maps.append(m)
    return in_maps, own_tiles


def kernel(**inputs):
    in_maps, own_tiles = prepare(inputs)
    if "nc" not in _NC_CACHE:
        _NC_CACHE["nc"] = build_nc()
    res = run_bass_kernel_spmd(_NC_CACHE["nc"], in_maps, core_ids=list(range(8)))
    R = res.results
    yp = np.zeros((2, SEQ, D), np.float32)
    kp = np.zeros((1, 2, SEQ, 1024), np.float32)
    vp = np.zeros((1, 2, SEQ, 1024), np.float32)
    cp = np.zeros((1, 2, 30, CC), np.float32)
    ys = np.zeros((32, 32, D), np.float32)
    ks = np.zeros((1, 32, 32, 1024), np.float32)
    vs = np.zeros((1, 32, 32, 1024), np.float32)
    cs = np.zeros((1, 32, 30, CC), np.float32)
    for core in range(8):
        b, j = core // 4, core % 4
        r = R[core]
        for i, t in enumerate(own_tiles[core]):
            yp[b, t * TT:(t + 1) * TT] = r["y_own"][i * TT:(i + 1) * TT]
            kp[0, b, t * TT:(t + 1) * TT] = r["k_own"][i * TT:(i + 1) * TT]
            vp[0, b, t * TT:(t + 1) * TT] = r["v_own"][i * TT:(i + 1) * TT]
        if j == 3:
            cp[0, b] = r["conv_p"]
        ys[4 * core:4 * core + 4] = r["y_s"].reshape(4, 32, D)
        ks[0, 4 * core:4 * core + 4] = r["k_s"].reshape(4, 32, 1024)
        vs[0, 4 * core:4 * core + 4] = r["v_s"].reshape(4, 32, 1024)
        cs[0, 4 * core:4 * core + 4] = r["conv_s"]
    return (yp, ys, kp.reshape(1, 2, SEQ, 8, 2, 64), vp.reshape(1, 2, SEQ, 8, 128), cp,
            ks.reshape(1, 32, 32, 8, 2, 64), vs.reshape(1, 32, 32, 8, 128), cs)
```

```python
import math
import numpy as np
import ml_dtypes
import concourse.bass as bass
import concourse.mybir as mybir
from concourse.bass_utils import run_bass_kernel_spmd

F32 = mybir.dt.float32
BF16 = mybir.dt.bfloat16
ALU = mybir.AluOpType
AF = mybir.ActivationFunctionType
AX = mybir.AxisListType

D = 2048
NCH = 16
SEQ = 8192
NH = 8
CC = 1024
DFF = 5632
NFC = 44
INC = 9216
EPS = 1e-6
TT = 256
NSB = TT // 128
NOWN = 2048 // TT
NOTH = 3 * NOWN
NTL = NOTH + NOWN
WS = 256
KC = 8
LAM_INIT = 0.8 - 0.6 * math.exp(0.0)
SKIP_STREAM_SYNC = False

P_GPRE, P_GPOST, P_GFFN, P_GPFF, P_BCO = 0, 16, 32, 48, 64
P_GSUB = 80
P_WDW = 81
P_BDW = P_WDW + 248
P_GCN = P_BDW + 8
P_BCN = P_GCN + 8
P_LAM = P_BCN + 8
NPAR = P_LAM + 256


class Prog:
    ENGS = ["pe", "act", "dve", "pool", "sp"]

    def __init__(self):
        self.ops = {e: [] for e in self.ENGS}
        self.res = {}
        self.dma_cnt = {}
        self.dma_ep = {}

    def op(self, eng, fn, reads=(), writes=(), dma=None, stream=False, join=False):
        o = {"eng": eng, "fn": fn, "deps": [], "needed": False, "dma": dma, "stream": stream}
        deps = []
        for r in reads:
            st = self.res.setdefault(r, {"w": None, "r": []})
            if st["w"] is not None:
                deps.append(st["w"])
        for w in writes:
            st = self.res.setdefault(w, {"w": None, "r": []})
            if st["w"] is not None and not (join and st["w"].get("dma_base") == dma):
                deps.append(st["w"])
            deps.extend(st["r"])
        seen = set()
        for a in deps:
            if id(a) in seen or a is o:
                continue
            seen.add(id(a))
            if a["dma"] is None and a["eng"] == eng:
                if eng in ("pe", "sp"):
                    continue
                if SKIP_STREAM_SYNC and a["stream"] and stream:
                    continue
            if a["dma"] is None:
                a["needed"] = True
            o["deps"].append(a)
        for r in reads:
            self.res[r]["r"].append(o)
        for w in writes:
            self.res[w] = {"w": o, "r": []}
        if dma is not None:
            ep = self.dma_ep.get(dma, 0)
            if not join and self.dma_cnt.get("%s_%d" % (dma, ep), 0) >= 1000:
                ep += 1
                self.dma_ep[dma] = ep
            dname = "%s_%d" % (dma, ep)
            o["dma"] = dname
            o["dma_base"] = dma
            self.dma_cnt[dname] = self.dma_cnt.get(dname, 0) + 1
            o["dmaval"] = 16 * self.dma_cnt[dname]
        self.ops[eng].append(o)
        return o

    def emit(self, nc, block):
        sems = {}
        allsems = []

        def getsem(name):
            if name not in sems:
                cm = nc.semaphore("s_" + name)
                s = cm.__enter__()
                allsems.append(cm)
                sems[name] = s
            return sems[name]

        for e in self.ENGS:
            n = 0
            ep = 0
            for o in self.ops[e]:
                if o["dma"] is None and o["needed"]:
                    if n >= 12000:
                        n = 0
                        ep += 1
                    n += 1
                    o["semval"] = n
                    o["semkey"] = "e_%s_%d" % (e, ep)
                    getsem(o["semkey"])
        for d in self.dma_cnt:
            getsem("d_" + d)

        def run(e, engobj):
            waited = {}
            for o in self.ops[e]:
                need = {}
                for a in o["deps"]:
                    if a["dma"] is not None:
                        key, val = "d_" + a["dma"], a["dmaval"]
                    else:
                        key, val = a["semkey"], a["semval"]
                    need[key] = max(need.get(key, 0), val)
                for key, val in need.items():
                    if waited.get(key, 0) >= val:
                        continue
                    waited[key] = val
                    engobj.wait_ge(sems[key], val)
                ins = o["fn"](engobj)
                if o["dma"] is not None:
                    ins.then_inc(sems["d_" + o["dma"]], 16)
                elif o["needed"]:
                    ins.then_inc(sems[o["semkey"]], 1)
            if e == "sp":
                for d, c in self.dma_cnt.items():
                    engobj.wait_ge(sems["d_" + d], 16 * c)

        block.tensor(lambda t: run("pe", t))
        block.scalar(lambda t: run("act", t))
        block.vector(lambda t: run("dve", t))
        block.gpsimd(lambda t: run("pool", t))
        block.sync(lambda t: run("sp", t))
        return allsems


def build_nc(cfg=None):
    cfg = cfg or {}
    C_HALO = cfg.get('halo', True)
    C_NOTH = cfg.get('noth', NOTH)
    C_NOWN = cfg.get('nown', NOWN)
    C_SMP = cfg.get('sample', True)
    C_TAIL = cfg.get('tail', True)
    C_SCR = cfg.get('scr', True)
    C_ROPE = cfg.get('rope', True)
    C_TR = cfg.get('tr', True)
    nc = bass.Bass("TRN2", target_bir_lowering=False)
    dt_in = lambda n, s, d=F32: nc.dram_tensor(n, list(s), d, kind="ExternalInput").ap()
    dt_out = lambda n, s: nc.dram_tensor(n, list(s), F32, kind="ExternalOutput").ap()
    x_oth = dt_in("x_oth", [NOTH * TT, D])
    x_own = dt_in("x_own", [NOWN * TT, D])
    x_halo = dt_in("x_halo", [NOWN * 32, D])
    x_smp = dt_in("x_smp", [128, D])
    rope_d = dt_in("rope", [128, NTL * NSB + 1, 16])
    flags_d = dt_in("flags", [128, NOTH])
    params_d = dt_in("params", [128, NPAR])
    consts_d = dt_in("consts", [128, 256])
    cache_k = dt_in("cache_k", [4, 4096, 1024])
    cache_v = dt_in("cache_v", [4, 4096, 1024])
    state_conv = dt_in("state_conv", [4, 30, CC])
    w_in = dt_in("w_in", [D, INC])
    w_ao = dt_in("w_ao", [CC, D])
    w_co = dt_in("w_co", [CC, D])
    w_o = dt_in("w_o", [D, D])
    w_fg = dt_in("w_fg", [D, DFF])
    w_fu = dt_in("w_fu", [D, DFF])
    w_fd = dt_in("w_fd", [DFF, D])
    y_own = dt_out("y_own", [NOWN * TT, D])
    k_own = dt_out("k_own", [NOWN * TT, 1024])
    v_own = dt_out("v_own", [NOWN * TT, 1024])
    conv_p = dt_out("conv_p", [30, CC])
    y_s = dt_out("y_s", [128, D])
    k_s = dt_out("k_s", [128, 1024])
    v_s = dt_out("v_s", [128, 1024])
    conv_s = dt_out("conv_s", [4, 30, CC])
    kT_scr = nc.dram_tensor("kT_scr", [NH, 128, NTL, TT], BF16, kind="Internal").ap()
    v_scr = nc.dram_tensor("v_scr", [NH, 128, NTL, NSB, 128], BF16, kind="Internal").ap()
    vm_scr = nc.dram_tensor("vm_scr", [NH, 128, NOTH, NSB, 128], BF16, kind="Internal").ap()

    P = Prog()
    cms = []

    def sb(name, shape, dt):
        cm = nc.sbuf_tensor(name, list(shape), dt)
        t = cm.__enter__()
        cms.append(cm)
        return t

    ps = []
    for i in range(8):
        cm = nc.psum_tensor("ps%d" % i, [128, 512], F32)
        ps.append(cm.__enter__())
        cms.append(cm)
    PSN = ["ps%d" % i for i in range(8)]

    TM = 128 + TT
    xT = sb("xT", [128, NCH, TT], F32)
    hT = sb("hT", [128, NCH, TT], BF16)
    bufA = sb("bufA", [128, NCH, TT], F32)
    xin = [sb("xin0", [128, D], F32)] * 2
    kvst = [sb("kvst%d" % i, [128, 2048], F32) for i in range(2)]
    NWSL = 2
    NSTG = 3
    wsl = [sb("wsl%d" % i, [128, NCH, WS], BF16) for i in range(NWSL)]
    stg = [sb("stg%d" % i, [128, 1024], F32) for i in range(NSTG)]
    QT = sb("QT", [128, 2, NH, TT], BF16)
    oT = sb("oT", [128, NH, TT], BF16)
    arena = sb("arena", [128, NFC * TT], BF16)
    actT = arena[:, :].rearrange("p (c t) -> p c t", t=TT)
    _e0 = 8 * (32 + TT) * 2
    ext = arena[:, 0:_e0].bitcast(F32).rearrange("p (c t) -> p c t", t=32 + TT)
    _a0 = _e0 + 8 * TT * 2
    acc = arena[:, _e0:_a0].bitcast(F32).rearrange("p (c t) -> p c t", t=TT)
    cnT = arena[:, _a0:_a0 + 8 * TT].rearrange("p (c t) -> p c t", t=TT)
    assert _a0 + 8 * TT <= NFC * TT
    kvbuf = sb("kvbuf", [128, 4 * KC * TT], BF16)
    KTc = [kvbuf[:, i * KC * TT:(i + 1) * KC * TT].rearrange("p (a b) -> p a b", b=TT) for i in range(2)]
    Vc = [kvbuf[:, (2 + i) * KC * TT:(3 + i) * KC * TT].rearrange("p (a b) -> p a b", b=128) for i in range(2)]
    mrgT = kvbuf[:, 0:NCH * TT].rearrange("p (a b) -> p a b", b=TT)
    assert NCH * TT <= 2 * KC * TT
    Pt = [sb("Pt%d" % i, [128, 512], BF16) for i in range(2)]
    tmpf = [sb("tmpf%d" % i, [128, 512], F32) for i in range(4)]
    qtmps = [sb("qtmp%d" % i, [128, 1024], F32) for i in range(2)]
    qbf = sb("qbf", [128, 1024], BF16)
    kbf = sb("kbf", [128, 1024], BF16)
    vbfs = [sb("vbf%d" % i, [128, 1024], BF16) for i in range(2)]
    vbf = vbfs[0]
    vmf = sb("vmf", [128, 1024], BF16)
    ktr = sb("ktr", [128, NH, 128], BF16)
    rtmp = [sb("rtmp%d" % i, [128, 16, 8], F32) for i in range(4)]
    tA = sb("tA", [128, 2, TT], F32)
    tB = sb("tB", [128, 2, TT], F32)
    rstd = sb("rstd", [128, TT], F32)
    mean = sb("mean", [128, TT], F32)
    par = sb("par", [128, NPAR], F32)
    rope = sb("rope_sb", [128, NTL * NSB + 1, 16], F32)
    flg = sb("flg", [128, NOTH], F32)
    cst = sb("cst_sb", [128, 256], F32)
    identb = sb("identb", [128, 128], BF16)
    onesb = sb("onesb", [128, 128], BF16)
    onesf = sb("onesf", [128, NOTH, 128], BF16)
    halo = sb("halo", [128, 8, NOWN, 32], F32)
    sc = sb("sc", [128, 16], F32)
    zrhs = sb("zrhs", [128, 512], BF16)
    ckb = KTc[0][:, :, :].rearrange("p a b -> p (a b)")[:, 0:1024]
    cvb = [Vc[i][:, :, :].rearrange("p a b -> p (a b)")[:, 0:1024] for i in range(2)]
    ckT = KTc[1][:, :, 0:128]
    stc = kvst[0][0:32, 0:CC]
    identf = cst[:, 0:128]
    onesF = cst[:, 128:256]

    state = {"ps": 0, "ws": 0, "xin": 0, "kv": 0, "ev": 0, "stg": 0, "ce": 0}

    def nps():
        i = state["ps"]
        state["ps"] = (i + 1) % 6
        return i + 2

    def evac_eng():
        state["ev"] ^= 1
        return "act" if state["ev"] else "dve"

    def copy_op(eng, out, in_, reads, writes, stream=True):
        if eng == "act":
            P.op("act", lambda e: e.copy(out=out, in_=in_), reads, writes, stream=stream)
        else:
            P.op(eng, lambda e: e.tensor_copy(out=out, in_=in_), reads, writes, stream=stream)

    ACCN = ["acc%d" % cb for cb in range(8)]

    def col(c):
        return par[:, c:c + 1]

    P.op("sp", lambda e: e.dma_start(out=par[:], in_=params_d), writes=["par"], dma="c0")
    P.op("sp", lambda e: e.dma_start(out=rope[:], in_=rope_d), writes=["rope"], dma="c1")
    P.op("sp", lambda e: e.dma_start(out=flg[:], in_=flags_d), writes=["flg"], dma="c2")
    P.op("sp", lambda e: e.dma_start(out=cst[:], in_=consts_d), writes=["cst"], dma="c3")
    copy_op("dve", identb[:], identf, ["cst"], ["identb"], stream=False)
    P.op("dve", lambda e: e.memset(zrhs[:, :], 0.0), [], ["zrhs"], stream=False)
    P.op("dve", lambda e: e.memset(QT[:, :, :, :].rearrange("p m h t -> p (m h t)"), 0.0), [], ["QT"], stream=False)
    copy_op("dve", onesb[:], onesF, ["cst"], ["onesb"], stream=False)
    for o in range(NOTH):
        P.op("dve", lambda e, o=o: e.tensor_scalar(out=onesf[:, o, :], in0=onesF, scalar1=flg[:, o:o + 1],
                                                   scalar2=None, op0=ALU.mult),
             ["cst", "flg"], ["onesf"], stream=False)
    lp = tmpf[0]
    P.op("dve", lambda e: e.tensor_tensor(out=lp[:, 0:64], in0=par[:, P_LAM:P_LAM + 64], in1=par[:, P_LAM + 64:P_LAM + 128], op=ALU.mult),
         ["par"], ["tmpf0"], stream=False)
    P.op("dve", lambda e: e.tensor_tensor(out=lp[:, 64:128], in0=par[:, P_LAM + 128:P_LAM + 192], in1=par[:, P_LAM + 192:P_LAM + 256], op=ALU.mult),
         ["par", "tmpf0"], ["tmpf0"], stream=False)
    P.op("dve", lambda e: e.tensor_reduce(out=sc[:, 3:4], in_=lp[:, 0:64], axis=AX.X, op=ALU.add), ["tmpf0"], ["sc"], stream=False)
    P.op("dve", lambda e: e.tensor_reduce(out=sc[:, 4:5], in_=lp[:, 64:128], axis=AX.X, op=ALU.add), ["tmpf0", "sc"], ["sc"], stream=False)
    P.op("act", lambda e: e.activation(out=sc[:, 5:7], in_=sc[:, 3:5], func=AF.Exp), ["sc"], ["sc"], stream=False)
    P.op("dve", lambda e: e.tensor_tensor(out=sc[:, 0:1], in0=sc[:, 5:6], in1=sc[:, 6:7], op=ALU.subtract), ["sc"], ["sc"], stream=False)
    P.op("dve", lambda e: e.tensor_scalar(out=sc[:, 1:2], in0=sc[:, 0:1], scalar1=LAM_INIT, scalar2=-1.0, op0=ALU.add, op1=ALU.mult),
         ["sc"], ["sc"], stream=False)
    P.op("dve", lambda e: e.tensor_scalar(out=sc[:, 2:3], in0=col(P_GSUB), scalar1=1.0 - LAM_INIT, scalar2=None, op0=ALU.mult),
         ["sc", "par"], ["sc"], stream=False)
    P.op("dve", lambda e: e.memset(sc[:, 8:9], EPS), ["sc"], ["sc"], stream=False)
    epsc = sc[:, 8:9]
    neglam = sc[:, 1:2]
    gsub = sc[:, 2:3]

    def rsqrt_op(dst, src, reads, dstname, scale):
        P.op("act", lambda e: e.activation(out=dst, in_=src, func=AF.Sqrt, bias=epsc, scale=scale), list(reads) + ["sc"], [dstname], stream=False)
        P.op("dve", lambda e: e.reciprocal(out=dst, in_=dst), [dstname], [dstname], stream=False)

    def wres(slot):
        return ["wsl%dq%d" % (slot, q) for q in range(4)]

    CAST_ENGS = ["dve", "act", "pool", "act", "dve"]

    def load_cast(dst, src, dstname, n_inner=None):
        st = state["stg"]
        state["stg"] = (st + 1) % NSTG
        sv = stg[st][:, :]
        if n_inner is not None:
            sv = stg[st][:, 0:dst.shape[1] * n_inner].rearrange("p (c n) -> p c n", n=n_inner)
        else:
            sv = stg[st][:, 0:dst.shape[1]]
        P.op("sp", lambda e: e.dma_start(out=sv, in_=src), writes=["stg%d" % st], dma="g%d" % st)
        eng = CAST_ENGS[state["ce"] % len(CAST_ENGS)]
        state["ce"] += 1
        copy_op(eng, dst, sv, ["stg%d" % st], [dstname])

    def load_w(src, k0, nk, c0, ncols=WS):
        s = state["ws"]
        state["ws"] = (s + 1) % NWSL
        for qi, q0 in enumerate(range(0, nk, 4)):
            q1 = min(nk, q0 + 4)
            load_cast(wsl[s][:, q0:q1, 0:ncols],
                      src[(k0 + q0) * 128:(k0 + q1) * 128, c0:c0 + ncols].rearrange("(c p) n -> p c n", p=128),
                      "wsl%dq%d" % (s, qi), n_inner=ncols)
        return s

    def norm_stats(srcT, srcname, nch, ntok, inv_n, tagps):
        b = nps()
        for c in range(nch):
            t = c % 2
            P.op("act", lambda e, c=c, t=t: e.activation(out=tmpf[t][:, 0:ntok], in_=srcT[:, c, 0:ntok], func=AF.Square),
                 [srcname], ["tmpf%d" % t], stream=True)
            P.op("pe", lambda e, c=c, t=t, b=b: e.matmul(ps[b][:, 0:ntok], lhsT=onesF, rhs=tmpf[t][:, 0:ntok],
                                                         start=(c == 0), stop=(c == nch - 1)),
                 ["tmpf%d" % t, "cst"], [PSN[b]])
        rsqrt_op(rstd[:, 0:ntok], ps[b][:, 0:ntok], [PSN[b]], "rstd", inv_n)

    def load_xT(xsrc, row0, nsub, ntok):
        for s in range(nsub):
            xi = 0
            P.op("sp", lambda e, xi=xi, s=s: e.dma_start(out=xin[xi][:], in_=xsrc[row0 + s * 128: row0 + (s + 1) * 128, :]),
                 writes=["xin%d" % xi], dma="x%d" % xi)
            for c4 in range(4):
                b = nps()
                for k in range(4):
                    c = c4 * 4 + k
                    P.op("pe", lambda e, xi=xi, c=c, k=k, b=b: e.transpose(out=ps[b][:, k * 128:(k + 1) * 128],
                                                                           in_=xin[xi][:, c * 128:(c + 1) * 128], identity=identf),
                         ["xin%d" % xi, "cst"], [PSN[b]])
                copy_op(evac_eng(), xT[:, c4 * 4:c4 * 4 + 4, s * 128:(s + 1) * 128],
                        ps[b][:, :].rearrange("p (c t) -> p c t", t=128), [PSN[b]], ["xT"])

    def make_hT(gbase, ntok):
        norm_stats(xT, "xT", NCH, ntok, 1.0 / D, None)
        for c in range(NCH):
            P.op("dve", lambda e, c=c: e.scalar_tensor_tensor(out=hT[:, c, 0:ntok], in0=xT[:, c, 0:ntok], scalar=col(gbase + c),
                                                              in1=rstd[:, 0:ntok], op0=ALU.mult, op1=ALU.mult),
                 ["xT", "rstd", "par"], ["hT"], stream=True)

    def tokmajor_mm(s, slot, ncols=WS):
        b = nps()
        for c in range(NCH):
            P.op("pe", lambda e, c=c, b=b: e.matmul(ps[b][:, 0:ncols], lhsT=hT[:, c, s * 128:(s + 1) * 128], rhs=wsl[slot][:, c, 0:ncols],
                                                    start=(c == 0), stop=(c == NCH - 1)),
                 (["hT"] + wres(slot)), [PSN[b]])
        return b

    def featmajor_mm(srcT, srcname, nk, slot, blk, ntok, wchunk0=0):
        b = nps()
        for c in range(nk):
            P.op("pe", lambda e, c=c, b=b: e.matmul(ps[b][:, 0:ntok], lhsT=wsl[slot][:, c, blk * 128:(blk + 1) * 128],
                                                    rhs=srcT[:, wchunk0 + c, 0:ntok], start=(c == 0), stop=(c == nk - 1)),
                 ([srcname] + wres(slot)), [PSN[b]])
        return b

    def rope_evac(b, dst, dstname, rsub, nblk):
        w = nblk * 64
        copy_op("act", dst, ps[b][:, 0:w], [PSN[b]], [dstname])
        if not C_ROPE:
            return
        dv = dst.rearrange("p (b d) -> p b d", d=64)
        pv = dv
        cosb = rope[:, rsub, 0:8].unsqueeze(1).to_broadcast([128, nblk, 8])
        sinb = rope[:, rsub, 8:16].unsqueeze(1).to_broadcast([128, nblk, 8])
        x1, x2 = pv[:, :, 0:8], pv[:, :, 8:16]
        r = [t[:, 0:nblk, :] for t in rtmp]
        rn = ["rtmp%d" % i for i in range(4)]
        P.op("dve", lambda e: e.tensor_tensor(out=r[0], in0=x1, in1=cosb, op=ALU.mult), [dstname, "rope"], [rn[0]], stream=False)
        P.op("dve", lambda e: e.tensor_tensor(out=r[1], in0=x2, in1=sinb, op=ALU.mult), [dstname, "rope"], [rn[1]], stream=False)
        P.op("dve", lambda e: e.tensor_tensor(out=r[2], in0=x2, in1=cosb, op=ALU.mult), [dstname, "rope"], [rn[2]], stream=False)
        P.op("dve", lambda e: e.tensor_tensor(out=r[3], in0=x1, in1=sinb, op=ALU.mult), [dstname, "rope"], [rn[3]], stream=False)
        P.op("dve", lambda e: e.tensor_tensor(out=dv[:, :, 0:8], in0=r[0], in1=r[1], op=ALU.subtract), [rn[0], rn[1], dstname], [dstname], stream=False)
        P.op("dve", lambda e: e.tensor_tensor(out=dv[:, :, 8:16], in0=r[2], in1=r[3], op=ALU.add), [rn[2], rn[3], dstname], [dstname], stream=False)

    def transpose8(srcbf, srcname, dst, dstname, nrows=128):
        b = nps()
        pb = ps[b][:, :].bitcast(BF16)
        for h in range(NH):
            P.op("pe", lambda e, h=h: e.transpose(out=pb[:, h * 128:h * 128 + nrows], in_=srcbf[0:nrows, h * 128:(h + 1) * 128],
                                                  identity=identb[0:nrows, 0:nrows]),
                 [srcname, "identb"], [PSN[b]])
        pb3 = pb.rearrange("p (h t) -> p h t", t=128)[:, :, 0:nrows]
        if isinstance(dst, tuple):
            eng = evac_eng()
            copy_op(eng, dst[0], pb3[0:64], [PSN[b]], [dstname])
            copy_op(eng, dst[1], pb3[64:128], [PSN[b]], [dstname])
        else:
            copy_op(evac_eng(), dst, pb3, [PSN[b]], [dstname])

    def qkv_phase(tslot, rsub0, nsub, kout, vout, orow0, do_q, flag_o=None, ntok_rows=128):
        groups = ([("q", 0), ("q", 1), ("q", 2), ("q", 3)] if do_q else []) + [("k", 0), ("k", 1), ("k", 2), ("k", 3)] + [("v", 0), ("v", 1), ("v", 2), ("v", 3)]
        groups = groups[:cfg.get('qg', 99)]
        for (kind, g) in groups:
            c0 = {"q": 0, "k": 1024, "v": 2048}[kind] + g * WS
            slot = load_w(w_in, 0, NCH, c0)
            for s in range(nsub):
                kst = kvst[s][:, 0:1024]
                vst = kvst[s][:, 1024:2048]
                kvn = "kvst%d" % s
                b = tokmajor_mm(s, slot)
                if kind == "q":
                    rope_evac(b, qtmps[s][:, g * WS:(g + 1) * WS], "qtmp%d" % s, rsub0 + s, WS // 64)
                elif kind == "k":
                    rope_evac(b, kst[:, g * WS:(g + 1) * WS], kvn, rsub0 + s, WS // 64)
                else:
                    copy_op("act", vst[:, g * WS:(g + 1) * WS], ps[b][:, 0:WS], [PSN[b]], [kvn])
                    copy_op("dve", vbfs[s][:, g * WS:(g + 1) * WS], vst[:, g * WS:(g + 1) * WS], [kvn], ["vbf%d" % s])
        for s in range(nsub):
            kv = s
            kst = kvst[s][:, 0:1024]
            vst = kvst[s][:, 1024:2048]
            kvn = "kvst%d" % s
            qtmp = qtmps[s]
            vbf = vbfs[s]
            vbn = "vbf%d" % s
            if do_q:
                copy_op("dve", qbf[:], qtmp[:], ["qtmp%d" % s], ["qbf"])
                transpose8(qbf, "qbf", (QT[0:64, 0, :, s * 128:(s + 1) * 128], QT[64:128, 1, :, s * 128:(s + 1) * 128]), "QT")
            copy_op("act", kbf[:], kst, [kvn], ["kbf"])
            if kout is not None:
                P.op("act", lambda e, s=s, kst=kst: e.dma_start(out=kout[orow0 + s * 128: orow0 + (s + 1) * 128, :], in_=kst),
                     reads=[kvn], dma="ok%d" % s)
                P.op("act", lambda e, s=s, vst=vst: e.dma_start(out=vout[orow0 + s * 128: orow0 + (s + 1) * 128, :], in_=vst),
                     reads=[kvn], dma="ov%d" % s)
            if tslot is not None and C_TR:
                transpose8(kbf, "kbf", ktr[:, :, :], "ktr")
            if tslot is not None and C_SCR:
                for hh in (0, 4):
                    P.op("act", lambda e, s=s, hh=hh: e.dma_start(out=kT_scr[hh:hh + 4, :, tslot, s * 128:(s + 1) * 128].rearrange("h p t -> p h t"), in_=ktr[:, hh:hh + 4, :]),
                         reads=["ktr"], writes=["kTs%d" % tslot], dma="sk", join=True)
                    P.op("act", lambda e, s=s, hh=hh, vbf=vbf: e.dma_start(out=v_scr[hh:hh + 4, :, tslot, s, :].rearrange("h p e -> p h e"),
                                                                 in_=vbf[:, hh * 128:(hh + 4) * 128].rearrange("p (h e) -> p h e", e=128)),
                         reads=[vbn], writes=["vs%d" % tslot], dma="sv%d" % s, join=True)
                if flag_o is not None:
                    P.op("dve", lambda e, vbf=vbf: e.tensor_scalar(out=vmf[:], in0=vbf[:], scalar1=flg[:, flag_o:flag_o + 1], scalar2=None, op0=ALU.mult),
                         [vbn, "flg"], ["vmf"], stream=True)
                    for hh in (0, 4):
                        P.op("act", lambda e, s=s, hh=hh: e.dma_start(out=vm_scr[hh:hh + 4, :, flag_o, s, :].rearrange("h p e -> p h e"),
                                                                     in_=vmf[:, hh * 128:(hh + 4) * 128].rearrange("p (h e) -> p h e", e=128)),
                             reads=["vmf"], writes=["vms%d" % flag_o], dma="sm", join=True)

    def glu_phase(ntok, dstfn, dstname, inview=lambda a: a):
        for g in range(CC // WS):
            sa = load_w(w_in, 0, NCH, 3072 + g * WS)
            sbb = load_w(w_in, 0, NCH, 4096 + g * WS)
            GL = cfg.get('glu', 4)
            for blk in range(WS // 128):
                cb = g * (WS // 128) + blk
                if GL < 2:
                    continue
                ba = featmajor_mm(hT, "hT", NCH, sa, blk, ntok)
                bb = featmajor_mm(hT, "hT", NCH, sbb, blk, ntok)
                if GL < 3:
                    continue
                P.op("act", lambda e, bb=bb: e.activation(out=tmpf[2][:, 0:ntok], in_=ps[bb][:, 0:ntok], func=AF.Sigmoid),
                     [PSN[bb]], ["tmpf2"], stream=True)
                if GL < 4:
                    continue
                if cfg.get('gdst') == 'tmp':
                    P.op("dve", lambda e, ba=ba, cb=cb: e.tensor_tensor(out=tmpf[3][:, 0:ntok], in0=ps[ba][:, 0:ntok], in1=tmpf[2][:, 0:ntok], op=ALU.mult),
                         [PSN[ba], "tmpf2"], ["tmpf3"], stream=True)
                    continue
                if cfg.get('gdst') == 'sb':
                    copy_op("act", tmpf[3][:, 0:ntok], ps[ba][:, 0:ntok], [PSN[ba]], ["tmpf3"])
                    P.op("dve", lambda e, ba=ba, cb=cb: e.tensor_tensor(out=dstfn(cb), in0=inview(tmpf[3][:, 0:ntok]), in1=inview(tmpf[2][:, 0:ntok]), op=ALU.mult),
                         ["tmpf3", "tmpf2"], [dstname], stream=True)
                    continue
                P.op("dve", lambda e, ba=ba, cb=cb: e.tensor_tensor(out=dstfn(cb), in0=inview(ps[ba][:, 0:ntok]), in1=inview(tmpf[2][:, 0:ntok]), op=ALU.mult),
                     [PSN[ba], "tmpf2"], [dstname], stream=True)

    def conv_phase(extv, accv, ntok_shape):
        for cb in range(8):
            P.op("dve", lambda e, cb=cb: e.tensor_scalar(out=accv(cb), in0=extv(cb, 0), scalar1=col(P_WDW + cb), scalar2=col(P_BDW + cb),
                                                          op0=ALU.mult, op1=ALU.add), ["ext", "par"], ["acc%d" % cb], stream=True)
        for k in range(1, 31):
            for cb in range(8):
                P.op("dve", lambda e, cb=cb, k=k: e.scalar_tensor_tensor(out=accv(cb), in0=extv(cb, k), scalar=col(P_WDW + k * 8 + cb),
                                                                          in1=accv(cb), op0=ALU.mult, op1=ALU.add),
                     ["ext", "acc%d" % cb, "par"], ["acc%d" % cb], stream=True)

    def conv_norm(ntok):
        b1 = nps()
        b2 = nps()
        for cb in range(8):
            P.op("pe", lambda e, cb=cb: e.matmul(ps[b1][:, 0:ntok], lhsT=onesF, rhs=acc[:, cb, 0:ntok], start=(cb == 0), stop=(cb == 7)),
                 ["acc%d" % cb, "cst"], [PSN[b1]])
        for cb in range(8):
            t = cb % 2
            P.op("act", lambda e, cb=cb, t=t: e.activation(out=tmpf[t][:, 0:ntok], in_=acc[:, cb, 0:ntok], func=AF.Square),
                 ["acc%d" % cb], ["tmpf%d" % t], stream=True)
            P.op("pe", lambda e, cb=cb, t=t: e.matmul(ps[b2][:, 0:ntok], lhsT=onesF, rhs=tmpf[t][:, 0:ntok], start=(cb == 0), stop=(cb == 7)),
                 ["tmpf%d" % t, "cst"], [PSN[b2]])
        P.op("dve", lambda e: e.tensor_scalar(out=mean[:, 0:ntok], in0=ps[b1][:, 0:ntok], scalar1=1.0 / CC, scalar2=None, op0=ALU.mult),
             [PSN[b1]], ["mean"], stream=False)
        P.op("dve", lambda e: e.tensor_tensor(out=tmpf[3][:, 0:ntok], in0=mean[:, 0:ntok], in1=mean[:, 0:ntok], op=ALU.mult),
             ["mean"], ["tmpf3"], stream=False)
        P.op("dve", lambda e: e.scalar_tensor_tensor(out=rstd[:, 0:ntok], in0=ps[b2][:, 0:ntok], scalar=1.0 / CC, in1=tmpf[3][:, 0:ntok],
                                                     op0=ALU.mult, op1=ALU.subtract), [PSN[b2], "tmpf3"], ["rstd"], stream=False)
        rsqrt_op(rstd[:, 0:ntok], rstd[:, 0:ntok], ["rstd"], "rstd", 1.0)
        for cb in range(8):
            if cfg.get('convl', 3) < 3:
                break
            t = 2 + cb % 2
            P.op("dve", lambda e, cb=cb, t=t: e.tensor_tensor(out=tmpf[t][:, 0:ntok], in0=acc[:, cb, 0:ntok], in1=mean[:, 0:ntok], op=ALU.subtract),
                 ["acc%d" % cb, "mean"], ["tmpf%d" % t], stream=False)
            P.op("dve", lambda e, cb=cb, t=t: e.tensor_tensor(out=tmpf[t][:, 0:ntok], in0=tmpf[t][:, 0:ntok], in1=rstd[:, 0:ntok], op=ALU.mult),
                 ["tmpf%d" % t, "rstd"], ["tmpf%d" % t], stream=False)
            if cfg.get('silu', 'split') == 'fused':
                P.op("act", lambda e, cb=cb, t=t: e.activation(out=cnT[:, cb, 0:ntok], in_=tmpf[t][:, 0:ntok], func=AF.Silu,
                                                               bias=col(P_BCN + cb), scale=col(P_GCN + cb)),
                     ["tmpf%d" % t, "par"], ["cnT"], stream=False)
            else:
                P.op("act", lambda e, cb=cb, t=t: e.activation(out=tmpf[t][:, 0:ntok], in_=tmpf[t][:, 0:ntok], func=AF.Identity,
                                                               bias=col(P_BCN + cb), scale=col(P_GCN + cb)),
                     ["tmpf%d" % t, "par"], ["tmpf%d" % t], stream=False)
                P.op("act", lambda e, cb=cb, t=t: e.activation(out=tmpf[t - 2][:, 0:ntok], in_=tmpf[t][:, 0:ntok], func=AF.Sigmoid),
                     ["tmpf%d" % t], ["tmpf%d" % (t - 2)], stream=False)
                P.op("dve", lambda e, cb=cb, t=t: e.tensor_tensor(out=cnT[:, cb, 0:ntok], in0=tmpf[t][:, 0:ntok], in1=tmpf[t - 2][:, 0:ntok], op=ALU.mult),
                     ["tmpf%d" % t, "tmpf%d" % (t - 2)], ["cnT"], stream=False)

    def head_finish(o_dst, w):
        P.op("dve", lambda e: e.reciprocal(out=tmpf[0][:, 0:2 * w], in_=ps[1][:, 0:2 * w]), [PSN[1]], ["tmpf0"], stream=False)
        P.op("dve", lambda e: e.tensor_tensor(out=tmpf[1][:, 0:2 * w], in0=ps[0][:, 0:2 * w], in1=tmpf[0][:, 0:2 * w], op=ALU.mult),
             [PSN[0], "tmpf0"], ["tmpf1"], stream=False)
        P.op("dve", lambda e: e.scalar_tensor_tensor(out=tmpf[2][:, 0:w], in0=tmpf[1][:, w:2 * w], scalar=neglam, in1=tmpf[1][:, 0:w],
                                                     op0=ALU.mult, op1=ALU.add), ["tmpf1", "sc"], ["tmpf2"], stream=False)
        P.op("act", lambda e: e.activation(out=tmpf[3][:, 0:w], in_=tmpf[2][:, 0:w], func=AF.Square), ["tmpf2"], ["tmpf3"], stream=False)
        b = nps()
        P.op("pe", lambda e: e.matmul(ps[b][:, 0:w], lhsT=onesF, rhs=tmpf[3][:, 0:w], start=True, stop=True), ["tmpf3", "cst"], [PSN[b]])
        rsqrt_op(tmpf[0][:, 0:w], ps[b][:, 0:w], [PSN[b]], "tmpf0", 1.0 / 128)
        P.op("dve", lambda e: e.scalar_tensor_tensor(out=o_dst, in0=tmpf[2][:, 0:w], scalar=gsub, in1=tmpf[0][:, 0:w], op0=ALU.mult, op1=ALU.mult),
             ["tmpf2", "tmpf0", "sc"], ["oT"], stream=False)

    def prompt_attention(i):
        ktiles = [("o", o) for o in range(3 * i + 3)] + [("w", s) for s in range(i + 1)]
        ATL = cfg.get('attl', 4)
        nkt = len(ktiles)
        for h in range(NH):
            first = True
            sbi = 0
            for c0 in range(0, nkt, KC):
                chunk = ktiles[c0:c0 + KC]
                cb_ = (c0 // KC) % 2
                kres, vres = "KTc%d" % cb_, "Vc%d" % cb_
                runs = []
                for idx, (kind, n) in enumerate(chunk):
                    ksl = n if kind == "o" else NOTH + n
                    vsrc = ("vm", n) if (kind == "o" and n >= 3 * i) else ("v", ksl)
                    runs.append((idx, ksl, vsrc))
                j = 0
                while j < len(runs):
                    j2 = j
                    while j2 + 1 < len(runs) and runs[j2 + 1][1] == runs[j2][1] + 1:
                        j2 += 1
                    a, bnd = runs[j][1], runs[j2][1] + 1
                    P.op("sp", lambda e, j=j, a=a, bnd=bnd, h=h, cb_=cb_: e.dma_start(out=KTc[cb_][:, j:j + bnd - a, :], in_=kT_scr[h, :, a:bnd, :]),
                         reads=["kTs%d" % t for t in range(a, bnd)], writes=[kres], dma="lk%d" % cb_)
                    j = j2 + 1
                j = 0
                while j < len(runs):
                    j2 = j
                    while j2 + 1 < len(runs) and runs[j2 + 1][2][0] == runs[j2][2][0] and runs[j2 + 1][2][1] == runs[j2][2][1] + 1:
                        j2 += 1
                    kindv, a = runs[j][2]
                    bnd = runs[j2][2][1] + 1
                    srcv = vm_scr if kindv == "vm" else v_scr
                    rn = [("vms%d" if kindv == "vm" else "vs%d") % t for t in range(a, bnd)]
                    P.op("sp", lambda e, j=j, a=a, bnd=bnd, h=h, cb_=cb_, srcv=srcv: e.dma_start(
                        out=Vc[cb_][:, j * NSB:(j + bnd - a) * NSB, :], in_=srcv[h, :, a:bnd, :, :].rearrange("p t s e -> p (t s) e")),
                         reads=rn, writes=[vres], dma="lv%d" % cb_)
                    j = j2 + 1
                for idx, (kind, n) in enumerate(chunk):
                    diag = (kind == "w" and n == i)
                    lones = onesf[:, n, :] if (kind == "o" and n >= 3 * i) else onesb[:, :]
                    lon = "onesf" if (kind == "o" and n >= 3 * i) else "onesb"
                    for sbk in range(NSB):
                        last = (c0 + idx == nkt - 1) and (sbk == NSB - 1)
                        q0 = sbk * 128 if diag else 0
                        wq = TT - q0
                        sbank = 2 + (sbi % 2)
                        pslot = sbi % 2
                        sbi += 1
                        sv = ps[sbank][:, :].rearrange("p (m q) -> p m q", m=2)
                        for m in range((cfg.get('nm', 2)) if ATL >= 2 else 0):
                            P.op("pe", lambda e, m=m, idx=idx, sbk=sbk, q0=q0, cb_=cb_, sbank=sbank, h=h: e.matmul(
                                ps[sbank][:, m * TT + q0:(m + 1) * TT],
                                lhsT=KTc[cb_][:, idx, sbk * 128:(sbk + 1) * 128],
                                rhs=QT[:, m, h, q0:TT], start=True, stop=True),
                                 [kres, "QT"], [PSN[sbank]])
                        if ATL < 2:
                            continue
                        pv = Pt[pslot][:, 0:2 * TT].rearrange("p (m q) -> p m q", m=2)
                        P.op("act", lambda e, q0=q0, pv=pv, sv=sv: e.activation(out=pv[:, :, q0:TT], in_=sv[:, :, q0:TT], func=AF.Exp, scale=0.125),
                             [PSN[sbank]], ["Pt%d" % pslot], stream=True)
                        if diag:
                            P.op("dve", lambda e, q0=q0, pv=pv: e.memset(pv[64:128, :, q0:q0 + 64], 0.0), ["Pt%d" % pslot], ["Pt%d" % pslot], stream=False)
                        if ATL < 3:
                            continue
                        if q0 == 0:
                            P.op("pe", lambda e, idx=idx, sbk=sbk, cb_=cb_, pslot=pslot, first=first, last=last: e.matmul(
                                ps[0][:, 0:2 * TT], lhsT=Vc[cb_][:, idx * NSB + sbk, :], rhs=Pt[pslot][:, 0:2 * TT], start=first, stop=last),
                                 [vres, "Pt%d" % pslot], [PSN[0]])
                            P.op("pe", lambda e, pslot=pslot, first=first, last=last, lones=lones: e.matmul(
                                ps[1][:, 0:2 * TT], lhsT=lones, rhs=Pt[pslot][:, 0:2 * TT], start=first, stop=last),
                                 [lon, "Pt%d" % pslot], [PSN[1]])
                        else:
                            for m in range(2):
                                lm = last and m == 1
                                P.op("pe", lambda e, m=m, idx=idx, sbk=sbk, cb_=cb_, pslot=pslot, q0=q0, lm=lm: e.matmul(
                                    ps[0][:, m * TT + q0:(m + 1) * TT], lhsT=Vc[cb_][:, idx * NSB + sbk, :],
                                    rhs=Pt[pslot][:, m * TT + q0:(m + 1) * TT], start=False, stop=lm),
                                     [vres, "Pt%d" % pslot], [PSN[0]])
                                P.op("pe", lambda e, m=m, pslot=pslot, q0=q0, lm=lm, lones=lones: e.matmul(
                                    ps[1][:, m * TT + q0:(m + 1) * TT], lhsT=lones, rhs=Pt[pslot][:, m * TT + q0:(m + 1) * TT], start=False, stop=lm),
                                     [lon, "Pt%d" % pslot], [PSN[1]])
                        first = False
            if ATL >= 4:
                head_finish(oT[:, h, 0:TT], TT)

    def sample_attention():
        for bt in range(4):
            first = True
            for kb in range(33):
                nk = 128 if kb < 32 else 32
                cv_ = kb % 2
                if kb < 32:
                    load_cast(ckb, cache_k[bt, kb * 128:(kb + 1) * 128, :], "KTc0")
                    load_cast(cvb[cv_], cache_v[bt, kb * 128:(kb + 1) * 128, :], "Vc%d" % cv_)
                    transpose8(ckb, "KTc0", ckT, "KTc1")
                    kTsrc, kTname, kc0 = ckT, "KTc1", 0
                    vsrc, vname = cvb[cv_], "Vc%d" % cv_
                else:
                    kTsrc, kTname, kc0 = ktr, "ktr", bt * 32
                    P.op("sp", lambda e, bt=bt, cv_=cv_: e.dma_start(out=cvb[cv_][0:32, :], in_=vbf[bt * 32:(bt + 1) * 32, :]),
                         reads=["vbf0"], writes=["Vc%d" % cv_], dma="mv%d" % cv_)
                    vsrc, vname = cvb[cv_], "Vc%d" % cv_
                sbank = 2 + (kb % 2)
                pslot = kb % 2
                for h in range(NH):
                    for m in range(2):
                        cidx = (h * 2 + m) * 32
                        P.op("pe", lambda e, h=h, m=m, cidx=cidx, nk=nk, sbank=sbank, bt=bt, kTsrc=kTsrc, kc0=kc0: e.matmul(
                            ps[sbank][0:nk, cidx:cidx + 32], lhsT=kTsrc[:, h, kc0:kc0 + nk],
                            rhs=QT[:, m, h, bt * 32:(bt + 1) * 32], start=True, stop=True),
                             [kTname, "QT"], [PSN[sbank]])
                P.op("act", lambda e, nk=nk, sbank=sbank, pslot=pslot: e.activation(out=Pt[pslot][0:nk, :], in_=ps[sbank][0:nk, :], func=AF.Exp, scale=0.125),
                     [PSN[sbank]], ["Pt%d" % pslot], stream=True)
                last = kb == 32
                if first:
                    P.op("pe", lambda e: e.matmul(ps[0][:, :], lhsT=onesb[:, :], rhs=zrhs[:, :], start=True, stop=False), ["onesb", "zrhs"], [PSN[0]])
                for h in range(NH):
                    cidx = h * 64
                    P.op("pe", lambda e, h=h, cidx=cidx, nk=nk, pslot=pslot, first=first, last=last, vsrc=vsrc: e.matmul(
                        ps[0][:, cidx:cidx + 64], lhsT=vsrc[0:nk, h * 128:(h + 1) * 128], rhs=Pt[pslot][0:nk, cidx:cidx + 64],
                        start=False, stop=(last and h == NH - 1)), [vname, "Pt%d" % pslot], [PSN[0]])
                P.op("pe", lambda e, nk=nk, pslot=pslot, first=first, last=last: e.matmul(
                    ps[1][:, :], lhsT=onesb[0:nk, :], rhs=Pt[pslot][0:nk, :], start=first, stop=last), ["onesb", "Pt%d" % pslot], [PSN[1]])
                first = False
            P.op("dve", lambda e: e.reciprocal(out=tmpf[0][:, :], in_=ps[1][:, :]), [PSN[1]], ["tmpf0"], stream=False)
            P.op("dve", lambda e: e.tensor_tensor(out=tmpf[1][:, :], in0=ps[0][:, :], in1=tmpf[0][:, :], op=ALU.mult), [PSN[0], "tmpf0"], ["tmpf1"], stream=False)
            t1 = tmpf[1][:, :].rearrange("p (h m q) -> p h m q", m=2, q=32)
            o2 = tmpf[2][:, 0:256].rearrange("p (h q) -> p h q", q=32)
            P.op("dve", lambda e, t1=t1, o2=o2: e.scalar_tensor_tensor(out=o2, in0=t1[:, :, 1, :], scalar=neglam, in1=t1[:, :, 0, :], op0=ALU.mult, op1=ALU.add),
                 ["tmpf1", "sc"], ["tmpf2"], stream=False)
            P.op("act", lambda e: e.activation(out=tmpf[3][:, 0:256], in_=tmpf[2][:, 0:256], func=AF.Square), ["tmpf2"], ["tmpf3"], stream=False)
            b = nps()
            P.op("pe", lambda e, b=b: e.matmul(ps[b][:, 0:256], lhsT=onesF, rhs=tmpf[3][:, 0:256], start=True, stop=True), ["tmpf3", "cst"], [PSN[b]])
            rsqrt_op(tmpf[0][:, 0:256], ps[b][:, 0:256], [PSN[b]], "tmpf0", 1.0 / 128)
            r0 = tmpf[0][:, 0:256].rearrange("p (h q) -> p h q", q=32)
            P.op("dve", lambda e, bt=bt, o2=o2, r0=r0: e.scalar_tensor_tensor(out=oT[:, :, bt * 32:(bt + 1) * 32], in0=o2, scalar=gsub, in1=r0, op0=ALU.mult, op1=ALU.mult),
                 ["tmpf2", "tmpf0", "sc"], ["oT"], stream=False)

    def merge_phase(ntok):
        for jg in range(D // WS):
            s_ao = load_w(w_ao, 0, 8, jg * WS)
            for blk in range(2):
                b = featmajor_mm(oT, "oT", 8, s_ao, blk, ntok)
                copy_op("act", tA[:, blk, 0:ntok], ps[b][:, 0:ntok], [PSN[b]], ["tA"])
            s_ga = load_w(w_in, 0, NCH, 5120 + jg * WS)
            for blk in range(2):
                b = featmajor_mm(hT, "hT", NCH, s_ga, blk, ntok)
                P.op("act", lambda e, b=b: e.activation(out=tmpf[2][:, 0:ntok], in_=ps[b][:, 0:ntok], func=AF.Sigmoid), [PSN[b]], ["tmpf2"], stream=True)
                P.op("dve", lambda e, blk=blk: e.tensor_tensor(out=tA[:, blk, 0:ntok], in0=tA[:, blk, 0:ntok], in1=tmpf[2][:, 0:ntok], op=ALU.mult),
                     ["tA", "tmpf2"], ["tA"], stream=True)
            s_co = load_w(w_co, 0, 8, jg * WS)
            for blk in range(2):
                j = jg * 2 + blk
                b = featmajor_mm(cnT, "cnT", 8, s_co, blk, ntok)
                P.op("dve", lambda e, b=b, blk=blk, j=j: e.tensor_scalar(out=tB[:, blk, 0:ntok], in0=ps[b][:, 0:ntok], scalar1=col(P_BCO + j), scalar2=None, op0=ALU.add),
                     [PSN[b], "par"], ["tB"], stream=True)
            s_gb = load_w(w_in, 0, NCH, 7168 + jg * WS)
            for blk in range(2):
                j = jg * 2 + blk
                b = featmajor_mm(hT, "hT", NCH, s_gb, blk, ntok)
                P.op("act", lambda e, b=b: e.activation(out=tmpf[3][:, 0:ntok], in_=ps[b][:, 0:ntok], func=AF.Sigmoid), [PSN[b]], ["tmpf3"], stream=True)
                P.op("dve", lambda e, blk=blk: e.tensor_tensor(out=tB[:, blk, 0:ntok], in0=tB[:, blk, 0:ntok], in1=tmpf[3][:, 0:ntok], op=ALU.mult),
                     ["tB", "tmpf3"], ["tB"], stream=True)
                P.op("dve", lambda e, blk=blk, j=j: e.tensor_tensor(out=mrgT[:, j, 0:ntok], in0=tA[:, blk, 0:ntok], in1=tB[:, blk, 0:ntok], op=ALU.add),
                     ["tA", "tB"], ["mrgT", "KTc0", "KTc1"], stream=True)

    def proj_norm_residual(srcT, srcname, nk_total, wsrc, gbase, ntok):
        for jg in range(D // WS):
            kgs = [(k0, min(NCH, nk_total - k0)) for k0 in range(0, nk_total, NCH)]
            bs = [nps(), nps()]
            for gi, (k0, nk) in enumerate(kgs):
                slot = load_w(wsrc, k0, nk, jg * WS)
                for blk in range(2):
                    b = bs[blk]
                    for c in range(nk):
                        P.op("pe", lambda e, c=c, b=b, blk=blk, slot=slot, k0=k0, gi=gi, nk=nk: e.matmul(
                            ps[b][:, 0:ntok], lhsT=wsl[slot][:, c, blk * 128:(blk + 1) * 128], rhs=srcT[:, k0 + c, 0:ntok],
                            start=(gi == 0 and c == 0), stop=(gi == len(kgs) - 1 and c == nk - 1)),
                             (([srcname] + wres(slot)) + (["ext", "cnT"] + ACCN if srcname == "actT" else ["KTc0", "KTc1"] if srcname == "mrgT" else [])), [PSN[b]])
            for blk in range(2):
                copy_op(evac_eng(), bufA[:, jg * 2 + blk, 0:ntok], ps[bs[blk]][:, 0:ntok], [PSN[bs[blk]]], ["bufA"])
        norm_stats(bufA, "bufA", NCH, ntok, 1.0 / D, None)
        for c in range(NCH):
            P.op("dve", lambda e, c=c: e.scalar_tensor_tensor(out=bufA[:, c, 0:ntok], in0=bufA[:, c, 0:ntok], scalar=col(gbase + c), in1=rstd[:, 0:ntok],
                                                              op0=ALU.mult, op1=ALU.mult), ["bufA", "rstd", "par"], ["bufA"], stream=True)
            P.op("dve", lambda e, c=c: e.tensor_tensor(out=xT[:, c, 0:ntok], in0=xT[:, c, 0:ntok], in1=bufA[:, c, 0:ntok], op=ALU.add),
                 ["bufA", "xT"], ["xT"], stream=True)

    def ffn_act(ntok):
        for fg in range(DFF // WS):
            sg = load_w(w_fg, 0, NCH, fg * WS)
            su = load_w(w_fu, 0, NCH, fg * WS)
            for blk in range(2):
                bg = featmajor_mm(hT, "hT", NCH, sg, blk, ntok)
                bu = featmajor_mm(hT, "hT", NCH, su, blk, ntok)
                P.op("act", lambda e, bg=bg: e.activation(out=tmpf[2][:, 0:ntok], in_=ps[bg][:, 0:ntok], func=AF.Sigmoid), [PSN[bg]], ["tmpf2"], stream=True)
                P.op("dve", lambda e, bg=bg: e.tensor_tensor(out=tmpf[2][:, 0:ntok], in0=ps[bg][:, 0:ntok], in1=tmpf[2][:, 0:ntok], op=ALU.mult),
                     [PSN[bg], "tmpf2"], ["tmpf2"], stream=True)
                P.op("dve", lambda e, bu=bu, fg=fg, blk=blk: e.tensor_tensor(out=actT[:, fg * 2 + blk, 0:ntok], in0=ps[bu][:, 0:ntok], in1=tmpf[2][:, 0:ntok], op=ALU.mult),
                     [PSN[bu], "tmpf2"], ["actT", "ext", "cnT"] + ACCN, stream=True)

    def store_y(ydst, row0, nsub):
        for s in range(nsub):
            xi = 0
            for c4 in range(4):
                b = nps()
                for k in range(4):
                    c = c4 * 4 + k
                    P.op("pe", lambda e, c=c, k=k, b=b, s=s: e.transpose(out=ps[b][:, k * 128:(k + 1) * 128], in_=xT[:, c, s * 128:(s + 1) * 128], identity=identf),
                         ["xT", "cst"], [PSN[b]])
                copy_op(evac_eng(), xin[xi][:, c4 * 512:(c4 + 1) * 512], ps[b][:, :], [PSN[b]], ["xin%d" % xi])
            P.op("act", lambda e, xi=xi, s=s: e.dma_start(out=ydst[row0 + s * 128: row0 + (s + 1) * 128, :], in_=xin[xi][:]),
                 reads=["xin%d" % xi], dma="y%d" % xi)

    def layer_tail(ntok):
        merge_phase(ntok)
        proj_norm_residual(mrgT, "mrgT", NCH, w_o, P_GPOST, ntok)
        norm_stats(xT, "xT", NCH, ntok, 1.0 / D, None)
        for c in range(NCH):
            P.op("dve", lambda e, c=c: e.scalar_tensor_tensor(out=hT[:, c, 0:ntok], in0=xT[:, c, 0:ntok], scalar=col(P_GFFN + c), in1=rstd[:, 0:ntok],
                                                              op0=ALU.mult, op1=ALU.mult), ["xT", "rstd", "par"], ["hT"], stream=True)
        ffn_act(ntok)
        proj_norm_residual(actT, "actT", NFC, w_fd, P_GPFF, ntok)

    NHS = (NOWN * 32) // 128
    halov = halo[:, :, :, :].rearrange("p c i t -> p c (i t)")
    C_ST = cfg.get('stage', 3)
    if C_HALO:
        if C_ST >= 1:
            load_xT(x_halo, 0, NHS, NHS * 128)
        if C_ST >= 2:
            make_hT(P_GPRE, NHS * 128)
        if C_ST >= 3:
            glu_phase(NHS * 128, lambda cb: halov[:, cb, :], "halo")

    for o in range(C_NOTH):
        load_xT(x_oth, o * TT, NSB, TT)
        make_hT(P_GPRE, TT)
        qkv_phase(o, o * NSB, NSB, None, None, 0, False, flag_o=o)

    for i in range(C_NOWN):
        load_xT(x_own, i * TT, NSB, TT)
        make_hT(P_GPRE, TT)
        qkv_phase(NOTH + i, (NOTH + i) * NSB, NSB, k_own, v_own, i * TT, True)
        for cb in range(8):
            copy_op("act", ext[:, cb, 0:32], halo[:, cb, i, :], ["halo"], ["ext"], stream=False)
        glu_phase(TT, lambda cb: ext[:, cb, 32:32 + TT], "ext")
        if i == NOWN - 1:
            b = nps()
            for cb in range(8):
                P.op("pe", lambda e, cb=cb, b=b: e.transpose(out=ps[b][0:32, (cb % 4) * 128:(cb % 4 + 1) * 128],
                                                             in_=ext[:, cb, TT:TT + 32], identity=identf), ["ext", "cst"], [PSN[b]])
                if cb % 4 == 3:
                    copy_op("dve", stc[0:32, (cb - 3) * 128:(cb + 1) * 128], ps[b][0:32, :], [PSN[b]], ["kvst0"], stream=False)
                    if cb == 3:
                        b = nps()
            P.op("act", lambda e: e.dma_start(out=conv_p[:, :], in_=stc[2:32, :]), reads=["kvst0"], dma="oc")
        if cfg.get('conv', True):
            conv_phase(lambda cb, k: ext[:, cb, 2 + k:2 + k + TT], lambda cb: acc[:, cb, 0:TT], None)
            if cfg.get('convl', 3) >= 2:
                conv_norm(TT)
        if cfg.get('attn', True):
            prompt_attention(i)
        if C_TAIL:
            layer_tail(TT)
        store_y(y_own, i * TT, NSB)

    if C_SMP:
        load_xT(x_smp, 0, 1, 128)
        make_hT(P_GPRE, 128)
        qkv_phase(None, NTL * NSB, 1, k_s, v_s, 0, True)
        transpose8(kbf, "kbf", ktr[:, :, :], "ktr")
        extS = ext[:, :, 0:248].rearrange("p c (b t) -> p c b t", t=62)
        for bt in range(4):
            P.op("sp", lambda e, bt=bt: e.dma_start(out=stc[0:30, :], in_=state_conv[bt, :, :]), writes=["kvst0"], dma="sc")
            b = nps()
            for cb in range(8):
                P.op("pe", lambda e, cb=cb, b=b: e.transpose(out=ps[b][:, cb * 32:cb * 32 + 30], in_=stc[0:30, cb * 128:(cb + 1) * 128], identity=identf[0:30, 0:30]),
                     ["kvst0", "cst"], [PSN[b]])
            copy_op("dve", extS[:, :, bt, 0:30], ps[b][:, 0:256].rearrange("p (c t) -> p c t", t=32)[:, :, 0:30], [PSN[b]], ["ext"], stream=False)
        glu_phase(128, lambda cb: extS[:, cb, :, 30:62], "ext", inview=lambda a: a.rearrange("p (b t) -> p b t", t=32))
        for bt in range(4):
            b = nps()
            for cb in range(8):
                P.op("pe", lambda e, cb=cb, b=b, bt=bt: e.transpose(out=ps[b][0:32, (cb % 4) * 128:(cb % 4 + 1) * 128], in_=extS[:, cb, bt, 30:62], identity=identf),
                     ["ext", "cst"], [PSN[b]])
                if cb % 4 == 3:
                    copy_op("dve", stc[0:32, (cb - 3) * 128:(cb + 1) * 128], ps[b][0:32, :], [PSN[b]], ["kvst0"], stream=False)
                    if cb == 3:
                        b = nps()
            P.op("act", lambda e, bt=bt: e.dma_start(out=conv_s[bt, :, :], in_=stc[2:32, :]), reads=["kvst0"], dma="oc")
        conv_phase(lambda cb, k: extS[:, cb, :, k:k + 32], lambda cb: acc[:, cb, 0:128].rearrange("p (b t) -> p b t", t=32), None)
        conv_norm(128)
        sample_attention()
        layer_tail(128)
        store_y(y_s, 0, 1)

    print("ops:", {e: len(v) for e, v in P.ops.items()}, flush=True)
    with nc.Block() as block:
        semcms = P.emit(nc, block)
    return nc


_NC_CACHE = {}


def _rope_tab(pos):
    inv = (500000.0 ** (-np.arange(0, 16, 2, dtype=np.float32) / 16.0)).astype(np.float32)
    ang = pos.astype(np.float32)[:, None] * inv[None, :]
    return np.concatenate([np.cos(ang), np.sin(ang)], axis=1).astype(np.float32)


def prepare(inputs):
    f = lambda k: np.ascontiguousarray(np.asarray(inputs[k], dtype=np.float32))
    xp, xs = f("x_prompt"), f("x_sample")
    ck, cv, scv = f("cache_k")[0], f("cache_v")[0], f("state_conv")[0]
    par = np.zeros((128, NPAR), np.float32)
    colmaj = lambda v, n: np.ascontiguousarray(v.reshape(n, 128).T)
    par[:, P_GPRE:P_GPRE + 16] = colmaj(f("g_pre_mix")[0], 16)
    par[:, P_GPOST:P_GPOST + 16] = colmaj(f("g_post_mix")[0], 16)
    par[:, P_GFFN:P_GFFN + 16] = colmaj(f("g_pre_ffn")[0], 16)
    par[:, P_GPFF:P_GPFF + 16] = colmaj(f("g_post_ffn")[0], 16)
    par[:, P_BCO:P_BCO + 16] = colmaj(f("b_conv_out")[0], 16)
    par[:, P_GSUB] = f("g_subln")[0]
    wdw = f("w_dw")[0]
    par[:, P_WDW:P_WDW + 248] = wdw.reshape(31, 8, 128).transpose(2, 0, 1).reshape(128, 248)
    par[:, P_BDW:P_BDW + 8] = colmaj(f("b_dw")[0], 8)
    par[:, P_GCN:P_GCN + 8] = colmaj(f("g_conv_norm")[0], 8)
    par[:, P_BCN:P_BCN + 8] = colmaj(f("b_conv_norm")[0], 8)
    for n, key in enumerate(["lambda_q1", "lambda_k1", "lambda_q2", "lambda_k2"]):
        par[:, P_LAM + 64 * n:P_LAM + 64 * (n + 1)] = f(key)[0][None, :]
    consts = np.concatenate([np.eye(128, dtype=np.float32), np.ones((128, 128), np.float32)], axis=1)
    w = {"w_in": f("w_in")[0], "w_ao": f("w_attn_out")[0], "w_co": f("w_conv_out")[0], "w_o": f("w_o")[0],
         "w_fg": f("w_ffn_gate")[0], "w_fu": f("w_ffn_up")[0], "w_fd": f("w_ffn_down")[0]}
    in_maps = []
    own_tiles = {}
    for core in range(8):
        b, j = core // 4, core % 4
        own = [4 * i + j for i in range(NOWN)]
        oth = [4 * wd + k for wd in range(NOWN) for k in range(4) if k != j]
        own_tiles[core] = own
        xb = xp[b]
        x_oth = np.concatenate([xb[t * TT:(t + 1) * TT] for t in oth], axis=0)
        x_own = np.concatenate([xb[t * TT:(t + 1) * TT] for t in own], axis=0)
        x_halo = np.zeros((NOWN * 32, D), np.float32)
        for i, t in enumerate(own):
            if t > 0:
                x_halo[i * 32:(i + 1) * 32] = xb[t * TT - 32:t * TT]
        pos = np.concatenate([np.arange(t * TT, (t + 1) * TT) for t in oth + own] + [4096 + (np.arange(128) % 32)])
        rt = _rope_tab(pos).reshape(NTL * NSB + 1, 128, 16).transpose(1, 0, 2)
        flags = np.zeros((128, NOTH), np.float32)
        for o, t in enumerate(oth):
            flags[:, o] = 1.0 if (t % 4) < j else 0.0
        m = {"x_oth": x_oth, "x_own": x_own, "x_halo": x_halo,
             "x_smp": np.ascontiguousarray(xs[4 * core:4 * core + 4].reshape(128, D)),
             "rope": np.ascontiguousarray(rt), "flags": flags, "params": par, "consts": consts,
             "cache_k": np.ascontiguousarray(ck[4 * core:4 * core + 4].reshape(4, 4096, 1024)),
             "cache_v": np.ascontiguousarray(cv[4 * core:4 * core + 4].reshape(4, 4096, 1024)),
             "state_conv": np.ascontiguousarray(scv[4 * core:4 * core + 4])}
        m.update(w)
        in_maps.append(m)
    return in_maps, own_tiles


def kernel(**inputs):
    in_maps, own_tiles = prepare(inputs)
    if "nc" not in _NC_CACHE:
        _NC_CACHE["nc"] = build_nc()
    res = run_bass_kernel_spmd(_NC_CACHE["nc"], in_maps, core_ids=list(range(8)))
    R = res.results
    yp = np.zeros((2, SEQ, D), np.float32)
    kp = np.zeros((1, 2, SEQ, 1024), np.float32)
    vp = np.zeros((1, 2, SEQ, 1024), np.float32)
    cp = np.zeros((1, 2, 30, CC), np.float32)
    ys = np.zeros((32, 32, D), np.float32)
    ks = np.zeros((1, 32, 32, 1024), np.float32)
    vs = np.zeros((1, 32, 32, 1024), np.float32)
    cs = np.zeros((1, 32, 30, CC), np.float32)
    for core in range(8):
        b, j = core // 4, core % 4
        r = R[core]
        for i, t in enumerate(own_tiles[core]):
            yp[b, t * TT:(t + 1) * TT] = r["y_own"][i * TT:(i + 1) * TT]
            kp[0, b, t * TT:(t + 1) * TT] = r["k_own"][i * TT:(i + 1) * TT]
            vp[0, b, t * TT:(t + 1) * TT] = r["v_own"][i * TT:(i + 1) * TT]
        if j == 3:
            cp[0, b] = r["conv_p"]
        ys[4 * core:4 * core + 4] = r["y_s"].reshape(4, 32, D)
        ks[0, 4 * core:4 * core + 4] = r["k_s"].reshape(4, 32, 1024)
        vs[0, 4 * core:4 * core + 4] = r["v_s"].reshape(4, 32, 1024)
        cs[0, 4 * core:4 * core + 4] = r["conv_s"]
    return (yp, ys, kp.reshape(1, 2, SEQ, 8, 2, 64), vp.reshape(1, 2, SEQ, 8, 128), cp,
            ks.reshape(1, 32, 32, 8, 2, 64), vs.reshape(1, 32, 32, 8, 128), cs)
```

```python
import math
import numpy as np
import ml_dtypes
import concourse.bass as bass
import concourse.mybir as mybir
from concourse.bass_utils import run_bass_kernel_spmd

F32 = mybir.dt.float32
BF16 = mybir.dt.bfloat16
ALU = mybir.AluOpType
AF = mybir.ActivationFunctionType
AX = mybir.AxisListType

D = 2048
NCH = 16
SEQ = 8192
NH = 8
CC = 1024
DFF = 5632
NFC = 44
INC = 9216
EPS = 1e-6
TT = 256
NSB = TT // 128
NOWN = 2048 // TT
NOTH = 3 * NOWN
NTL = NOTH + NOWN
WS = 256
KC = 8
LAM_INIT = 0.8 - 0.6 * math.exp(0.0)
SKIP_STREAM_SYNC = False

P_GPRE, P_GPOST, P_GFFN, P_GPFF, P_BCO = 0, 16, 32, 48, 64
P_GSUB = 80
P_WDW = 81
P_BDW = P_WDW + 248
P_GCN = P_BDW + 8
P_BCN = P_GCN + 8
P_LAM = P_BCN + 8
NPAR = P_LAM + 256


class Prog:
    ENGS = ["pe", "act", "dve", "pool", "sp"]

    def __init__(self):
        self.ops = {e: [] for e in self.ENGS}
        self.res = {}
        self.dma_cnt = {}
        self.dma_ep = {}

    def op(self, eng, fn, reads=(), writes=(), dma=None, stream=False, join=False):
        o = {"eng": eng, "fn": fn, "deps": [], "needed": False, "dma": dma, "stream": stream}
        deps = []
        for r in reads:
            st = self.res.setdefault(r, {"w": None, "r": []})
            if st["w"] is not None:
                deps.append(st["w"])
        for w in writes:
            st = self.res.setdefault(w, {"w": None, "r": []})
            if st["w"] is not None and not (join and st["w"].get("dma_base") == dma):
                deps.append(st["w"])
            deps.extend(st["r"])
        seen = set()
        for a in deps:
            if id(a) in seen or a is o:
                continue
            seen.add(id(a))
            if a["dma"] is None and a["eng"] == eng:
                if eng in ("pe", "sp"):
                    continue
                if SKIP_STREAM_SYNC and a["stream"] and stream:
                    continue
            if a["dma"] is None:
                a["needed"] = True
            o["deps"].append(a)
        for r in reads:
            self.res[r]["r"].append(o)
        for w in writes:
            self.res[w] = {"w": o, "r": []}
        if dma is not None:
            ep = self.dma_ep.get(dma, 0)
            if not join and self.dma_cnt.get("%s_%d" % (dma, ep), 0) >= 1000:
                ep += 1
                self.dma_ep[dma] = ep
            dname = "%s_%d" % (dma, ep)
            o["dma"] = dname
            o["dma_base"] = dma
            self.dma_cnt[dname] = self.dma_cnt.get(dname, 0) + 1
            o["dmaval"] = 16 * self.dma_cnt[dname]
        self.ops[eng].append(o)
        return o

    def emit(self, nc, block):
        sems = {}
        allsems = []

        def getsem(name):
            if name not in sems:
                cm = nc.semaphore("s_" + name)
                s = cm.__enter__()
                allsems.append(cm)
                sems[name] = s
            return sems[name]

        for e in self.ENGS:
            n = 0
            ep = 0
            for o in self.ops[e]:
                if o["dma"] is None and o["needed"]:
                    if n >= 12000:
                        n = 0
                        ep += 1
                    n += 1
                    o["semval"] = n
                    o["semkey"] = "e_%s_%d" % (e, ep)
                    getsem(o["semkey"])
        for d in self.dma_cnt:
            getsem("d_" + d)

        def run(e, engobj):
            waited = {}
            for o in self.ops[e]:
                need = {}
                for a in o["deps"]:
                    if a["dma"] is not None:
                        key, val = "d_" + a["dma"], a["dmaval"]
                    else:
                        key, val = a["semkey"], a["semval"]
                    need[key] = max(need.get(key, 0), val)
                for key, val in need.items():
                    if waited.get(key, 0) >= val:
                        continue
                    waited[key] = val
                    engobj.wait_ge(sems[key], val)
                ins = o["fn"](engobj)
                if o["dma"] is not None:
                    ins.then_inc(sems["d_" + o["dma"]], 16)
                elif o["needed"]:
                    ins.then_inc(sems[o["semkey"]], 1)
            if e == "sp":
                for d, c in self.dma_cnt.items():
                    engobj.wait_ge(sems["d_" + d], 16 * c)

        block.tensor(lambda t: run("pe", t))
        block.scalar(lambda t: run("act", t))
        block.vector(lambda t: run("dve", t))
        block.gpsimd(lambda t: run("pool", t))
        block.sync(lambda t: run("sp", t))
        return allsems


def build_nc(cfg=None):
    cfg = cfg or {}
    C_HALO = cfg.get('halo', True)
    C_NOTH = cfg.get('noth', NOTH)
    C_NOWN = cfg.get('nown', NOWN)
    C_SMP = cfg.get('sample', True)
    C_TAIL = cfg.get('tail', True)
    C_SCR = cfg.get('scr', True)
    C_ROPE = cfg.get('rope', True)
    C_TR = cfg.get('tr', True)
    nc = bass.Bass("TRN2", target_bir_lowering=False)
    dt_in = lambda n, s, d=F32: nc.dram_tensor(n, list(s), d, kind="ExternalInput").ap()
    dt_out = lambda n, s: nc.dram_tensor(n, list(s), F32, kind="ExternalOutput").ap()
    x_oth = dt_in("x_oth", [NOTH * TT, D])
    x_own = dt_in("x_own", [NOWN * TT, D])
    x_halo = dt_in("x_halo", [NOWN * 32, D])
    x_smp = dt_in("x_smp", [128, D])
    rope_d = dt_in("rope", [128, NTL * NSB + 1, 16])
    flags_d = dt_in("flags", [128, NOTH])
    params_d = dt_in("params", [128, NPAR])
    consts_d = dt_in("consts", [128, 256])
    cache_k = dt_in("cache_k", [4, 4096, 1024])
    cache_v = dt_in("cache_v", [4, 4096, 1024])
    state_conv = dt_in("state_conv", [4, 30, CC])
    w_in = dt_in("w_in", [D, INC])
    w_ao = dt_in("w_ao", [CC, D])
    w_co = dt_in("w_co", [CC, D])
    w_o = dt_in("w_o", [D, D])
    w_fg = dt_in("w_fg", [D, DFF])
    w_fu = dt_in("w_fu", [D, DFF])
    w_fd = dt_in("w_fd", [DFF, D])
    y_own = dt_out("y_own", [NOWN * TT, D])
    k_own = dt_out("k_own", [NOWN * TT, 1024])
    v_own = dt_out("v_own", [NOWN * TT, 1024])
    conv_p = dt_out("conv_p", [30, CC])
    y_s = dt_out("y_s", [128, D])
    k_s = dt_out("k_s", [128, 1024])
    v_s = dt_out("v_s", [128, 1024])
    conv_s = dt_out("conv_s", [4, 30, CC])
    kT_scr = nc.dram_tensor("kT_scr", [NH, 128, NTL, TT], BF16, kind="Internal").ap()
    v_scr = nc.dram_tensor("v_scr", [NH, 128, NTL, NSB, 128], BF16, kind="Internal").ap()
    vm_scr = nc.dram_tensor("vm_scr", [NH, 128, NOTH, NSB, 128], BF16, kind="Internal").ap()

    P = Prog()
    cms = []

    def sb(name, shape, dt):
        cm = nc.sbuf_tensor(name, list(shape), dt)
        t = cm.__enter__()
        cms.append(cm)
        return t

    ps = []
    for i in range(8):
        cm = nc.psum_tensor("ps%d" % i, [128, 512], F32)
        ps.append(cm.__enter__())
        cms.append(cm)
    PSN = ["ps%d" % i for i in range(8)]

    TM = 128 + TT
    xT = sb("xT", [128, NCH, TT], F32)
    hT = sb("hT", [128, NCH, TT], BF16)
    bufA = sb("bufA", [128, NCH, TT], F32)
    xin = [sb("xin0", [128, D], F32)] * 2
    kvst = [sb("kvst%d" % i, [128, 2048], F32) for i in range(2)]
    NWSL = 2
    NSTG = 4
    wsl = [sb("wsl%d" % i, [128, NCH, WS], BF16) for i in range(NWSL)]
    stg = [sb("stg%d" % i, [128, 1024], F32) for i in range(NSTG)]
    QT = sb("QT", [128, 2, NH, TT], BF16)
    oT = sb("oT", [128, NH, TT], BF16)
    arena = sb("arena", [128, NFC * TT], BF16)
    actT = arena[:, :].rearrange("p (c t) -> p c t", t=TT)
    _e0 = 8 * (32 + TT) * 2
    ext = arena[:, 0:_e0].bitcast(F32).rearrange("p (c t) -> p c t", t=32 + TT)
    _a0 = _e0 + 8 * TT * 2
    acc = arena[:, _e0:_a0].bitcast(F32).rearrange("p (c t) -> p c t", t=TT)
    cnT = arena[:, _a0:_a0 + 8 * TT].rearrange("p (c t) -> p c t", t=TT)
    assert _a0 + 8 * TT <= NFC * TT
    kvbuf = sb("kvbuf", [128, 4 * KC * TT], BF16)
    KTc = [kvbuf[:, i * KC * TT:(i + 1) * KC * TT].rearrange("p (a b) -> p a b", b=TT) for i in range(2)]
    Vc = [kvbuf[:, (2 + i) * KC * TT:(3 + i) * KC * TT].rearrange("p (a b) -> p a b", b=128) for i in range(2)]
    mrgT = kvbuf[:, 0:NCH * TT].rearrange("p (a b) -> p a b", b=TT)
    assert NCH * TT <= 2 * KC * TT
    Pt = [sb("Pt%d" % i, [128, 512], BF16) for i in range(2)]
    tmpf = [sb("tmpf%d" % i, [128, 512], F32) for i in range(4)]
    qtmps = [sb("qtmp%d" % i, [128, 1024], F32) for i in range(2)]
    qbf = sb("qbf", [128, 1024], BF16)
    kbf = sb("kbf", [128, 1024], BF16)
    vbfs = [sb("vbf%d" % i, [128, 1024], BF16) for i in range(2)]
    vbf = vbfs[0]
    vmf = sb("vmf", [128, 1024], BF16)
    ktr = sb("ktr", [128, NH, 128], BF16)
    rtmp = [sb("rtmp%d" % i, [128, 16, 8], F32) for i in range(4)]
    tA = sb("tA", [128, 2, TT], F32)
    tB = sb("tB", [128, 2, TT], F32)
    rstd = sb("rstd", [128, TT], F32)
    mean = sb("mean", [128, TT], F32)
    par = sb("par", [128, NPAR], F32)
    rope = sb("rope_sb", [128, NTL * NSB + 1, 16], F32)
    flg = sb("flg", [128, NOTH], F32)
    cst = sb("cst_sb", [128, 256], F32)
    identb = sb("identb", [128, 128], BF16)
    onesb = sb("onesb", [128, 128], BF16)
    onesf = sb("onesf", [128, 3, 128], BF16)
    halo = sb("halo", [128, 8, NOWN, 32], F32)
    sc = sb("sc", [128, 16], F32)
    zrhs = sb("zrhs", [128, 512], BF16)
    ckb = KTc[0][:, :, :].rearrange("p a b -> p (a b)")[:, 0:1024]
    cvb = [Vc[i][:, :, :].rearrange("p a b -> p (a b)")[:, 0:1024] for i in range(2)]
    ckT = KTc[1][:, :, 0:128]
    stc = kvst[0][0:32, 0:CC]
    identf = cst[:, 0:128]
    onesF = cst[:, 128:256]

    state = {"ps": 0, "ws": 0, "xin": 0, "kv": 0, "ev": 0, "stg": 0, "ce": 0}

    def nps():
        i = state["ps"]
        state["ps"] = (i + 1) % 6
        return i + 2

    def evac_eng():
        state["ev"] ^= 1
        return "act" if state["ev"] else "dve"

    def copy_op(eng, out, in_, reads, writes, stream=True):
        if eng == "act":
            P.op("act", lambda e: e.copy(out=out, in_=in_), reads, writes, stream=stream)
        else:
            P.op(eng, lambda e: e.tensor_copy(out=out, in_=in_), reads, writes, stream=stream)

    ACCN = ["acc%d" % cb for cb in range(8)]

    def col(c):
        return par[:, c:c + 1]

    P.op("sp", lambda e: e.dma_start(out=par[:], in_=params_d), writes=["par"], dma="c0")
    P.op("sp", lambda e: e.dma_start(out=rope[:], in_=rope_d), writes=["rope"], dma="c1")
    P.op("sp", lambda e: e.dma_start(out=flg[:], in_=flags_d), writes=["flg"], dma="c2")
    P.op("sp", lambda e: e.dma_start(out=cst[:], in_=consts_d), writes=["cst"], dma="c3")
    copy_op("dve", identb[:], identf, ["cst"], ["identb"], stream=False)
    P.op("dve", lambda e: e.memset(zrhs[:, :], 0.0), [], ["zrhs"], stream=False)
    P.op("dve", lambda e: e.memset(QT[:, :, :, :].rearrange("p m h t -> p (m h t)"), 0.0), [], ["QT"], stream=False)
    copy_op("dve", onesb[:], onesF, ["cst"], ["onesb"], stream=False)
    lp = tmpf[0]
    P.op("dve", lambda e: e.tensor_tensor(out=lp[:, 0:64], in0=par[:, P_LAM:P_LAM + 64], in1=par[:, P_LAM + 64:P_LAM + 128], op=ALU.mult),
         ["par"], ["tmpf0"], stream=False)
    P.op("dve", lambda e: e.tensor_tensor(out=lp[:, 64:128], in0=par[:, P_LAM + 128:P_LAM + 192], in1=par[:, P_LAM + 192:P_LAM + 256], op=ALU.mult),
         ["par", "tmpf0"], ["tmpf0"], stream=False)
    P.op("dve", lambda e: e.tensor_reduce(out=sc[:, 3:4], in_=lp[:, 0:64], axis=AX.X, op=ALU.add), ["tmpf0"], ["sc"], stream=False)
    P.op("dve", lambda e: e.tensor_reduce(out=sc[:, 4:5], in_=lp[:, 64:128], axis=AX.X, op=ALU.add), ["tmpf0", "sc"], ["sc"], stream=False)
    P.op("act", lambda e: e.activation(out=sc[:, 5:7], in_=sc[:, 3:5], func=AF.Exp), ["sc"], ["sc"], stream=False)
    P.op("dve", lambda e: e.tensor_tensor(out=sc[:, 0:1], in0=sc[:, 5:6], in1=sc[:, 6:7], op=ALU.subtract), ["sc"], ["sc"], stream=False)
    P.op("dve", lambda e: e.tensor_scalar(out=sc[:, 1:2], in0=sc[:, 0:1], scalar1=LAM_INIT, scalar2=-1.0, op0=ALU.add, op1=ALU.mult),
         ["sc"], ["sc"], stream=False)
    P.op("dve", lambda e: e.tensor_scalar(out=sc[:, 2:3], in0=col(P_GSUB), scalar1=1.0 - LAM_INIT, scalar2=None, op0=ALU.mult),
         ["sc", "par"], ["sc"], stream=False)
    P.op("dve", lambda e: e.memset(sc[:, 8:9], EPS), ["sc"], ["sc"], stream=False)
    epsc = sc[:, 8:9]
    neglam = sc[:, 1:2]
    gsub = sc[:, 2:3]

    def rsqrt_op(dst, src, reads, dstname, scale):
        P.op("act", lambda e: e.activation(out=dst, in_=src, func=AF.Sqrt, bias=epsc, scale=scale), list(reads) + ["sc"], [dstname], stream=False)
        P.op("dve", lambda e: e.reciprocal(out=dst, in_=dst), [dstname], [dstname], stream=False)

    def wres(slot):
        return ["wsl%dq%d" % (slot, q) for q in range(4)]

    CAST_ENGS = ["dve", "act", "pool", "act", "dve"]

    def load_cast(dst, src, dstname, n_inner=None):
        st = state["stg"]
        state["stg"] = (st + 1) % NSTG
        sv = stg[st][:, :]
        if n_inner is not None:
            sv = stg[st][:, 0:dst.shape[1] * n_inner].rearrange("p (c n) -> p c n", n=n_inner)
        else:
            sv = stg[st][:, 0:dst.shape[1]]
        P.op("sp", lambda e: e.dma_start(out=sv, in_=src), writes=["stg%d" % st], dma="g%d" % st)
        eng = CAST_ENGS[state["ce"] % len(CAST_ENGS)]
        state["ce"] += 1
        copy_op(eng, dst, sv, ["stg%d" % st], [dstname])

    def load_w(src, k0, nk, c0, ncols=WS):
        s = state["ws"]
        state["ws"] = (s + 1) % NWSL
        for qi, q0 in enumerate(range(0, nk, 4)):
            q1 = min(nk, q0 + 4)
            load_cast(wsl[s][:, q0:q1, 0:ncols],
                      src[(k0 + q0) * 128:(k0 + q1) * 128, c0:c0 + ncols].rearrange("(c p) n -> p c n", p=128),
                      "wsl%dq%d" % (s, qi), n_inner=ncols)
        return s

    def norm_stats(srcT, srcname, nch, ntok, inv_n, tagps):
        b = nps()
        for c in range(nch):
            t = c % 2
            P.op("act", lambda e, c=c, t=t: e.activation(out=tmpf[t][:, 0:ntok], in_=srcT[:, c, 0:ntok], func=AF.Square),
                 [srcname], ["tmpf%d" % t], stream=True)
            P.op("pe", lambda e, c=c, t=t, b=b: e.matmul(ps[b][:, 0:ntok], lhsT=onesF, rhs=tmpf[t][:, 0:ntok],
                                                         start=(c == 0), stop=(c == nch - 1)),
                 ["tmpf%d" % t, "cst"], [PSN[b]])
        rsqrt_op(rstd[:, 0:ntok], ps[b][:, 0:ntok], [PSN[b]], "rstd", inv_n)

    def load_xT(xsrc, row0, nsub, ntok):
        for s in range(nsub):
            xi = 0
            P.op("sp", lambda e, xi=xi, s=s: e.dma_start(out=xin[xi][:], in_=xsrc[row0 + s * 128: row0 + (s + 1) * 128, :]),
                 writes=["xin%d" % xi], dma="x%d" % xi)
            for c4 in range(4):
                b = nps()
                for k in range(4):
                    c = c4 * 4 + k
                    P.op("pe", lambda e, xi=xi, c=c, k=k, b=b: e.transpose(out=ps[b][:, k * 128:(k + 1) * 128],
                                                                           in_=xin[xi][:, c * 128:(c + 1) * 128], identity=identf),
                         ["xin%d" % xi, "cst"], [PSN[b]])
                copy_op(evac_eng(), xT[:, c4 * 4:c4 * 4 + 4, s * 128:(s + 1) * 128],
                        ps[b][:, :].rearrange("p (c t) -> p c t", t=128), [PSN[b]], ["xT"])

    def make_hT(gbase, ntok):
        norm_stats(xT, "xT", NCH, ntok, 1.0 / D, None)
        for c in range(NCH):
            P.op("dve", lambda e, c=c: e.scalar_tensor_tensor(out=hT[:, c, 0:ntok], in0=xT[:, c, 0:ntok], scalar=col(gbase + c),
                                                              in1=rstd[:, 0:ntok], op0=ALU.mult, op1=ALU.mult),
                 ["xT", "rstd", "par"], ["hT"], stream=True)

    def tokmajor_mm(s, slot, ncols=WS):
        b = nps()
        for c in range(NCH):
            P.op("pe", lambda e, c=c, b=b: e.matmul(ps[b][:, 0:ncols], lhsT=hT[:, c, s * 128:(s + 1) * 128], rhs=wsl[slot][:, c, 0:ncols],
                                                    start=(c == 0), stop=(c == NCH - 1)),
                 (["hT"] + wres(slot)), [PSN[b]])
        return b

    def featmajor_mm(srcT, srcname, nk, slot, blk, ntok, wchunk0=0):
        b = nps()
        for c in range(nk):
            P.op("pe", lambda e, c=c, b=b: e.matmul(ps[b][:, 0:ntok], lhsT=wsl[slot][:, c, blk * 128:(blk + 1) * 128],
                                                    rhs=srcT[:, wchunk0 + c, 0:ntok], start=(c == 0), stop=(c == nk - 1)),
                 ([srcname] + wres(slot)), [PSN[b]])
        return b

    def rope_evac(b, dst, dstname, rsub, nblk):
        w = nblk * 64
        copy_op("act", dst, ps[b][:, 0:w], [PSN[b]], [dstname])
        if not C_ROPE:
            return
        dv = dst.rearrange("p (b d) -> p b d", d=64)
        pv = dv
        cosb = rope[:, rsub, 0:8].unsqueeze(1).to_broadcast([128, nblk, 8])
        sinb = rope[:, rsub, 8:16].unsqueeze(1).to_broadcast([128, nblk, 8])
        x1, x2 = pv[:, :, 0:8], pv[:, :, 8:16]
        r = [t[:, 0:nblk, :] for t in rtmp]
        rn = ["rtmp%d" % i for i in range(4)]
        P.op("dve", lambda e: e.tensor_tensor(out=r[0], in0=x1, in1=cosb, op=ALU.mult), [dstname, "rope"], [rn[0]], stream=False)
        P.op("dve", lambda e: e.tensor_tensor(out=r[1], in0=x2, in1=sinb, op=ALU.mult), [dstname, "rope"], [rn[1]], stream=False)
        P.op("dve", lambda e: e.tensor_tensor(out=r[2], in0=x2, in1=cosb, op=ALU.mult), [dstname, "rope"], [rn[2]], stream=False)
        P.op("dve", lambda e: e.tensor_tensor(out=r[3], in0=x1, in1=sinb, op=ALU.mult), [dstname, "rope"], [rn[3]], stream=False)
        P.op("dve", lambda e: e.tensor_tensor(out=dv[:, :, 0:8], in0=r[0], in1=r[1], op=ALU.subtract), [rn[0], rn[1], dstname], [dstname], stream=False)
        P.op("dve", lambda e: e.tensor_tensor(out=dv[:, :, 8:16], in0=r[2], in1=r[3], op=ALU.add), [rn[2], rn[3], dstname], [dstname], stream=False)

    def transpose8(srcbf, srcname, dst, dstname, nrows=128):
        b = nps()
        pb = ps[b][:, :].bitcast(BF16)
        for h in range(NH):
            P.op("pe", lambda e, h=h: e.transpose(out=pb[:, h * 128:h * 128 + nrows], in_=srcbf[0:nrows, h * 128:(h + 1) * 128],
                                                  identity=identb[0:nrows, 0:nrows]),
                 [srcname, "identb"], [PSN[b]])
        pb3 = pb.rearrange("p (h t) -> p h t", t=128)[:, :, 0:nrows]
        if isinstance(dst, tuple):
            eng = evac_eng()
            copy_op(eng, dst[0], pb3[0:64], [PSN[b]], [dstname])
            copy_op(eng, dst[1], pb3[64:128], [PSN[b]], [dstname])
        else:
            copy_op(evac_eng(), dst, pb3, [PSN[b]], [dstname])

    def qkv_phase(tslot, rsub0, nsub, kout, vout, orow0, do_q, flag_o=None, ntok_rows=128):
        groups = ([("q", 0), ("q", 1), ("q", 2), ("q", 3)] if do_q else []) + [("k", 0), ("k", 1), ("k", 2), ("k", 3)] + [("v", 0), ("v", 1), ("v", 2), ("v", 3)]
        groups = groups[:cfg.get('qg', 99)]
        for (kind, g) in groups:
            c0 = {"q": 0, "k": 1024, "v": 2048}[kind] + g * WS
            slot = load_w(w_in, 0, NCH, c0)
            for s in range(nsub):
                kst = kvst[s][:, 0:1024]
                vst = kvst[s][:, 1024:2048]
                kvn = "kvst%d" % s
                b = tokmajor_mm(s, slot)
                if kind == "q":
                    rope_evac(b, qtmps[s][:, g * WS:(g + 1) * WS], "qtmp%d" % s, rsub0 + s, WS // 64)
                elif kind == "k":
                    rope_evac(b, kst[:, g * WS:(g + 1) * WS], kvn, rsub0 + s, WS // 64)
                else:
                    copy_op("act", vst[:, g * WS:(g + 1) * WS], ps[b][:, 0:WS], [PSN[b]], [kvn])
                    copy_op("dve", vbfs[s][:, g * WS:(g + 1) * WS], vst[:, g * WS:(g + 1) * WS], [kvn], ["vbf%d" % s])
        for s in range(nsub):
            kv = s
            kst = kvst[s][:, 0:1024]
            vst = kvst[s][:, 1024:2048]
            kvn = "kvst%d" % s
            qtmp = qtmps[s]
            vbf = vbfs[s]
            vbn = "vbf%d" % s
            if do_q:
                copy_op("dve", qbf[:], qtmp[:], ["qtmp%d" % s], ["qbf"])
                transpose8(qbf, "qbf", (QT[0:64, 0, :, s * 128:(s + 1) * 128], QT[64:128, 1, :, s * 128:(s + 1) * 128]), "QT")
            copy_op("act", kbf[:], kst, [kvn], ["kbf"])
            if kout is not None:
                P.op("act", lambda e, s=s, kst=kst: e.dma_start(out=kout[orow0 + s * 128: orow0 + (s + 1) * 128, :], in_=kst),
                     reads=[kvn], dma="ok%d" % s)
                P.op("act", lambda e, s=s, vst=vst: e.dma_start(out=vout[orow0 + s * 128: orow0 + (s + 1) * 128, :], in_=vst),
                     reads=[kvn], dma="ov%d" % s)
            if tslot is not None and C_TR:
                transpose8(kbf, "kbf", ktr[:, :, :], "ktr")
            if tslot is not None and C_SCR:
                for hh in (0, 4):
                    P.op("act", lambda e, s=s, hh=hh: e.dma_start(out=kT_scr[hh:hh + 4, :, tslot, s * 128:(s + 1) * 128].rearrange("h p t -> p h t"), in_=ktr[:, hh:hh + 4, :]),
                         reads=["ktr"], writes=["kTs%d" % tslot], dma="sk", join=True)
                    P.op("act", lambda e, s=s, hh=hh, vbf=vbf: e.dma_start(out=v_scr[hh:hh + 4, :, tslot, s, :].rearrange("h p e -> p h e"),
                                                                 in_=vbf[:, hh * 128:(hh + 4) * 128].rearrange("p (h e) -> p h e", e=128)),
                         reads=[vbn], writes=["vs%d" % tslot], dma="sv%d" % s, join=True)
                if flag_o is not None:
                    P.op("dve", lambda e, vbf=vbf: e.tensor_scalar(out=vmf[:], in0=vbf[:], scalar1=flg[:, flag_o:flag_o + 1], scalar2=None, op0=ALU.mult),
                         [vbn, "flg"], ["vmf"], stream=True)
                    for hh in (0, 4):
                        P.op("act", lambda e, s=s, hh=hh: e.dma_start(out=vm_scr[hh:hh + 4, :, flag_o, s, :].rearrange("h p e -> p h e"),
                                                                     in_=vmf[:, hh * 128:(hh + 4) * 128].rearrange("p (h e) -> p h e", e=128)),
                             reads=["vmf"], writes=["vms%d" % flag_o], dma="sm", join=True)

    def glu_phase(ntok, dstfn, dstname, inview=lambda a: a):
        for g in range(CC // WS):
            sa = load_w(w_in, 0, NCH, 3072 + g * WS)
            sbb = load_w(w_in, 0, NCH, 4096 + g * WS)
            GL = cfg.get('glu', 4)
            for blk in range(WS // 128):
                cb = g * (WS // 128) + blk
                if GL < 2:
                    continue
                ba = featmajor_mm(hT, "hT", NCH, sa, blk, ntok)
                bb = featmajor_mm(hT, "hT", NCH, sbb, blk, ntok)
                if GL < 3:
                    continue
                P.op("act", lambda e, bb=bb: e.activation(out=tmpf[2][:, 0:ntok], in_=ps[bb][:, 0:ntok], func=AF.Sigmoid),
                     [PSN[bb]], ["tmpf2"], stream=True)
                if GL < 4:
                    continue
                if cfg.get('gdst') == 'tmp':
                    P.op("dve", lambda e, ba=ba, cb=cb: e.tensor_tensor(out=tmpf[3][:, 0:ntok], in0=ps[ba][:, 0:ntok], in1=tmpf[2][:, 0:ntok], op=ALU.mult),
                         [PSN[ba], "tmpf2"], ["tmpf3"], stream=True)
                    continue
                if cfg.get('gdst') == 'sb':
                    copy_op("act", tmpf[3][:, 0:ntok], ps[ba][:, 0:ntok], [PSN[ba]], ["tmpf3"])
                    P.op("dve", lambda e, ba=ba, cb=cb: e.tensor_tensor(out=dstfn(cb), in0=inview(tmpf[3][:, 0:ntok]), in1=inview(tmpf[2][:, 0:ntok]), op=ALU.mult),
                         ["tmpf3", "tmpf2"], [dstname], stream=True)
                    continue
                P.op("dve", lambda e, ba=ba, cb=cb: e.tensor_tensor(out=dstfn(cb), in0=inview(ps[ba][:, 0:ntok]), in1=inview(tmpf[2][:, 0:ntok]), op=ALU.mult),
                     [PSN[ba], "tmpf2"], [dstname], stream=True)

    def conv_phase(extv, accv, ntok_shape):
        for cb in range(8):
            P.op("dve", lambda e, cb=cb: e.tensor_scalar(out=accv(cb), in0=extv(cb, 0), scalar1=col(P_WDW + cb), scalar2=col(P_BDW + cb),
                                                          op0=ALU.mult, op1=ALU.add), ["ext", "par"], ["acc%d" % cb], stream=True)
        for k in range(1, 31):
            for cb in range(8):
                P.op("dve", lambda e, cb=cb, k=k: e.scalar_tensor_tensor(out=accv(cb), in0=extv(cb, k), scalar=col(P_WDW + k * 8 + cb),
                                                                          in1=accv(cb), op0=ALU.mult, op1=ALU.add),
                     ["ext", "acc%d" % cb, "par"], ["acc%d" % cb], stream=True)

    def conv_norm(ntok):
        b1 = nps()
        b2 = nps()
        for cb in range(8):
            P.op("pe", lambda e, cb=cb: e.matmul(ps[b1][:, 0:ntok], lhsT=onesF, rhs=acc[:, cb, 0:ntok], start=(cb == 0), stop=(cb == 7)),
                 ["acc%d" % cb, "cst"], [PSN[b1]])
        for cb in range(8):
            t = cb % 2
            P.op("act", lambda e, cb=cb, t=t: e.activation(out=tmpf[t][:, 0:ntok], in_=acc[:, cb, 0:ntok], func=AF.Square),
                 ["acc%d" % cb], ["tmpf%d" % t], stream=True)
            P.op("pe", lambda e, cb=cb, t=t: e.matmul(ps[b2][:, 0:ntok], lhsT=onesF, rhs=tmpf[t][:, 0:ntok], start=(cb == 0), stop=(cb == 7)),
                 ["tmpf%d" % t, "cst"], [PSN[b2]])
        P.op("dve", lambda e: e.tensor_scalar(out=mean[:, 0:ntok], in0=ps[b1][:, 0:ntok], scalar1=1.0 / CC, scalar2=None, op0=ALU.mult),
             [PSN[b1]], ["mean"], stream=False)
        P.op("dve", lambda e: e.tensor_tensor(out=tmpf[3][:, 0:ntok], in0=mean[:, 0:ntok], in1=mean[:, 0:ntok], op=ALU.mult),
             ["mean"], ["tmpf3"], stream=False)
        P.op("dve", lambda e: e.scalar_tensor_tensor(out=rstd[:, 0:ntok], in0=ps[b2][:, 0:ntok], scalar=1.0 / CC, in1=tmpf[3][:, 0:ntok],
                                                     op0=ALU.mult, op1=ALU.subtract), [PSN[b2], "tmpf3"], ["rstd"], stream=False)
        rsqrt_op(rstd[:, 0:ntok], rstd[:, 0:ntok], ["rstd"], "rstd", 1.0)
        for cb in range(8):
            if cfg.get('convl', 3) < 3:
                break
            t = 2 + cb % 2
            P.op("dve", lambda e, cb=cb, t=t: e.tensor_tensor(out=tmpf[t][:, 0:ntok], in0=acc[:, cb, 0:ntok], in1=mean[:, 0:ntok], op=ALU.subtract),
                 ["acc%d" % cb, "mean"], ["tmpf%d" % t], stream=False)
            P.op("dve", lambda e, cb=cb, t=t: e.tensor_tensor(out=tmpf[t][:, 0:ntok], in0=tmpf[t][:, 0:ntok], in1=rstd[:, 0:ntok], op=ALU.mult),
                 ["tmpf%d" % t, "rstd"], ["tmpf%d" % t], stream=False)
            if cfg.get('silu', 'split') == 'fused':
                P.op("act", lambda e, cb=cb, t=t: e.activation(out=cnT[:, cb, 0:ntok], in_=tmpf[t][:, 0:ntok], func=AF.Silu,
                                                               bias=col(P_BCN + cb), scale=col(P_GCN + cb)),
                     ["tmpf%d" % t, "par"], ["cnT"], stream=False)
            else:
                P.op("act", lambda e, cb=cb, t=t: e.activation(out=tmpf[t][:, 0:ntok], in_=tmpf[t][:, 0:ntok], func=AF.Identity,
                                                               bias=col(P_BCN + cb), scale=col(P_GCN + cb)),
                     ["tmpf%d" % t, "par"], ["tmpf%d" % t], stream=False)
                P.op("act", lambda e, cb=cb, t=t: e.activation(out=tmpf[t - 2][:, 0:ntok], in_=tmpf[t][:, 0:ntok], func=AF.Sigmoid),
                     ["tmpf%d" % t], ["tmpf%d" % (t - 2)], stream=False)
                P.op("dve", lambda e, cb=cb, t=t: e.tensor_tensor(out=cnT[:, cb, 0:ntok], in0=tmpf[t][:, 0:ntok], in1=tmpf[t - 2][:, 0:ntok], op=ALU.mult),
                     ["tmpf%d" % t, "tmpf%d" % (t - 2)], ["cnT"], stream=False)

    def head_finish(o_dst, w):
        P.op("dve", lambda e: e.reciprocal(out=tmpf[0][:, 0:2 * w], in_=ps[1][:, 0:2 * w]), [PSN[1]], ["tmpf0"], stream=False)
        P.op("dve", lambda e: e.tensor_tensor(out=tmpf[1][:, 0:2 * w], in0=ps[0][:, 0:2 * w], in1=tmpf[0][:, 0:2 * w], op=ALU.mult),
             [PSN[0], "tmpf0"], ["tmpf1"], stream=False)
        P.op("dve", lambda e: e.scalar_tensor_tensor(out=tmpf[2][:, 0:w], in0=tmpf[1][:, w:2 * w], scalar=neglam, in1=tmpf[1][:, 0:w],
                                                     op0=ALU.mult, op1=ALU.add), ["tmpf1", "sc"], ["tmpf2"], stream=False)
        P.op("act", lambda e: e.activation(out=tmpf[3][:, 0:w], in_=tmpf[2][:, 0:w], func=AF.Square), ["tmpf2"], ["tmpf3"], stream=False)
        b = nps()
        P.op("pe", lambda e: e.matmul(ps[b][:, 0:w], lhsT=onesF, rhs=tmpf[3][:, 0:w], start=True, stop=True), ["tmpf3", "cst"], [PSN[b]])
        rsqrt_op(tmpf[0][:, 0:w], ps[b][:, 0:w], [PSN[b]], "tmpf0", 1.0 / 128)
        P.op("dve", lambda e: e.scalar_tensor_tensor(out=o_dst, in0=tmpf[2][:, 0:w], scalar=gsub, in1=tmpf[0][:, 0:w], op0=ALU.mult, op1=ALU.mult),
             ["tmpf2", "tmpf0", "sc"], ["oT"], stream=False)

    def prompt_attention(i):
        ktiles = [("o", o) for o in range(3 * i + 3)] + [("w", s) for s in range(i + 1)]
        ATL = cfg.get('attl', 4)
        for wdx in range(3):
            P.op("dve", lambda e, wdx=wdx: e.tensor_scalar(out=onesf[:, wdx, :], in0=onesF, scalar1=flg[:, 3 * i + wdx:3 * i + wdx + 1],
                                                           scalar2=None, op0=ALU.mult),
                 ["cst", "flg"], ["onesf"], stream=False)
        nkt = len(ktiles)
        for h in range(NH):
            first = True
            sbi = 0
            for c0 in range(0, nkt, KC):
                chunk = ktiles[c0:c0 + KC]
                cb_ = (c0 // KC) % 2
                kres, vres = "KTc%d" % cb_, "Vc%d" % cb_
                runs = []
                for idx, (kind, n) in enumerate(chunk):
                    ksl = n if kind == "o" else NOTH + n
                    vsrc = ("vm", n) if (kind == "o" and n >= 3 * i) else ("v", ksl)
                    runs.append((idx, ksl, vsrc))
                j = 0
                while j < len(runs):
                    j2 = j
                    while j2 + 1 < len(runs) and runs[j2 + 1][1] == runs[j2][1] + 1:
                        j2 += 1
                    a, bnd = runs[j][1], runs[j2][1] + 1
                    P.op("sp", lambda e, j=j, a=a, bnd=bnd, h=h, cb_=cb_: e.dma_start(out=KTc[cb_][:, j:j + bnd - a, :], in_=kT_scr[h, :, a:bnd, :]),
                         reads=["kTs%d" % t for t in range(a, bnd)], writes=[kres], dma="lk%d" % cb_)
                    j = j2 + 1
                j = 0
                while j < len(runs):
                    j2 = j
                    while j2 + 1 < len(runs) and runs[j2 + 1][2][0] == runs[j2][2][0] and runs[j2 + 1][2][1] == runs[j2][2][1] + 1:
                        j2 += 1
                    kindv, a = runs[j][2]
                    bnd = runs[j2][2][1] + 1
                    srcv = vm_scr if kindv == "vm" else v_scr
                    rn = [("vms%d" if kindv == "vm" else "vs%d") % t for t in range(a, bnd)]
                    P.op("sp", lambda e, j=j, a=a, bnd=bnd, h=h, cb_=cb_, srcv=srcv: e.dma_start(
                        out=Vc[cb_][:, j * NSB:(j + bnd - a) * NSB, :], in_=srcv[h, :, a:bnd, :, :].rearrange("p t s e -> p (t s) e")),
                         reads=rn, writes=[vres], dma="lv%d" % cb_)
                    j = j2 + 1
                for idx, (kind, n) in enumerate(chunk):
                    diag = (kind == "w" and n == i)
                    lones = onesf[:, n - 3 * i, :] if (kind == "o" and n >= 3 * i) else onesb[:, :]
                    lon = "onesf" if (kind == "o" and n >= 3 * i) else "onesb"
                    for sbk in range(NSB):
                        last = (c0 + idx == nkt - 1) and (sbk == NSB - 1)
                        q0 = sbk * 128 if diag else 0
                        wq = TT - q0
                        sbank = 2 + (sbi % 2)
                        pslot = sbi % 2
                        sbi += 1
                        sv = ps[sbank][:, :].rearrange("p (m q) -> p m q", m=2)
                        for m in range((cfg.get('nm', 2)) if ATL >= 2 else 0):
                            P.op("pe", lambda e, m=m, idx=idx, sbk=sbk, q0=q0, cb_=cb_, sbank=sbank, h=h: e.matmul(
                                ps[sbank][:, m * TT + q0:(m + 1) * TT],
                                lhsT=KTc[cb_][:, idx, sbk * 128:(sbk + 1) * 128],
                                rhs=QT[:, m, h, q0:TT], start=True, stop=True),
                                 [kres, "QT"], [PSN[sbank]])
                        if ATL < 2:
                            continue
                        pv = Pt[pslot][:, 0:2 * TT].rearrange("p (m q) -> p m q", m=2)
                        P.op("act", lambda e, q0=q0, pv=pv, sv=sv: e.activation(out=pv[:, :, q0:TT], in_=sv[:, :, q0:TT], func=AF.Exp, scale=0.125),
                             [PSN[sbank]], ["Pt%d" % pslot], stream=True)
                        if diag:
                            P.op("dve", lambda e, q0=q0, pv=pv: e.memset(pv[64:128, :, q0:q0 + 64], 0.0), ["Pt%d" % pslot], ["Pt%d" % pslot], stream=False)
                        if ATL < 3:
                            continue
                        if q0 == 0:
                            P.op("pe", lambda e, idx=idx, sbk=sbk, cb_=cb_, pslot=pslot, first=first, last=last: e.matmul(
                                ps[0][:, 0:2 * TT], lhsT=Vc[cb_][:, idx * NSB + sbk, :], rhs=Pt[pslot][:, 0:2 * TT], start=first, stop=last),
                                 [vres, "Pt%d" % pslot], [PSN[0]])
                            P.op("pe", lambda e, pslot=pslot, first=first, last=last, lones=lones: e.matmul(
                                ps[1][:, 0:2 * TT], lhsT=lones, rhs=Pt[pslot][:, 0:2 * TT], start=first, stop=last),
                                 [lon, "Pt%d" % pslot], [PSN[1]])
                        else:
                            for m in range(2):
                                lm = last and m == 1
                                P.op("pe", lambda e, m=m, idx=idx, sbk=sbk, cb_=cb_, pslot=pslot, q0=q0, lm=lm: e.matmul(
                                    ps[0][:, m * TT + q0:(m + 1) * TT], lhsT=Vc[cb_][:, idx * NSB + sbk, :],
                                    rhs=Pt[pslot][:, m * TT + q0:(m + 1) * TT], start=False, stop=lm),
                                     [vres, "Pt%d" % pslot], [PSN[0]])
                                P.op("pe", lambda e, m=m, pslot=pslot, q0=q0, lm=lm, lones=lones: e.matmul(
                                    ps[1][:, m * TT + q0:(m + 1) * TT], lhsT=lones, rhs=Pt[pslot][:, m * TT + q0:(m + 1) * TT], start=False, stop=lm),
                                     [lon, "Pt%d" % pslot], [PSN[1]])
                        first = False
            if ATL >= 4:
                head_finish(oT[:, h, 0:TT], TT)

    def sample_attention():
        for bt in range(4):
            first = True
            for kb in range(33):
                nk = 128 if kb < 32 else 32
                cv_ = kb % 2
                if kb < 32:
                    load_cast(ckb, cache_k[bt, kb * 128:(kb + 1) * 128, :], "KTc0")
                    load_cast(cvb[cv_], cache_v[bt, kb * 128:(kb + 1) * 128, :], "Vc%d" % cv_)
                    transpose8(ckb, "KTc0", ckT, "KTc1")
                    kTsrc, kTname, kc0 = ckT, "KTc1", 0
                    vsrc, vname = cvb[cv_], "Vc%d" % cv_
                else:
                    kTsrc, kTname, kc0 = ktr, "ktr", bt * 32
                    P.op("sp", lambda e, bt=bt, cv_=cv_: e.dma_start(out=cvb[cv_][0:32, :], in_=vbf[bt * 32:(bt + 1) * 32, :]),
                         reads=["vbf0"], writes=["Vc%d" % cv_], dma="mv%d" % cv_)
                    vsrc, vname = cvb[cv_], "Vc%d" % cv_
                sbank = 2 + (kb % 2)
                pslot = kb % 2
                for h in range(NH):
                    for m in range(2):
                        cidx = (h * 2 + m) * 32
                        P.op("pe", lambda e, h=h, m=m, cidx=cidx, nk=nk, sbank=sbank, bt=bt, kTsrc=kTsrc, kc0=kc0: e.matmul(
                            ps[sbank][0:nk, cidx:cidx + 32], lhsT=kTsrc[:, h, kc0:kc0 + nk],
                            rhs=QT[:, m, h, bt * 32:(bt + 1) * 32], start=True, stop=True),
                             [kTname, "QT"], [PSN[sbank]])
                P.op("act", lambda e, nk=nk, sbank=sbank, pslot=pslot: e.activation(out=Pt[pslot][0:nk, :], in_=ps[sbank][0:nk, :], func=AF.Exp, scale=0.125),
                     [PSN[sbank]], ["Pt%d" % pslot], stream=True)
                last = kb == 32
                if first:
                    P.op("pe", lambda e: e.matmul(ps[0][:, :], lhsT=onesb[:, :], rhs=zrhs[:, :], start=True, stop=False), ["onesb", "zrhs"], [PSN[0]])
                for h in range(NH):
                    cidx = h * 64
                    P.op("pe", lambda e, h=h, cidx=cidx, nk=nk, pslot=pslot, first=first, last=last, vsrc=vsrc: e.matmul(
                        ps[0][:, cidx:cidx + 64], lhsT=vsrc[0:nk, h * 128:(h + 1) * 128], rhs=Pt[pslot][0:nk, cidx:cidx + 64],
                        start=False, stop=(last and h == NH - 1)), [vname, "Pt%d" % pslot], [PSN[0]])
                P.op("pe", lambda e, nk=nk, pslot=pslot, first=first, last=last: e.matmul(
                    ps[1][:, :], lhsT=onesb[0:nk, :], rhs=Pt[pslot][0:nk, :], start=first, stop=last), ["onesb", "Pt%d" % pslot], [PSN[1]])
                first = False
            P.op("dve", lambda e: e.reciprocal(out=tmpf[0][:, :], in_=ps[1][:, :]), [PSN[1]], ["tmpf0"], stream=False)
            P.op("dve", lambda e: e.tensor_tensor(out=tmpf[1][:, :], in0=ps[0][:, :], in1=tmpf[0][:, :], op=ALU.mult), [PSN[0], "tmpf0"], ["tmpf1"], stream=False)
            t1 = tmpf[1][:, :].rearrange("p (h m q) -> p h m q", m=2, q=32)
            o2 = tmpf[2][:, 0:256].rearrange("p (h q) -> p h q", q=32)
            P.op("dve", lambda e, t1=t1, o2=o2: e.scalar_tensor_tensor(out=o2, in0=t1[:, :, 1, :], scalar=neglam, in1=t1[:, :, 0, :], op0=ALU.mult, op1=ALU.add),
                 ["tmpf1", "sc"], ["tmpf2"], stream=False)
            P.op("act", lambda e: e.activation(out=tmpf[3][:, 0:256], in_=tmpf[2][:, 0:256], func=AF.Square), ["tmpf2"], ["tmpf3"], stream=False)
            b = nps()
            P.op("pe", lambda e, b=b: e.matmul(ps[b][:, 0:256], lhsT=onesF, rhs=tmpf[3][:, 0:256], start=True, stop=True), ["tmpf3", "cst"], [PSN[b]])
            rsqrt_op(tmpf[0][:, 0:256], ps[b][:, 0:256], [PSN[b]], "tmpf0", 1.0 / 128)
            r0 = tmpf[0][:, 0:256].rearrange("p (h q) -> p h q", q=32)
            P.op("dve", lambda e, bt=bt, o2=o2, r0=r0: e.scalar_tensor_tensor(out=oT[:, :, bt * 32:(bt + 1) * 32], in0=o2, scalar=gsub, in1=r0, op0=ALU.mult, op1=ALU.mult),
                 ["tmpf2", "tmpf0", "sc"], ["oT"], stream=False)

    def merge_phase(ntok):
        for jg in range(D // WS):
            s_ao = load_w(w_ao, 0, 8, jg * WS)
            for blk in range(2):
                b = featmajor_mm(oT, "oT", 8, s_ao, blk, ntok)
                copy_op("act", tA[:, blk, 0:ntok], ps[b][:, 0:ntok], [PSN[b]], ["tA"])
            s_ga = load_w(w_in, 0, NCH, 5120 + jg * WS)
            for blk in range(2):
                b = featmajor_mm(hT, "hT", NCH, s_ga, blk, ntok)
                P.op("act", lambda e, b=b: e.activation(out=tmpf[2][:, 0:ntok], in_=ps[b][:, 0:ntok], func=AF.Sigmoid), [PSN[b]], ["tmpf2"], stream=True)
                P.op("dve", lambda e, blk=blk: e.tensor_tensor(out=tA[:, blk, 0:ntok], in0=tA[:, blk, 0:ntok], in1=tmpf[2][:, 0:ntok], op=ALU.mult),
                     ["tA", "tmpf2"], ["tA"], stream=True)
            s_co = load_w(w_co, 0, 8, jg * WS)
            for blk in range(2):
                j = jg * 2 + blk
                b = featmajor_mm(cnT, "cnT", 8, s_co, blk, ntok)
                P.op("dve", lambda e, b=b, blk=blk, j=j: e.tensor_scalar(out=tB[:, blk, 0:ntok], in0=ps[b][:, 0:ntok], scalar1=col(P_BCO + j), scalar2=None, op0=ALU.add),
                     [PSN[b], "par"], ["tB"], stream=True)
            s_gb = load_w(w_in, 0, NCH, 7168 + jg * WS)
            for blk in range(2):
                j = jg * 2 + blk
                b = featmajor_mm(hT, "hT", NCH, s_gb, blk, ntok)
                P.op("act", lambda e, b=b: e.activation(out=tmpf[3][:, 0:ntok], in_=ps[b][:, 0:ntok], func=AF.Sigmoid), [PSN[b]], ["tmpf3"], stream=True)
                P.op("dve", lambda e, blk=blk: e.tensor_tensor(out=tB[:, blk, 0:ntok], in0=tB[:, blk, 0:ntok], in1=tmpf[3][:, 0:ntok], op=ALU.mult),
                     ["tB", "tmpf3"], ["tB"], stream=True)
                P.op("dve", lambda e, blk=blk, j=j: e.tensor_tensor(out=mrgT[:, j, 0:ntok], in0=tA[:, blk, 0:ntok], in1=tB[:, blk, 0:ntok], op=ALU.add),
                     ["tA", "tB"], ["mrgT", "KTc0", "KTc1"], stream=True)

    def proj_norm_residual(srcT, srcname, nk_total, wsrc, gbase, ntok):
        for jg in range(D // WS):
            kgs = [(k0, min(NCH, nk_total - k0)) for k0 in range(0, nk_total, NCH)]
            bs = [nps(), nps()]
            for gi, (k0, nk) in enumerate(kgs):
                slot = load_w(wsrc, k0, nk, jg * WS)
                for blk in range(2):
                    b = bs[blk]
                    for c in range(nk):
                        P.op("pe", lambda e, c=c, b=b, blk=blk, slot=slot, k0=k0, gi=gi, nk=nk: e.matmul(
                            ps[b][:, 0:ntok], lhsT=wsl[slot][:, c, blk * 128:(blk + 1) * 128], rhs=srcT[:, k0 + c, 0:ntok],
                            start=(gi == 0 and c == 0), stop=(gi == len(kgs) - 1 and c == nk - 1)),
                             (([srcname] + wres(slot)) + (["ext", "cnT"] + ACCN if srcname == "actT" else ["KTc0", "KTc1"] if srcname == "mrgT" else [])), [PSN[b]])
            for blk in range(2):
                copy_op(evac_eng(), bufA[:, jg * 2 + blk, 0:ntok], ps[bs[blk]][:, 0:ntok], [PSN[bs[blk]]], ["bufA"])
        norm_stats(bufA, "bufA", NCH, ntok, 1.0 / D, None)
        for c in range(NCH):
            P.op("dve", lambda e, c=c: e.scalar_tensor_tensor(out=bufA[:, c, 0:ntok], in0=bufA[:, c, 0:ntok], scalar=col(gbase + c), in1=rstd[:, 0:ntok],
                                                              op0=ALU.mult, op1=ALU.mult), ["bufA", "rstd", "par"], ["bufA"], stream=True)
            P.op("dve", lambda e, c=c: e.tensor_tensor(out=xT[:, c, 0:ntok], in0=xT[:, c, 0:ntok], in1=bufA[:, c, 0:ntok], op=ALU.add),
                 ["bufA", "xT"], ["xT"], stream=True)

    def ffn_act(ntok):
        for fg in range(DFF // WS):
            sg = load_w(w_fg, 0, NCH, fg * WS)
            su = load_w(w_fu, 0, NCH, fg * WS)
            for blk in range(2):
                bg = featmajor_mm(hT, "hT", NCH, sg, blk, ntok)
                bu = featmajor_mm(hT, "hT", NCH, su, blk, ntok)
                P.op("act", lambda e, bg=bg: e.activation(out=tmpf[2][:, 0:ntok], in_=ps[bg][:, 0:ntok], func=AF.Sigmoid), [PSN[bg]], ["tmpf2"], stream=True)
                P.op("dve", lambda e, bg=bg: e.tensor_tensor(out=tmpf[2][:, 0:ntok], in0=ps[bg][:, 0:ntok], in1=tmpf[2][:, 0:ntok], op=ALU.mult),
                     [PSN[bg], "tmpf2"], ["tmpf2"], stream=True)
                P.op("dve", lambda e, bu=bu, fg=fg, blk=blk: e.tensor_tensor(out=actT[:, fg * 2 + blk, 0:ntok], in0=ps[bu][:, 0:ntok], in1=tmpf[2][:, 0:ntok], op=ALU.mult),
                     [PSN[bu], "tmpf2"], ["actT", "ext", "cnT"] + ACCN, stream=True)

    def store_y(ydst, row0, nsub):
        for s in range(nsub):
            xi = 0
            for c4 in range(4):
                b = nps()
                for k in range(4):
                    c = c4 * 4 + k
                    P.op("pe", lambda e, c=c, k=k, b=b, s=s: e.transpose(out=ps[b][:, k * 128:(k + 1) * 128], in_=xT[:, c, s * 128:(s + 1) * 128], identity=identf),
                         ["xT", "cst"], [PSN[b]])
                copy_op(evac_eng(), xin[xi][:, c4 * 512:(c4 + 1) * 512], ps[b][:, :], [PSN[b]], ["xin%d" % xi])
            P.op("act", lambda e, xi=xi, s=s: e.dma_start(out=ydst[row0 + s * 128: row0 + (s + 1) * 128, :], in_=xin[xi][:]),
                 reads=["xin%d" % xi], dma="y%d" % xi)

    def layer_tail(ntok):
        merge_phase(ntok)
        proj_norm_residual(mrgT, "mrgT", NCH, w_o, P_GPOST, ntok)
        norm_stats(xT, "xT", NCH, ntok, 1.0 / D, None)
        for c in range(NCH):
            P.op("dve", lambda e, c=c: e.scalar_tensor_tensor(out=hT[:, c, 0:ntok], in0=xT[:, c, 0:ntok], scalar=col(P_GFFN + c), in1=rstd[:, 0:ntok],
                                                              op0=ALU.mult, op1=ALU.mult), ["xT", "rstd", "par"], ["hT"], stream=True)
        ffn_act(ntok)
        proj_norm_residual(actT, "actT", NFC, w_fd, P_GPFF, ntok)

    NHS = (NOWN * 32) // 128
    halov = halo[:, :, :, :].rearrange("p c i t -> p c (i t)")
    C_ST = cfg.get('stage', 3)
    if C_HALO:
        if C_ST >= 1:
            load_xT(x_halo, 0, NHS, NHS * 128)
        if C_ST >= 2:
            make_hT(P_GPRE, NHS * 128)
        if C_ST >= 3:
            glu_phase(NHS * 128, lambda cb: halov[:, cb, :], "halo")

    for o in range(C_NOTH):
        load_xT(x_oth, o * TT, NSB, TT)
        make_hT(P_GPRE, TT)
        qkv_phase(o, o * NSB, NSB, None, None, 0, False, flag_o=o)

    for i in range(C_NOWN):
        load_xT(x_own, i * TT, NSB, TT)
        make_hT(P_GPRE, TT)
        qkv_phase(NOTH + i, (NOTH + i) * NSB, NSB, k_own, v_own, i * TT, True)
        for cb in range(8):
            copy_op("act", ext[:, cb, 0:32], halo[:, cb, i, :], ["halo"], ["ext"], stream=False)
        glu_phase(TT, lambda cb: ext[:, cb, 32:32 + TT], "ext")
        if i == NOWN - 1:
            b = nps()
            for cb in range(8):
                P.op("pe", lambda e, cb=cb, b=b: e.transpose(out=ps[b][0:32, (cb % 4) * 128:(cb % 4 + 1) * 128],
                                                             in_=ext[:, cb, TT:TT + 32], identity=identf), ["ext", "cst"], [PSN[b]])
                if cb % 4 == 3:
                    copy_op("dve", stc[0:32, (cb - 3) * 128:(cb + 1) * 128], ps[b][0:32, :], [PSN[b]], ["kvst0"], stream=False)
                    if cb == 3:
                        b = nps()
            P.op("act", lambda e: e.dma_start(out=conv_p[:, :], in_=stc[2:32, :]), reads=["kvst0"], dma="oc")
        if cfg.get('conv', True):
            conv_phase(lambda cb, k: ext[:, cb, 2 + k:2 + k + TT], lambda cb: acc[:, cb, 0:TT], None)
            if cfg.get('convl', 3) >= 2:
                conv_norm(TT)
        if cfg.get('attn', True):
            prompt_attention(i)
        if C_TAIL:
            layer_tail(TT)
        store_y(y_own, i * TT, NSB)

    if C_SMP:
        load_xT(x_smp, 0, 1, 128)
        make_hT(P_GPRE, 128)
        qkv_phase(None, NTL * NSB, 1, k_s, v_s, 0, True)
        transpose8(kbf, "kbf", ktr[:, :, :], "ktr")
        extS = ext[:, :, 0:248].rearrange("p c (b t) -> p c b t", t=62)
        for bt in range(4):
            P.op("sp", lambda e, bt=bt: e.dma_start(out=stc[0:30, :], in_=state_conv[bt, :, :]), writes=["kvst0"], dma="sc")
            b = nps()
            for cb in range(8):
                P.op("pe", lambda e, cb=cb, b=b: e.transpose(out=ps[b][:, cb * 32:cb * 32 + 30], in_=stc[0:30, cb * 128:(cb + 1) * 128], identity=identf[0:30, 0:30]),
                     ["kvst0", "cst"], [PSN[b]])
            copy_op("dve", extS[:, :, bt, 0:30], ps[b][:, 0:256].rearrange("p (c t) -> p c t", t=32)[:, :, 0:30], [PSN[b]], ["ext"], stream=False)
        glu_phase(128, lambda cb: extS[:, cb, :, 30:62], "ext", inview=lambda a: a.rearrange("p (b t) -> p b t", t=32))
        for bt in range(4):
            b = nps()
            for cb in range(8):
                P.op("pe", lambda e, cb=cb, b=b, bt=bt: e.transpose(out=ps[b][0:32, (cb % 4) * 128:(cb % 4 + 1) * 128], in_=extS[:, cb, bt, 30:62], identity=identf),
                     ["ext", "cst"], [PSN[b]])
                if cb % 4 == 3:
                    copy_op("dve", stc[0:32, (cb - 3) * 128:(cb + 1) * 128], ps[b][0:32, :], [PSN[b]], ["kvst0"], stream=False)
                    if cb == 3:
                        b = nps()
            P.op("act", lambda e, bt=bt: e.dma_start(out=conv_s[bt, :, :], in_=stc[2:32, :]), reads=["kvst0"], dma="oc")
        conv_phase(lambda cb, k: extS[:, cb, :, k:k + 32], lambda cb: acc[:, cb, 0:128].rearrange("p (b t) -> p b t", t=32), None)
        conv_norm(128)
        sample_attention()
        layer_tail(128)
        store_y(y_s, 0, 1)

    print("ops:", {e: len(v) for e, v in P.ops.items()}, flush=True)
    with nc.Block() as block:
        semcms = P.emit(nc, block)
    return nc


_NC_CACHE = {}


def _rope_tab(pos):
    inv = (500000.0 ** (-np.arange(0, 16, 2, dtype=np.float32) / 16.0)).astype(np.float32)
    ang = pos.astype(np.float32)[:, None] * inv[None, :]
    return np.concatenate([np.cos(ang), np.sin(ang)], axis=1).astype(np.float32)


def prepare(inputs):
    f = lambda k: np.ascontiguousarray(np.asarray(inputs[k], dtype=np.float32))
    xp, xs = f("x_prompt"), f("x_sample")
    ck, cv, scv = f("cache_k")[0], f("cache_v")[0], f("state_conv")[0]
    par = np.zeros((128, NPAR), np.float32)
    colmaj = lambda v, n: np.ascontiguousarray(v.reshape(n, 128).T)
    par[:, P_GPRE:P_GPRE + 16] = colmaj(f("g_pre_mix")[0], 16)
    par[:, P_GPOST:P_GPOST + 16] = colmaj(f("g_post_mix")[0], 16)
    par[:, P_GFFN:P_GFFN + 16] = colmaj(f("g_pre_ffn")[0], 16)
    par[:, P_GPFF:P_GPFF + 16] = colmaj(f("g_post_ffn")[0], 16)
    par[:, P_BCO:P_BCO + 16] = colmaj(f("b_conv_out")[0], 16)
    par[:, P_GSUB] = f("g_subln")[0]
    wdw = f("w_dw")[0]
    par[:, P_WDW:P_WDW + 248] = wdw.reshape(31, 8, 128).transpose(2, 0, 1).reshape(128, 248)
    par[:, P_BDW:P_BDW + 8] = colmaj(f("b_dw")[0], 8)
    par[:, P_GCN:P_GCN + 8] = colmaj(f("g_conv_norm")[0], 8)
    par[:, P_BCN:P_BCN + 8] = colmaj(f("b_conv_norm")[0], 8)
    for n, key in enumerate(["lambda_q1", "lambda_k1", "lambda_q2", "lambda_k2"]):
        par[:, P_LAM + 64 * n:P_LAM + 64 * (n + 1)] = f(key)[0][None, :]
    consts = np.concatenate([np.eye(128, dtype=np.float32), np.ones((128, 128), np.float32)], axis=1)
    w = {"w_in": f("w_in")[0], "w_ao": f("w_attn_out")[0], "w_co": f("w_conv_out")[0], "w_o": f("w_o")[0],
         "w_fg": f("w_ffn_gate")[0], "w_fu": f("w_ffn_up")[0], "w_fd": f("w_ffn_down")[0]}
    in_maps = []
    own_tiles = {}
    for core in range(8):
        b, j = core // 4, core % 4
        own = [4 * i + j for i in range(NOWN)]
        oth = [4 * wd + k for wd in range(NOWN) for k in range(4) if k != j]
        own_tiles[core] = own
        xb = xp[b]
        x_oth = np.concatenate([xb[t * TT:(t + 1) * TT] for t in oth], axis=0)
        x_own = np.concatenate([xb[t * TT:(t + 1) * TT] for t in own], axis=0)
        x_halo = np.zeros((NOWN * 32, D), np.float32)
        for i, t in enumerate(own):
            if t > 0:
                x_halo[i * 32:(i + 1) * 32] = xb[t * TT - 32:t * TT]
        pos = np.concatenate([np.arange(t * TT, (t + 1) * TT) for t in oth + own] + [4096 + (np.arange(128) % 32)])
        rt = _rope_tab(pos).reshape(NTL * NSB + 1, 128, 16).transpose(1, 0, 2)
        flags = np.zeros((128, NOTH), np.float32)
        for o, t in enumerate(oth):
            flags[:, o] = 1.0 if (t % 4) < j else 0.0
        m = {"x_oth": x_oth, "x_own": x_own, "x_halo": x_halo,
             "x_smp": np.ascontiguousarray(xs[4 * core:4 * core + 4].reshape(128, D)),
             "rope": np.ascontiguousarray(rt), "flags": flags, "params": par, "consts": consts,
             "cache_k": np.ascontiguousarray(ck[4 * core:4 * core + 4].reshape(4, 4096, 1024)),
             "cache_v": np.ascontiguousarray(cv[4 * core:4 * core + 4].reshape(4, 4096, 1024)),
             "state_conv": np.ascontiguousarray(scv[4 * core:4 * core + 4])}
        m.update(w)
        in_maps.append(m)
    return in_maps, own_tiles


def kernel(**inputs):
    in_maps, own_tiles = prepare(inputs)
    if "nc" not in _NC_CACHE:
        _NC_CACHE["nc"] = build_nc()
    res = run_bass_kernel_spmd(_NC_CACHE["nc"], in_maps, core_ids=list(range(8)))
    R = res.results
    yp = np.zeros((2, SEQ, D), np.float32)
    kp = np.zeros((1, 2, SEQ, 1024), np.float32)
    vp = np.zeros((1, 2, SEQ, 1024), np.float32)
    cp = np.zeros((1, 2, 30, CC), np.float32)
    ys = np.zeros((32, 32, D), np.float32)
    ks = np.zeros((1, 32, 32, 1024), np.float32)
    vs = np.zeros((1, 32, 32, 1024), np.float32)
    cs = np.zeros((1, 32, 30, CC), np.float32)
    for core in range(8):
        b, j = core // 4, core % 4
        r = R[core]
        for i, t in enumerate(own_tiles[core]):
            yp[b, t * TT:(t + 1) * TT] = r["y_own"][i * TT:(i + 1) * TT]
            kp[0, b, t * TT:(t + 1) * TT] = r["k_own"][i * TT:(i + 1) * TT]
            vp[0, b, t * TT:(t + 1) * TT] = r["v_own"][i * TT:(i + 1) * TT]
        if j == 3:
            cp[0, b] = r["conv_p"]
        ys[4 * core:4 * core + 4] = r["y_s"].reshape(4, 32, D)
        ks[0, 4 * core:4 * core + 4] = r["k_s"].reshape(4, 32, 1024)
        vs[0, 4 * core:4 * core + 4] = r["v_s"].reshape(4, 32, 1024)
        cs[0, 4 * core:4 * core + 4] = r["conv_s"]
    return (yp, ys, kp.reshape(1, 2, SEQ, 8, 2, 64), vp.reshape(1, 2, SEQ, 8, 128), cp,
            ks.reshape(1, 32, 32, 8, 2, 64), vs.reshape(1, 32, 32, 8, 128), cs)
```

```python
import math
import numpy as np
import ml_dtypes
import concourse.bass as bass
import concourse.mybir as mybir
from concourse.bass_utils import run_bass_kernel_spmd

F32 = mybir.dt.float32
BF16 = mybir.dt.bfloat16
ALU = mybir.AluOpType
AF = mybir.ActivationFunctionType
AX = mybir.AxisListType

D = 2048
NCH = 16
SEQ = 8192
NH = 8
CC = 1024
DFF = 5632
NFC = 44
INC = 9216
EPS = 1e-6
TT = 256
NSB = TT // 128
NOWN = 2048 // TT
NOTH = 3 * NOWN
NTL = NOTH + NOWN
WS = 256
KC = 8
LAM_INIT = 0.8 - 0.6 * math.exp(0.0)
SKIP_STREAM_SYNC = False

P_GPRE, P_GPOST, P_GFFN, P_GPFF, P_BCO = 0, 16, 32, 48, 64
P_GSUB = 80
P_WDW = 81
P_BDW = P_WDW + 248
P_GCN = P_BDW + 8
P_BCN = P_GCN + 8
P_LAM = P_BCN + 8
NPAR = P_LAM + 256


class Prog:
    ENGS = ["pe", "act", "dve", "pool", "sp"]

    def __init__(self):
        self.ops = {e: [] for e in self.ENGS}
        self.res = {}
        self.dma_cnt = {}
        self.dma_ep = {}

    def op(self, eng, fn, reads=(), writes=(), dma=None, stream=False, join=False):
        o = {"eng": eng, "fn": fn, "deps": [], "needed": False, "dma": dma, "stream": stream}
        deps = []
        for r in reads:
            st = self.res.setdefault(r, {"w": None, "r": []})
            if st["w"] is not None:
                deps.append(st["w"])
        for w in writes:
            st = self.res.setdefault(w, {"w": None, "r": []})
            if st["w"] is not None and not (join and st["w"].get("dma_base") == dma):
                deps.append(st["w"])
            deps.extend(st["r"])
        seen = set()
        for a in deps:
            if id(a) in seen or a is o:
                continue
            seen.add(id(a))
            if a["dma"] is None and a["eng"] == eng:
                if eng in ("pe", "sp"):
                    continue
                if SKIP_STREAM_SYNC and a["stream"] and stream:
                    continue
            if a["dma"] is None:
                a["needed"] = True
            o["deps"].append(a)
        for r in reads:
            self.res[r]["r"].append(o)
        for w in writes:
            self.res[w] = {"w": o, "r": []}
        if dma is not None:
            ep = self.dma_ep.get(dma, 0)
            if not join and self.dma_cnt.get("%s_%d" % (dma, ep), 0) >= 1000:
                ep += 1
                self.dma_ep[dma] = ep
            dname = "%s_%d" % (dma, ep)
            o["dma"] = dname
            o["dma_base"] = dma
            self.dma_cnt[dname] = self.dma_cnt.get(dname, 0) + 1
            o["dmaval"] = 16 * self.dma_cnt[dname]
        self.ops[eng].append(o)
        return o

    def emit(self, nc, block):
        sems = {}
        allsems = []

        def getsem(name):
            if name not in sems:
                cm = nc.semaphore("s_" + name)
                s = cm.__enter__()
                allsems.append(cm)
                sems[name] = s
            return sems[name]

        for e in self.ENGS:
            n = 0
            ep = 0
            for o in self.ops[e]:
                if o["dma"] is None and o["needed"]:
                    if n >= 12000:
                        n = 0
                        ep += 1
                    n += 1
                    o["semval"] = n
                    o["semkey"] = "e_%s_%d" % (e, ep)
                    getsem(o["semkey"])
        for d in self.dma_cnt:
            getsem("d_" + d)

        def run(e, engobj):
            waited = {}
            for o in self.ops[e]:
                need = {}
                for a in o["deps"]:
                    if a["dma"] is not None:
                        key, val = "d_" + a["dma"], a["dmaval"]
                    else:
                        key, val = a["semkey"], a["semval"]
                    need[key] = max(need.get(key, 0), val)
                for key, val in need.items():
                    if waited.get(key, 0) >= val:
                        continue
                    waited[key] = val
                    engobj.wait_ge(sems[key], val)
                ins = o["fn"](engobj)
                if o["dma"] is not None:
                    ins.then_inc(sems["d_" + o["dma"]], 16)
                elif o["needed"]:
                    ins.then_inc(sems[o["semkey"]], 1)
            if e == "sp":
                for d, c in self.dma_cnt.items():
                    engobj.wait_ge(sems["d_" + d], 16 * c)

        block.tensor(lambda t: run("pe", t))
        block.scalar(lambda t: run("act", t))
        block.vector(lambda t: run("dve", t))
        block.gpsimd(lambda t: run("pool", t))
        block.sync(lambda t: run("sp", t))
        return allsems


def build_nc(cfg=None):
    cfg = cfg or {}
    C_HALO = cfg.get('halo', True)
    C_NOTH = cfg.get('noth', NOTH)
    C_NOWN = cfg.get('nown', NOWN)
    C_SMP = cfg.get('sample', True)
    C_TAIL = cfg.get('tail', True)
    C_SCR = cfg.get('scr', True)
    C_ROPE = cfg.get('rope', True)
    C_TR = cfg.get('tr', True)
    nc = bass.Bass("TRN2", target_bir_lowering=False)
    dt_in = lambda n, s, d=F32: nc.dram_tensor(n, list(s), d, kind="ExternalInput").ap()
    dt_out = lambda n, s: nc.dram_tensor(n, list(s), F32, kind="ExternalOutput").ap()
    x_oth = dt_in("x_oth", [NOTH * TT, D])
    x_own = dt_in("x_own", [NOWN * TT, D])
    x_halo = dt_in("x_halo", [NOWN * 32, D])
    x_smp = dt_in("x_smp", [128, D])
    rope_d = dt_in("rope", [128, NTL * NSB + 1, 16])
    flags_d = dt_in("flags", [128, NOTH])
    params_d = dt_in("params", [128, NPAR])
    consts_d = dt_in("consts", [128, 256])
    cache_k = dt_in("cache_k", [4, 4096, 1024])
    cache_v = dt_in("cache_v", [4, 4096, 1024])
    state_conv = dt_in("state_conv", [4, 30, CC])
    w_in = dt_in("w_in", [D, INC])
    w_ao = dt_in("w_ao", [CC, D])
    w_co = dt_in("w_co", [CC, D])
    w_o = dt_in("w_o", [D, D])
    w_fg = dt_in("w_fg", [D, DFF])
    w_fu = dt_in("w_fu", [D, DFF])
    w_fd = dt_in("w_fd", [DFF, D])
    y_own = dt_out("y_own", [NOWN * TT, D])
    k_own = dt_out("k_own", [NOWN * TT, 1024])
    v_own = dt_out("v_own", [NOWN * TT, 1024])
    conv_p = dt_out("conv_p", [30, CC])
    y_s = dt_out("y_s", [128, D])
    k_s = dt_out("k_s", [128, 1024])
    v_s = dt_out("v_s", [128, 1024])
    conv_s = dt_out("conv_s", [4, 30, CC])
    kT_scr = nc.dram_tensor("kT_scr", [NH, 128, NTL, TT], BF16, kind="Internal").ap()
    v_scr = nc.dram_tensor("v_scr", [NH, 128, NTL, NSB, 128], BF16, kind="Internal").ap()
    vm_scr = nc.dram_tensor("vm_scr", [NH, 128, NOTH, NSB, 128], BF16, kind="Internal").ap()

    P = Prog()
    cms = []

    def sb(name, shape, dt):
        cm = nc.sbuf_tensor(name, list(shape), dt)
        t = cm.__enter__()
        cms.append(cm)
        return t

    ps = []
    for i in range(8):
        cm = nc.psum_tensor("ps%d" % i, [128, 512], F32)
        ps.append(cm.__enter__())
        cms.append(cm)
    PSN = ["ps%d" % i for i in range(8)]

    TM = 128 + TT
    xT = sb("xT", [128, NCH, TT], F32)
    hT = sb("hT", [128, NCH, TT], BF16)
    bufA = sb("bufA", [128, NCH, TT], F32)
    xin = [sb("xin0", [128, D], F32)] * 2
    kvst = [sb("kvst%d" % i, [128, 2048], F32) for i in range(2)]
    NWSL = 2
    NSTG = 5
    wsl = [sb("wsl%d" % i, [128, NCH, WS], BF16) for i in range(NWSL)]
    stg = [sb("stg%d" % i, [128, 1024], F32) for i in range(NSTG)]
    QT = sb("QT", [128, 2, NH, TT], BF16)
    oT = sb("oT", [128, NH, TT], BF16)
    arena = sb("arena", [128, NFC * TT], BF16)
    actT = arena[:, :].rearrange("p (c t) -> p c t", t=TT)
    _e0 = 8 * (32 + TT) * 2
    ext = arena[:, 0:_e0].bitcast(F32).rearrange("p (c t) -> p c t", t=32 + TT)
    _a0 = _e0 + 8 * TT * 2
    acc = arena[:, _e0:_a0].bitcast(F32).rearrange("p (c t) -> p c t", t=TT)
    cnT = arena[:, _a0:_a0 + 8 * TT].rearrange("p (c t) -> p c t", t=TT)
    assert _a0 + 8 * TT <= NFC * TT
    kvbuf = sb("kvbuf", [128, 4 * KC * TT], BF16)
    KTc = [kvbuf[:, i * KC * TT:(i + 1) * KC * TT].rearrange("p (a b) -> p a b", b=TT) for i in range(2)]
    Vc = [kvbuf[:, (2 + i) * KC * TT:(3 + i) * KC * TT].rearrange("p (a b) -> p a b", b=128) for i in range(2)]
    mrgT = kvbuf[:, 0:NCH * TT].rearrange("p (a b) -> p a b", b=TT)
    assert NCH * TT <= 2 * KC * TT
    Pt = [sb("Pt%d" % i, [128, 512], BF16) for i in range(2)]
    tmpf = [sb("tmpf%d" % i, [128, 512], F32) for i in range(4)]
    qtmps = [sb("qtmp%d" % i, [128, 1024], F32) for i in range(2)]
    qbf = sb("qbf", [128, 1024], BF16)
    kbf = sb("kbf", [128, 1024], BF16)
    vbfs = [sb("vbf%d" % i, [128, 1024], BF16) for i in range(2)]
    vbf = vbfs[0]
    vmf = sb("vmf", [128, 1024], BF16)
    ktr = sb("ktr", [128, NH, 128], BF16)
    rtmp = [sb("rtmp%d" % i, [128, 16, 8], F32) for i in range(4)]
    tA = sb("tA", [128, 2, TT], F32)
    tB = sb("tB", [128, 2, TT], F32)
    rstd = sb("rstd", [128, TT], F32)
    mean = sb("mean", [128, TT], F32)
    par = sb("par", [128, NPAR], F32)
    rope = sb("rope_sb", [128, NTL * NSB + 1, 16], F32)
    flg = sb("flg", [128, NOTH], F32)
    cst = sb("cst_sb", [128, 256], F32)
    identb = sb("identb", [128, 128], BF16)
    onesb = sb("onesb", [128, 128], BF16)
    onesf = sb("onesf", [128, 3, 128], BF16)
    halo = sb("halo", [128, 8, NOWN, 32], F32)
    sc = sb("sc", [128, 16], F32)
    zrhs = sb("zrhs", [128, 512], BF16)
    ckb = KTc[0][:, :, :].rearrange("p a b -> p (a b)")[:, 0:1024]
    cvb = [Vc[i][:, :, :].rearrange("p a b -> p (a b)")[:, 0:1024] for i in range(2)]
    ckT = KTc[1][:, :, 0:128]
    stc = kvst[0][0:32, 0:CC]
    identf = cst[:, 0:128]
    onesF = cst[:, 128:256]

    state = {"ps": 0, "ws": 0, "xin": 0, "kv": 0, "ev": 0, "stg": 0, "ce": 0}

    def nps():
        i = state["ps"]
        state["ps"] = (i + 1) % 6
        return i + 2

    def evac_eng():
        state["ev"] ^= 1
        return "act" if state["ev"] else "dve"

    def copy_op(eng, out, in_, reads, writes, stream=True):
        if eng == "act":
            P.op("act", lambda e: e.copy(out=out, in_=in_), reads, writes, stream=stream)
        else:
            P.op(eng, lambda e: e.tensor_copy(out=out, in_=in_), reads, writes, stream=stream)

    ACCN = ["acc%d" % cb for cb in range(8)]

    def col(c):
        return par[:, c:c + 1]

    P.op("sp", lambda e: e.dma_start(out=par[:], in_=params_d), writes=["par"], dma="c0")
    P.op("sp", lambda e: e.dma_start(out=rope[:], in_=rope_d), writes=["rope"], dma="c1")
    P.op("sp", lambda e: e.dma_start(out=flg[:], in_=flags_d), writes=["flg"], dma="c2")
    P.op("sp", lambda e: e.dma_start(out=cst[:], in_=consts_d), writes=["cst"], dma="c3")
    copy_op("dve", identb[:], identf, ["cst"], ["identb"], stream=False)
    P.op("dve", lambda e: e.memset(zrhs[:, :], 0.0), [], ["zrhs"], stream=False)
    P.op("dve", lambda e: e.memset(QT[:, :, :, :].rearrange("p m h t -> p (m h t)"), 0.0), [], ["QT"], stream=False)
    copy_op("dve", onesb[:], onesF, ["cst"], ["onesb"], stream=False)
    lp = tmpf[0]
    P.op("dve", lambda e: e.tensor_tensor(out=lp[:, 0:64], in0=par[:, P_LAM:P_LAM + 64], in1=par[:, P_LAM + 64:P_LAM + 128], op=ALU.mult),
         ["par"], ["tmpf0"], stream=False)
    P.op("dve", lambda e: e.tensor_tensor(out=lp[:, 64:128], in0=par[:, P_LAM + 128:P_LAM + 192], in1=par[:, P_LAM + 192:P_LAM + 256], op=ALU.mult),
         ["par", "tmpf0"], ["tmpf0"], stream=False)
    P.op("dve", lambda e: e.tensor_reduce(out=sc[:, 3:4], in_=lp[:, 0:64], axis=AX.X, op=ALU.add), ["tmpf0"], ["sc"], stream=False)
    P.op("dve", lambda e: e.tensor_reduce(out=sc[:, 4:5], in_=lp[:, 64:128], axis=AX.X, op=ALU.add), ["tmpf0", "sc"], ["sc"], stream=False)
    P.op("act", lambda e: e.activation(out=sc[:, 5:7], in_=sc[:, 3:5], func=AF.Exp), ["sc"], ["sc"], stream=False)
    P.op("dve", lambda e: e.tensor_tensor(out=sc[:, 0:1], in0=sc[:, 5:6], in1=sc[:, 6:7], op=ALU.subtract), ["sc"], ["sc"], stream=False)
    P.op("dve", lambda e: e.tensor_scalar(out=sc[:, 1:2], in0=sc[:, 0:1], scalar1=LAM_INIT, scalar2=-1.0, op0=ALU.add, op1=ALU.mult),
         ["sc"], ["sc"], stream=False)
    P.op("dve", lambda e: e.tensor_scalar(out=sc[:, 2:3], in0=col(P_GSUB), scalar1=1.0 - LAM_INIT, scalar2=None, op0=ALU.mult),
         ["sc", "par"], ["sc"], stream=False)
    P.op("dve", lambda e: e.memset(sc[:, 8:9], EPS), ["sc"], ["sc"], stream=False)
    epsc = sc[:, 8:9]
    neglam = sc[:, 1:2]
    gsub = sc[:, 2:3]

    def rsqrt_op(dst, src, reads, dstname, scale):
        P.op("act", lambda e: e.activation(out=dst, in_=src, func=AF.Sqrt, bias=epsc, scale=scale), list(reads) + ["sc"], [dstname], stream=False)
        P.op("dve", lambda e: e.reciprocal(out=dst, in_=dst), [dstname], [dstname], stream=False)

    def wres(slot):
        return ["wsl%dq%d" % (slot, q) for q in range(4)]

    CAST_ENGS = ["dve", "act", "pool", "act", "dve"]

    def load_cast(dst, src, dstname, n_inner=None):
        st = state["stg"]
        state["stg"] = (st + 1) % NSTG
        sv = stg[st][:, :]
        if n_inner is not None:
            sv = stg[st][:, 0:dst.shape[1] * n_inner].rearrange("p (c n) -> p c n", n=n_inner)
        else:
            sv = stg[st][:, 0:dst.shape[1]]
        P.op("sp", lambda e: e.dma_start(out=sv, in_=src), writes=["stg%d" % st], dma="g%d" % st)
        eng = CAST_ENGS[state["ce"] % len(CAST_ENGS)]
        state["ce"] += 1
        copy_op(eng, dst, sv, ["stg%d" % st], [dstname])

    def load_w(src, k0, nk, c0, ncols=WS):
        s = state["ws"]
        state["ws"] = (s + 1) % NWSL
        for qi, q0 in enumerate(range(0, nk, 4)):
            q1 = min(nk, q0 + 4)
            load_cast(wsl[s][:, q0:q1, 0:ncols],
                      src[(k0 + q0) * 128:(k0 + q1) * 128, c0:c0 + ncols].rearrange("(c p) n -> p c n", p=128),
                      "wsl%dq%d" % (s, qi), n_inner=ncols)
        return s

    def norm_stats(srcT, srcname, nch, ntok, inv_n, tagps):
        b = nps()
        for c in range(nch):
            t = c % 2
            P.op("act", lambda e, c=c, t=t: e.activation(out=tmpf[t][:, 0:ntok], in_=srcT[:, c, 0:ntok], func=AF.Square),
                 [srcname], ["tmpf%d" % t], stream=True)
            P.op("pe", lambda e, c=c, t=t, b=b: e.matmul(ps[b][:, 0:ntok], lhsT=onesF, rhs=tmpf[t][:, 0:ntok],
                                                         start=(c == 0), stop=(c == nch - 1)),
                 ["tmpf%d" % t, "cst"], [PSN[b]])
        rsqrt_op(rstd[:, 0:ntok], ps[b][:, 0:ntok], [PSN[b]], "rstd", inv_n)

    def load_xT(xsrc, row0, nsub, ntok):
        for s in range(nsub):
            xi = 0
            P.op("sp", lambda e, xi=xi, s=s: e.dma_start(out=xin[xi][:], in_=xsrc[row0 + s * 128: row0 + (s + 1) * 128, :]),
                 writes=["xin%d" % xi], dma="x%d" % xi)
            for c4 in range(4):
                b = nps()
                for k in range(4):
                    c = c4 * 4 + k
                    P.op("pe", lambda e, xi=xi, c=c, k=k, b=b: e.transpose(out=ps[b][:, k * 128:(k + 1) * 128],
                                                                           in_=xin[xi][:, c * 128:(c + 1) * 128], identity=identf),
                         ["xin%d" % xi, "cst"], [PSN[b]])
                copy_op(evac_eng(), xT[:, c4 * 4:c4 * 4 + 4, s * 128:(s + 1) * 128],
                        ps[b][:, :].rearrange("p (c t) -> p c t", t=128), [PSN[b]], ["xT"])

    def make_hT(gbase, ntok):
        norm_stats(xT, "xT", NCH, ntok, 1.0 / D, None)
        for c in range(NCH):
            P.op("dve", lambda e, c=c: e.scalar_tensor_tensor(out=hT[:, c, 0:ntok], in0=xT[:, c, 0:ntok], scalar=col(gbase + c),
                                                              in1=rstd[:, 0:ntok], op0=ALU.mult, op1=ALU.mult),
                 ["xT", "rstd", "par"], ["hT"], stream=True)

    def tokmajor_mm(s, slot, ncols=WS):
        b = nps()
        for c in range(NCH):
            P.op("pe", lambda e, c=c, b=b: e.matmul(ps[b][:, 0:ncols], lhsT=hT[:, c, s * 128:(s + 1) * 128], rhs=wsl[slot][:, c, 0:ncols],
                                                    start=(c == 0), stop=(c == NCH - 1)),
                 (["hT"] + wres(slot)), [PSN[b]])
        return b

    def featmajor_mm(srcT, srcname, nk, slot, blk, ntok, wchunk0=0):
        b = nps()
        for c in range(nk):
            P.op("pe", lambda e, c=c, b=b: e.matmul(ps[b][:, 0:ntok], lhsT=wsl[slot][:, c, blk * 128:(blk + 1) * 128],
                                                    rhs=srcT[:, wchunk0 + c, 0:ntok], start=(c == 0), stop=(c == nk - 1)),
                 ([srcname] + wres(slot)), [PSN[b]])
        return b

    def rope_evac(b, dst, dstname, rsub, nblk):
        w = nblk * 64
        copy_op("act", dst, ps[b][:, 0:w], [PSN[b]], [dstname])
        if not C_ROPE:
            return
        dv = dst.rearrange("p (b d) -> p b d", d=64)
        pv = dv
        cosb = rope[:, rsub, 0:8].unsqueeze(1).to_broadcast([128, nblk, 8])
        sinb = rope[:, rsub, 8:16].unsqueeze(1).to_broadcast([128, nblk, 8])
        x1, x2 = pv[:, :, 0:8], pv[:, :, 8:16]
        r = [t[:, 0:nblk, :] for t in rtmp]
        rn = ["rtmp%d" % i for i in range(4)]
        P.op("dve", lambda e: e.tensor_tensor(out=r[0], in0=x1, in1=cosb, op=ALU.mult), [dstname, "rope"], [rn[0]], stream=False)
        P.op("dve", lambda e: e.tensor_tensor(out=r[1], in0=x2, in1=sinb, op=ALU.mult), [dstname, "rope"], [rn[1]], stream=False)
        P.op("dve", lambda e: e.tensor_tensor(out=r[2], in0=x2, in1=cosb, op=ALU.mult), [dstname, "rope"], [rn[2]], stream=False)
        P.op("dve", lambda e: e.tensor_tensor(out=r[3], in0=x1, in1=sinb, op=ALU.mult), [dstname, "rope"], [rn[3]], stream=False)
        P.op("dve", lambda e: e.tensor_tensor(out=dv[:, :, 0:8], in0=r[0], in1=r[1], op=ALU.subtract), [rn[0], rn[1], dstname], [dstname], stream=False)
        P.op("dve", lambda e: e.tensor_tensor(out=dv[:, :, 8:16], in0=r[2], in1=r[3], op=ALU.add), [rn[2], rn[3], dstname], [dstname], stream=False)

    def transpose8(srcbf, srcname, dst, dstname, nrows=128):
        b = nps()
        pb = ps[b][:, :].bitcast(BF16)
        for h in range(NH):
            P.op("pe", lambda e, h=h: e.transpose(out=pb[:, h * 128:h * 128 + nrows], in_=srcbf[0:nrows, h * 128:(h + 1) * 128],
                                                  identity=identb[0:nrows, 0:nrows]),
                 [srcname, "identb"], [PSN[b]])
        pb3 = pb.rearrange("p (h t) -> p h t", t=128)[:, :, 0:nrows]
        if isinstance(dst, tuple):
            eng = evac_eng()
            copy_op(eng, dst[0], pb3[0:64], [PSN[b]], [dstname])
            copy_op(eng, dst[1], pb3[64:128], [PSN[b]], [dstname])
        else:
            copy_op(evac_eng(), dst, pb3, [PSN[b]], [dstname])

    def qkv_phase(tslot, rsub0, nsub, kout, vout, orow0, do_q, flag_o=None, ntok_rows=128):
        groups = ([("q", 0), ("q", 1), ("q", 2), ("q", 3)] if do_q else []) + [("k", 0), ("k", 1), ("k", 2), ("k", 3)] + [("v", 0), ("v", 1), ("v", 2), ("v", 3)]
        groups = groups[:cfg.get('qg', 99)]
        for (kind, g) in groups:
            c0 = {"q": 0, "k": 1024, "v": 2048}[kind] + g * WS
            slot = load_w(w_in, 0, NCH, c0)
            for s in range(nsub):
                kst = kvst[s][:, 0:1024]
                vst = kvst[s][:, 1024:2048]
                kvn = "kvst%d" % s
                b = tokmajor_mm(s, slot)
                if kind == "q":
                    rope_evac(b, qtmps[s][:, g * WS:(g + 1) * WS], "qtmp%d" % s, rsub0 + s, WS // 64)
                elif kind == "k":
                    rope_evac(b, kst[:, g * WS:(g + 1) * WS], kvn, rsub0 + s, WS // 64)
                else:
                    copy_op("act", vst[:, g * WS:(g + 1) * WS], ps[b][:, 0:WS], [PSN[b]], [kvn])
                    copy_op("dve", vbfs[s][:, g * WS:(g + 1) * WS], vst[:, g * WS:(g + 1) * WS], [kvn], ["vbf%d" % s])
        for s in range(nsub):
            kv = s
            kst = kvst[s][:, 0:1024]
            vst = kvst[s][:, 1024:2048]
            kvn = "kvst%d" % s
            qtmp = qtmps[s]
            vbf = vbfs[s]
            vbn = "vbf%d" % s
            if do_q:
                copy_op("dve", qbf[:], qtmp[:], ["qtmp%d" % s], ["qbf"])
                transpose8(qbf, "qbf", (QT[0:64, 0, :, s * 128:(s + 1) * 128], QT[64:128, 1, :, s * 128:(s + 1) * 128]), "QT")
            copy_op("act", kbf[:], kst, [kvn], ["kbf"])
            if kout is not None:
                P.op("act", lambda e, s=s, kst=kst: e.dma_start(out=kout[orow0 + s * 128: orow0 + (s + 1) * 128, :], in_=kst),
                     reads=[kvn], dma="ok%d" % s)
                P.op("act", lambda e, s=s, vst=vst: e.dma_start(out=vout[orow0 + s * 128: orow0 + (s + 1) * 128, :], in_=vst),
                     reads=[kvn], dma="ov%d" % s)
            if tslot is not None and C_TR:
                transpose8(kbf, "kbf", ktr[:, :, :], "ktr")
            if tslot is not None and C_SCR:
                for hh in (0, 4):
                    P.op("act", lambda e, s=s, hh=hh: e.dma_start(out=kT_scr[hh:hh + 4, :, tslot, s * 128:(s + 1) * 128].rearrange("h p t -> p h t"), in_=ktr[:, hh:hh + 4, :]),
                         reads=["ktr"], writes=["kTs%d" % tslot], dma="sk", join=True)
                    P.op("act", lambda e, s=s, hh=hh, vbf=vbf: e.dma_start(out=v_scr[hh:hh + 4, :, tslot, s, :].rearrange("h p e -> p h e"),
                                                                 in_=vbf[:, hh * 128:(hh + 4) * 128].rearrange("p (h e) -> p h e", e=128)),
                         reads=[vbn], writes=["vs%d" % tslot], dma="sv%d" % s, join=True)
                if flag_o is not None:
                    P.op("dve", lambda e, vbf=vbf: e.tensor_scalar(out=vmf[:], in0=vbf[:], scalar1=flg[:, flag_o:flag_o + 1], scalar2=None, op0=ALU.mult),
                         [vbn, "flg"], ["vmf"], stream=True)
                    for hh in (0, 4):
                        P.op("act", lambda e, s=s, hh=hh: e.dma_start(out=vm_scr[hh:hh + 4, :, flag_o, s, :].rearrange("h p e -> p h e"),
                                                                     in_=vmf[:, hh * 128:(hh + 4) * 128].rearrange("p (h e) -> p h e", e=128)),
                             reads=["vmf"], writes=["vms%d" % flag_o], dma="sm", join=True)

    def glu_phase(ntok, dstfn, dstname, inview=lambda a: a):
        for g in range(CC // WS):
            sa = load_w(w_in, 0, NCH, 3072 + g * WS)
            sbb = load_w(w_in, 0, NCH, 4096 + g * WS)
            GL = cfg.get('glu', 4)
            for blk in range(WS // 128):
                cb = g * (WS // 128) + blk
                if GL < 2:
                    continue
                ba = featmajor_mm(hT, "hT", NCH, sa, blk, ntok)
                bb = featmajor_mm(hT, "hT", NCH, sbb, blk, ntok)
                if GL < 3:
                    continue
                P.op("act", lambda e, bb=bb: e.activation(out=tmpf[2][:, 0:ntok], in_=ps[bb][:, 0:ntok], func=AF.Sigmoid),
                     [PSN[bb]], ["tmpf2"], stream=True)
                if GL < 4:
                    continue
                if cfg.get('gdst') == 'tmp':
                    P.op("dve", lambda e, ba=ba, cb=cb: e.tensor_tensor(out=tmpf[3][:, 0:ntok], in0=ps[ba][:, 0:ntok], in1=tmpf[2][:, 0:ntok], op=ALU.mult),
                         [PSN[ba], "tmpf2"], ["tmpf3"], stream=True)
                    continue
                if cfg.get('gdst') == 'sb':
                    copy_op("act", tmpf[3][:, 0:ntok], ps[ba][:, 0:ntok], [PSN[ba]], ["tmpf3"])
                    P.op("dve", lambda e, ba=ba, cb=cb: e.tensor_tensor(out=dstfn(cb), in0=inview(tmpf[3][:, 0:ntok]), in1=inview(tmpf[2][:, 0:ntok]), op=ALU.mult),
                         ["tmpf3", "tmpf2"], [dstname], stream=True)
                    continue
                P.op("dve", lambda e, ba=ba, cb=cb: e.tensor_tensor(out=dstfn(cb), in0=inview(ps[ba][:, 0:ntok]), in1=inview(tmpf[2][:, 0:ntok]), op=ALU.mult),
                     [PSN[ba], "tmpf2"], [dstname], stream=True)

    def conv_phase(extv, accv, ntok_shape):
        for cb in range(8):
            P.op("dve", lambda e, cb=cb: e.tensor_scalar(out=accv(cb), in0=extv(cb, 0), scalar1=col(P_WDW + cb), scalar2=col(P_BDW + cb),
                                                          op0=ALU.mult, op1=ALU.add), ["ext", "par"], ["acc%d" % cb], stream=True)
        for k in range(1, 31):
            for cb in range(8):
                P.op("dve", lambda e, cb=cb, k=k: e.scalar_tensor_tensor(out=accv(cb), in0=extv(cb, k), scalar=col(P_WDW + k * 8 + cb),
                                                                          in1=accv(cb), op0=ALU.mult, op1=ALU.add),
                     ["ext", "acc%d" % cb, "par"], ["acc%d" % cb], stream=True)

    def conv_norm(ntok):
        b1 = nps()
        b2 = nps()
        for cb in range(8):
            P.op("pe", lambda e, cb=cb: e.matmul(ps[b1][:, 0:ntok], lhsT=onesF, rhs=acc[:, cb, 0:ntok], start=(cb == 0), stop=(cb == 7)),
                 ["acc%d" % cb, "cst"], [PSN[b1]])
        for cb in range(8):
            t = cb % 2
            P.op("act", lambda e, cb=cb, t=t: e.activation(out=tmpf[t][:, 0:ntok], in_=acc[:, cb, 0:ntok], func=AF.Square),
                 ["acc%d" % cb], ["tmpf%d" % t], stream=True)
            P.op("pe", lambda e, cb=cb, t=t: e.matmul(ps[b2][:, 0:ntok], lhsT=onesF, rhs=tmpf[t][:, 0:ntok], start=(cb == 0), stop=(cb == 7)),
                 ["tmpf%d" % t, "cst"], [PSN[b2]])
        P.op("dve", lambda e: e.tensor_scalar(out=mean[:, 0:ntok], in0=ps[b1][:, 0:ntok], scalar1=1.0 / CC, scalar2=None, op0=ALU.mult),
             [PSN[b1]], ["mean"], stream=False)
        P.op("dve", lambda e: e.tensor_tensor(out=tmpf[3][:, 0:ntok], in0=mean[:, 0:ntok], in1=mean[:, 0:ntok], op=ALU.mult),
             ["mean"], ["tmpf3"], stream=False)
        P.op("dve", lambda e: e.scalar_tensor_tensor(out=rstd[:, 0:ntok], in0=ps[b2][:, 0:ntok], scalar=1.0 / CC, in1=tmpf[3][:, 0:ntok],
                                                     op0=ALU.mult, op1=ALU.subtract), [PSN[b2], "tmpf3"], ["rstd"], stream=False)
        rsqrt_op(rstd[:, 0:ntok], rstd[:, 0:ntok], ["rstd"], "rstd", 1.0)
        for cb in range(8):
            if cfg.get('convl', 3) < 3:
                break
            t = 2 + cb % 2
            P.op("dve", lambda e, cb=cb, t=t: e.tensor_tensor(out=tmpf[t][:, 0:ntok], in0=acc[:, cb, 0:ntok], in1=mean[:, 0:ntok], op=ALU.subtract),
                 ["acc%d" % cb, "mean"], ["tmpf%d" % t], stream=False)
            P.op("dve", lambda e, cb=cb, t=t: e.tensor_tensor(out=tmpf[t][:, 0:ntok], in0=tmpf[t][:, 0:ntok], in1=rstd[:, 0:ntok], op=ALU.mult),
                 ["tmpf%d" % t, "rstd"], ["tmpf%d" % t], stream=False)
            if cfg.get('silu', 'split') == 'fused':
                P.op("act", lambda e, cb=cb, t=t: e.activation(out=cnT[:, cb, 0:ntok], in_=tmpf[t][:, 0:ntok], func=AF.Silu,
                                                               bias=col(P_BCN + cb), scale=col(P_GCN + cb)),
                     ["tmpf%d" % t, "par"], ["cnT"], stream=False)
            else:
                P.op("act", lambda e, cb=cb, t=t: e.activation(out=tmpf[t][:, 0:ntok], in_=tmpf[t][:, 0:ntok], func=AF.Identity,
                                                               bias=col(P_BCN + cb), scale=col(P_GCN + cb)),
                     ["tmpf%d" % t, "par"], ["tmpf%d" % t], stream=False)
                P.op("act", lambda e, cb=cb, t=t: e.activation(out=tmpf[t - 2][:, 0:ntok], in_=tmpf[t][:, 0:ntok], func=AF.Sigmoid),
                     ["tmpf%d" % t], ["tmpf%d" % (t - 2)], stream=False)
                P.op("dve", lambda e, cb=cb, t=t: e.tensor_tensor(out=cnT[:, cb, 0:ntok], in0=tmpf[t][:, 0:ntok], in1=tmpf[t - 2][:, 0:ntok], op=ALU.mult),
                     ["tmpf%d" % t, "tmpf%d" % (t - 2)], ["cnT"], stream=False)

    def head_finish(o_dst, w):
        P.op("dve", lambda e: e.reciprocal(out=tmpf[0][:, 0:2 * w], in_=ps[1][:, 0:2 * w]), [PSN[1]], ["tmpf0"], stream=False)
        P.op("dve", lambda e: e.tensor_tensor(out=tmpf[1][:, 0:2 * w], in0=ps[0][:, 0:2 * w], in1=tmpf[0][:, 0:2 * w], op=ALU.mult),
             [PSN[0], "tmpf0"], ["tmpf1"], stream=False)
        P.op("dve", lambda e: e.scalar_tensor_tensor(out=tmpf[2][:, 0:w], in0=tmpf[1][:, w:2 * w], scalar=neglam, in1=tmpf[1][:, 0:w],
                                                     op0=ALU.mult, op1=ALU.add), ["tmpf1", "sc"], ["tmpf2"], stream=False)
        P.op("act", lambda e: e.activation(out=tmpf[3][:, 0:w], in_=tmpf[2][:, 0:w], func=AF.Square), ["tmpf2"], ["tmpf3"], stream=False)
        b = nps()
        P.op("pe", lambda e: e.matmul(ps[b][:, 0:w], lhsT=onesF, rhs=tmpf[3][:, 0:w], start=True, stop=True), ["tmpf3", "cst"], [PSN[b]])
        rsqrt_op(tmpf[0][:, 0:w], ps[b][:, 0:w], [PSN[b]], "tmpf0", 1.0 / 128)
        P.op("dve", lambda e: e.scalar_tensor_tensor(out=o_dst, in0=tmpf[2][:, 0:w], scalar=gsub, in1=tmpf[0][:, 0:w], op0=ALU.mult, op1=ALU.mult),
             ["tmpf2", "tmpf0", "sc"], ["oT"], stream=False)

    def prompt_attention(i):
        ktiles = [("o", o) for o in range(3 * i + 3)] + [("w", s) for s in range(i + 1)]
        ATL = cfg.get('attl', 4)
        for wdx in range(3):
            P.op("dve", lambda e, wdx=wdx: e.tensor_scalar(out=onesf[:, wdx, :], in0=onesF, scalar1=flg[:, 3 * i + wdx:3 * i + wdx + 1],
                                                           scalar2=None, op0=ALU.mult),
                 ["cst", "flg"], ["onesf"], stream=False)
        nkt = len(ktiles)
        for h in range(NH):
            first = True
            sbi = 0
            for c0 in range(0, nkt, KC):
                chunk = ktiles[c0:c0 + KC]
                cb_ = (c0 // KC) % 2
                kres, vres = "KTc%d" % cb_, "Vc%d" % cb_
                runs = []
                for idx, (kind, n) in enumerate(chunk):
                    ksl = n if kind == "o" else NOTH + n
                    vsrc = ("vm", n) if (kind == "o" and n >= 3 * i) else ("v", ksl)
                    runs.append((idx, ksl, vsrc))
                j = 0
                while j < len(runs):
                    j2 = j
                    while j2 + 1 < len(runs) and runs[j2 + 1][1] == runs[j2][1] + 1:
                        j2 += 1
                    a, bnd = runs[j][1], runs[j2][1] + 1
                    P.op("sp", lambda e, j=j, a=a, bnd=bnd, h=h, cb_=cb_: e.dma_start(out=KTc[cb_][:, j:j + bnd - a, :], in_=kT_scr[h, :, a:bnd, :]),
                         reads=["kTs%d" % t for t in range(a, bnd)], writes=[kres], dma="lk%d" % cb_)
                    j = j2 + 1
                j = 0
                while j < len(runs):
                    j2 = j
                    while j2 + 1 < len(runs) and runs[j2 + 1][2][0] == runs[j2][2][0] and runs[j2 + 1][2][1] == runs[j2][2][1] + 1:
                        j2 += 1
                    kindv, a = runs[j][2]
                    bnd = runs[j2][2][1] + 1
                    srcv = vm_scr if kindv == "vm" else v_scr
                    rn = [("vms%d" if kindv == "vm" else "vs%d") % t for t in range(a, bnd)]
                    P.op("sp", lambda e, j=j, a=a, bnd=bnd, h=h, cb_=cb_, srcv=srcv: e.dma_start(
                        out=Vc[cb_][:, j * NSB:(j + bnd - a) * NSB, :], in_=srcv[h, :, a:bnd, :, :].rearrange("p t s e -> p (t s) e")),
                         reads=rn, writes=[vres], dma="lv%d" % cb_)
                    j = j2 + 1
                for idx, (kind, n) in enumerate(chunk):
                    diag = (kind == "w" and n == i)
                    lones = onesf[:, n - 3 * i, :] if (kind == "o" and n >= 3 * i) else onesb[:, :]
                    lon = "onesf" if (kind == "o" and n >= 3 * i) else "onesb"
                    for sbk in range(NSB):
                        last = (c0 + idx == nkt - 1) and (sbk == NSB - 1)
                        q0 = sbk * 128 if diag else 0
                        wq = TT - q0
                        sbank = 2 + (sbi % 2)
                        pslot = sbi % 2
                        sbi += 1
                        sv = ps[sbank][:, :].rearrange("p (m q) -> p m q", m=2)
                        for m in range((cfg.get('nm', 2)) if ATL >= 2 else 0):
                            P.op("pe", lambda e, m=m, idx=idx, sbk=sbk, q0=q0, cb_=cb_, sbank=sbank, h=h: e.matmul(
                                ps[sbank][:, m * TT + q0:(m + 1) * TT],
                                lhsT=KTc[cb_][:, idx, sbk * 128:(sbk + 1) * 128],
                                rhs=QT[:, m, h, q0:TT], start=True, stop=True),
                                 [kres, "QT"], [PSN[sbank]])
                        if ATL < 2:
                            continue
                        pv = Pt[pslot][:, 0:2 * TT].rearrange("p (m q) -> p m q", m=2)
                        P.op("act", lambda e, q0=q0, pv=pv, sv=sv: e.activation(out=pv[:, :, q0:TT], in_=sv[:, :, q0:TT], func=AF.Exp, scale=0.125),
                             [PSN[sbank]], ["Pt%d" % pslot], stream=True)
                        if diag:
                            P.op("dve", lambda e, q0=q0, pv=pv: e.memset(pv[64:128, :, q0:q0 + 64], 0.0), ["Pt%d" % pslot], ["Pt%d" % pslot], stream=False)
                        if ATL < 3:
                            continue
                        if q0 == 0:
                            P.op("pe", lambda e, idx=idx, sbk=sbk, cb_=cb_, pslot=pslot, first=first, last=last: e.matmul(
                                ps[0][:, 0:2 * TT], lhsT=Vc[cb_][:, idx * NSB + sbk, :], rhs=Pt[pslot][:, 0:2 * TT], start=first, stop=last),
                                 [vres, "Pt%d" % pslot], [PSN[0]])
                            P.op("pe", lambda e, pslot=pslot, first=first, last=last, lones=lones: e.matmul(
                                ps[1][:, 0:2 * TT], lhsT=lones, rhs=Pt[pslot][:, 0:2 * TT], start=first, stop=last),
                                 [lon, "Pt%d" % pslot], [PSN[1]])
                        else:
                            for m in range(2):
                                lm = last and m == 1
                                P.op("pe", lambda e, m=m, idx=idx, sbk=sbk, cb_=cb_, pslot=pslot, q0=q0, lm=lm: e.matmul(
                                    ps[0][:, m * TT + q0:(m + 1) * TT], lhsT=Vc[cb_][:, idx * NSB + sbk, :],
                                    rhs=Pt[pslot][:, m * TT + q0:(m + 1) * TT], start=False, stop=lm),
                                     [vres, "Pt%d" % pslot], [PSN[0]])
                                P.op("pe", lambda e, m=m, pslot=pslot, q0=q0, lm=lm, lones=lones: e.matmul(
                                    ps[1][:, m * TT + q0:(m + 1) * TT], lhsT=lones, rhs=Pt[pslot][:, m * TT + q0:(m + 1) * TT], start=False, stop=lm),
                                     [lon, "Pt%d" % pslot], [PSN[1]])
                        first = False
            if ATL >= 4:
                head_finish(oT[:, h, 0:TT], TT)

    def sample_attention():
        for bt in range(4):
            first = True
            for kb in range(33):
                nk = 128 if kb < 32 else 32
                cv_ = kb % 2
                if kb < 32:
                    load_cast(ckb, cache_k[bt, kb * 128:(kb + 1) * 128, :], "KTc0")
                    load_cast(cvb[cv_], cache_v[bt, kb * 128:(kb + 1) * 128, :], "Vc%d" % cv_)
                    transpose8(ckb, "KTc0", ckT, "KTc1")
                    kTsrc, kTname, kc0 = ckT, "KTc1", 0
                    vsrc, vname = cvb[cv_], "Vc%d" % cv_
                else:
                    kTsrc, kTname, kc0 = ktr, "ktr", bt * 32
                    P.op("sp", lambda e, bt=bt, cv_=cv_: e.dma_start(out=cvb[cv_][0:32, :], in_=vbf[bt * 32:(bt + 1) * 32, :]),
                         reads=["vbf0"], writes=["Vc%d" % cv_], dma="mv%d" % cv_)
                    vsrc, vname = cvb[cv_], "Vc%d" % cv_
                sbank = 2 + (kb % 2)
                pslot = kb % 2
                for h in range(NH):
                    for m in range(2):
                        cidx = (h * 2 + m) * 32
                        P.op("pe", lambda e, h=h, m=m, cidx=cidx, nk=nk, sbank=sbank, bt=bt, kTsrc=kTsrc, kc0=kc0: e.matmul(
                            ps[sbank][0:nk, cidx:cidx + 32], lhsT=kTsrc[:, h, kc0:kc0 + nk],
                            rhs=QT[:, m, h, bt * 32:(bt + 1) * 32], start=True, stop=True),
                             [kTname, "QT"], [PSN[sbank]])
                P.op("act", lambda e, nk=nk, sbank=sbank, pslot=pslot: e.activation(out=Pt[pslot][0:nk, :], in_=ps[sbank][0:nk, :], func=AF.Exp, scale=0.125),
                     [PSN[sbank]], ["Pt%d" % pslot], stream=True)
                last = kb == 32
                if first:
                    P.op("pe", lambda e: e.matmul(ps[0][:, :], lhsT=onesb[:, :], rhs=zrhs[:, :], start=True, stop=False), ["onesb", "zrhs"], [PSN[0]])
                for h in range(NH):
                    cidx = h * 64
                    P.op("pe", lambda e, h=h, cidx=cidx, nk=nk, pslot=pslot, first=first, last=last, vsrc=vsrc: e.matmul(
                        ps[0][:, cidx:cidx + 64], lhsT=vsrc[0:nk, h * 128:(h + 1) * 128], rhs=Pt[pslot][0:nk, cidx:cidx + 64],
                        start=False, stop=(last and h == NH - 1)), [vname, "Pt%d" % pslot], [PSN[0]])
                P.op("pe", lambda e, nk=nk, pslot=pslot, first=first, last=last: e.matmul(
                    ps[1][:, :], lhsT=onesb[0:nk, :], rhs=Pt[pslot][0:nk, :], start=first, stop=last), ["onesb", "Pt%d" % pslot], [PSN[1]])
                first = False
            P.op("dve", lambda e: e.reciprocal(out=tmpf[0][:, :], in_=ps[1][:, :]), [PSN[1]], ["tmpf0"], stream=False)
            P.op("dve", lambda e: e.tensor_tensor(out=tmpf[1][:, :], in0=ps[0][:, :], in1=tmpf[0][:, :], op=ALU.mult), [PSN[0], "tmpf0"], ["tmpf1"], stream=False)
            t1 = tmpf[1][:, :].rearrange("p (h m q) -> p h m q", m=2, q=32)
            o2 = tmpf[2][:, 0:256].rearrange("p (h q) -> p h q", q=32)
            P.op("dve", lambda e, t1=t1, o2=o2: e.scalar_tensor_tensor(out=o2, in0=t1[:, :, 1, :], scalar=neglam, in1=t1[:, :, 0, :], op0=ALU.mult, op1=ALU.add),
                 ["tmpf1", "sc"], ["tmpf2"], stream=False)
            P.op("act", lambda e: e.activation(out=tmpf[3][:, 0:256], in_=tmpf[2][:, 0:256], func=AF.Square), ["tmpf2"], ["tmpf3"], stream=False)
            b = nps()
            P.op("pe", lambda e, b=b: e.matmul(ps[b][:, 0:256], lhsT=onesF, rhs=tmpf[3][:, 0:256], start=True, stop=True), ["tmpf3", "cst"], [PSN[b]])
            rsqrt_op(tmpf[0][:, 0:256], ps[b][:, 0:256], [PSN[b]], "tmpf0", 1.0 / 128)
            r0 = tmpf[0][:, 0:256].rearrange("p (h q) -> p h q", q=32)
            P.op("dve", lambda e, bt=bt, o2=o2, r0=r0: e.scalar_tensor_tensor(out=oT[:, :, bt * 32:(bt + 1) * 32], in0=o2, scalar=gsub, in1=r0, op0=ALU.mult, op1=ALU.mult),
                 ["tmpf2", "tmpf0", "sc"], ["oT"], stream=False)

    def merge_phase(ntok):
        for jg in range(D // WS):
            s_ao = load_w(w_ao, 0, 8, jg * WS)
            for blk in range(2):
                b = featmajor_mm(oT, "oT", 8, s_ao, blk, ntok)
                copy_op("act", tA[:, blk, 0:ntok], ps[b][:, 0:ntok], [PSN[b]], ["tA"])
            s_ga = load_w(w_in, 0, NCH, 5120 + jg * WS)
            for blk in range(2):
                b = featmajor_mm(hT, "hT", NCH, s_ga, blk, ntok)
                P.op("act", lambda e, b=b: e.activation(out=tmpf[2][:, 0:ntok], in_=ps[b][:, 0:ntok], func=AF.Sigmoid), [PSN[b]], ["tmpf2"], stream=True)
                P.op("dve", lambda e, blk=blk: e.tensor_tensor(out=tA[:, blk, 0:ntok], in0=tA[:, blk, 0:ntok], in1=tmpf[2][:, 0:ntok], op=ALU.mult),
                     ["tA", "tmpf2"], ["tA"], stream=True)
            s_co = load_w(w_co, 0, 8, jg * WS)
            for blk in range(2):
                j = jg * 2 + blk
                b = featmajor_mm(cnT, "cnT", 8, s_co, blk, ntok)
                P.op("dve", lambda e, b=b, blk=blk, j=j: e.tensor_scalar(out=tB[:, blk, 0:ntok], in0=ps[b][:, 0:ntok], scalar1=col(P_BCO + j), scalar2=None, op0=ALU.add),
                     [PSN[b], "par"], ["tB"], stream=True)
            s_gb = load_w(w_in, 0, NCH, 7168 + jg * WS)
            for blk in range(2):
                j = jg * 2 + blk
                b = featmajor_mm(hT, "hT", NCH, s_gb, blk, ntok)
                P.op("act", lambda e, b=b: e.activation(out=tmpf[3][:, 0:ntok], in_=ps[b][:, 0:ntok], func=AF.Sigmoid), [PSN[b]], ["tmpf3"], stream=True)
                P.op("dve", lambda e, blk=blk: e.tensor_tensor(out=tB[:, blk, 0:ntok], in0=tB[:, blk, 0:ntok], in1=tmpf[3][:, 0:ntok], op=ALU.mult),
                     ["tB", "tmpf3"], ["tB"], stream=True)
                P.op("dve", lambda e, blk=blk, j=j: e.tensor_tensor(out=mrgT[:, j, 0:ntok], in0=tA[:, blk, 0:ntok], in1=tB[:, blk, 0:ntok], op=ALU.add),
                     ["tA", "tB"], ["mrgT", "KTc0", "KTc1"], stream=True)

    def proj_norm_residual(srcT, srcname, nk_total, wsrc, gbase, ntok):
        for jg in range(D // WS):
            kgs = [(k0, min(NCH, nk_total - k0)) for k0 in range(0, nk_total, NCH)]
            bs = [nps(), nps()]
            for gi, (k0, nk) in enumerate(kgs):
                slot = load_w(wsrc, k0, nk, jg * WS)
                for blk in range(2):
                    b = bs[blk]
                    for c in range(nk):
                        P.op("pe", lambda e, c=c, b=b, blk=blk, slot=slot, k0=k0, gi=gi, nk=nk: e.matmul(
                            ps[b][:, 0:ntok], lhsT=wsl[slot][:, c, blk * 128:(blk + 1) * 128], rhs=srcT[:, k0 + c, 0:ntok],
                            start=(gi == 0 and c == 0), stop=(gi == len(kgs) - 1 and c == nk - 1)),
                             (([srcname] + wres(slot)) + (["ext", "cnT"] + ACCN if srcname == "actT" else ["KTc0", "KTc1"] if srcname == "mrgT" else [])), [PSN[b]])
            for blk in range(2):
                copy_op(evac_eng(), bufA[:, jg * 2 + blk, 0:ntok], ps[bs[blk]][:, 0:ntok], [PSN[bs[blk]]], ["bufA"])
        norm_stats(bufA, "bufA", NCH, ntok, 1.0 / D, None)
        for c in range(NCH):
            P.op("dve", lambda e, c=c: e.scalar_tensor_tensor(out=bufA[:, c, 0:ntok], in0=bufA[:, c, 0:ntok], scalar=col(gbase + c), in1=rstd[:, 0:ntok],
                                                              op0=ALU.mult, op1=ALU.mult), ["bufA", "rstd", "par"], ["bufA"], stream=True)
            P.op("dve", lambda e, c=c: e.tensor_tensor(out=xT[:, c, 0:ntok], in0=xT[:, c, 0:ntok], in1=bufA[:, c, 0:ntok], op=ALU.add),
                 ["bufA", "xT"], ["xT"], stream=True)

    def ffn_act(ntok):
        for fg in range(DFF // WS):
            sg = load_w(w_fg, 0, NCH, fg * WS)
            su = load_w(w_fu, 0, NCH, fg * WS)
            for blk in range(2):
                bg = featmajor_mm(hT, "hT", NCH, sg, blk, ntok)
                bu = featmajor_mm(hT, "hT", NCH, su, blk, ntok)
                P.op("act", lambda e, bg=bg: e.activation(out=tmpf[2][:, 0:ntok], in_=ps[bg][:, 0:ntok], func=AF.Sigmoid), [PSN[bg]], ["tmpf2"], stream=True)
                P.op("dve", lambda e, bg=bg: e.tensor_tensor(out=tmpf[2][:, 0:ntok], in0=ps[bg][:, 0:ntok], in1=tmpf[2][:, 0:ntok], op=ALU.mult),
                     [PSN[bg], "tmpf2"], ["tmpf2"], stream=True)
                P.op("dve", lambda e, bu=bu, fg=fg, blk=blk: e.tensor_tensor(out=actT[:, fg * 2 + blk, 0:ntok], in0=ps[bu][:, 0:ntok], in1=tmpf[2][:, 0:ntok], op=ALU.mult),
                     [PSN[bu], "tmpf2"], ["actT", "ext", "cnT"] + ACCN, stream=True)

    def store_y(ydst, row0, nsub):
        for s in range(nsub):
            xi = 0
            for c4 in range(4):
                b = nps()
                for k in range(4):
                    c = c4 * 4 + k
                    P.op("pe", lambda e, c=c, k=k, b=b, s=s: e.transpose(out=ps[b][:, k * 128:(k + 1) * 128], in_=xT[:, c, s * 128:(s + 1) * 128], identity=identf),
                         ["xT", "cst"], [PSN[b]])
                copy_op(evac_eng(), xin[xi][:, c4 * 512:(c4 + 1) * 512], ps[b][:, :], [PSN[b]], ["xin%d" % xi])
            P.op("act", lambda e, xi=xi, s=s: e.dma_start(out=ydst[row0 + s * 128: row0 + (s + 1) * 128, :], in_=xin[xi][:]),
                 reads=["xin%d" % xi], dma="y%d" % xi)

    def layer_tail(ntok):
        merge_phase(ntok)
        proj_norm_residual(mrgT, "mrgT", NCH, w_o, P_GPOST, ntok)
        norm_stats(xT, "xT", NCH, ntok, 1.0 / D, None)
        for c in range(NCH):
            P.op("dve", lambda e, c=c: e.scalar_tensor_tensor(out=hT[:, c, 0:ntok], in0=xT[:, c, 0:ntok], scalar=col(P_GFFN + c), in1=rstd[:, 0:ntok],
                                                              op0=ALU.mult, op1=ALU.mult), ["xT", "rstd", "par"], ["hT"], stream=True)
        ffn_act(ntok)
        proj_norm_residual(actT, "actT", NFC, w_fd, P_GPFF, ntok)

    NHS = (NOWN * 32) // 128
    halov = halo[:, :, :, :].rearrange("p c i t -> p c (i t)")
    C_ST = cfg.get('stage', 3)
    if C_HALO:
        if C_ST >= 1:
            load_xT(x_halo, 0, NHS, NHS * 128)
        if C_ST >= 2:
            make_hT(P_GPRE, NHS * 128)
        if C_ST >= 3:
            glu_phase(NHS * 128, lambda cb: halov[:, cb, :], "halo")

    for o in range(C_NOTH):
        load_xT(x_oth, o * TT, NSB, TT)
        make_hT(P_GPRE, TT)
        qkv_phase(o, o * NSB, NSB, None, None, 0, False, flag_o=o)

    for i in range(C_NOWN):
        load_xT(x_own, i * TT, NSB, TT)
        make_hT(P_GPRE, TT)
        qkv_phase(NOTH + i, (NOTH + i) * NSB, NSB, k_own, v_own, i * TT, True)
        for cb in range(8):
            copy_op("act", ext[:, cb, 0:32], halo[:, cb, i, :], ["halo"], ["ext"], stream=False)
        glu_phase(TT, lambda cb: ext[:, cb, 32:32 + TT], "ext")
        if i == NOWN - 1:
            b = nps()
            for cb in range(8):
                P.op("pe", lambda e, cb=cb, b=b: e.transpose(out=ps[b][0:32, (cb % 4) * 128:(cb % 4 + 1) * 128],
                                                             in_=ext[:, cb, TT:TT + 32], identity=identf), ["ext", "cst"], [PSN[b]])
                if cb % 4 == 3:
                    copy_op("dve", stc[0:32, (cb - 3) * 128:(cb + 1) * 128], ps[b][0:32, :], [PSN[b]], ["kvst0"], stream=False)
                    if cb == 3:
                        b = nps()
            P.op("act", lambda e: e.dma_start(out=conv_p[:, :], in_=stc[2:32, :]), reads=["kvst0"], dma="oc")
        if cfg.get('conv', True):
            conv_phase(lambda cb, k: ext[:, cb, 2 + k:2 + k + TT], lambda cb: acc[:, cb, 0:TT], None)
            if cfg.get('convl', 3) >= 2:
                conv_norm(TT)
        if cfg.get('attn', True):
            prompt_attention(i)
        if C_TAIL:
            layer_tail(TT)
        store_y(y_own, i * TT, NSB)

    if C_SMP:
        load_xT(x_smp, 0, 1, 128)
        make_hT(P_GPRE, 128)
        qkv_phase(None, NTL * NSB, 1, k_s, v_s, 0, True)
        transpose8(kbf, "kbf", ktr[:, :, :], "ktr")
        extS = ext[:, :, 0:248].rearrange("p c (b t) -> p c b t", t=62)
        for bt in range(4):
            P.op("sp", lambda e, bt=bt: e.dma_start(out=stc[0:30, :], in_=state_conv[bt, :, :]), writes=["kvst0"], dma="sc")
            b = nps()
            for cb in range(8):
                P.op("pe", lambda e, cb=cb, b=b: e.transpose(out=ps[b][:, cb * 32:cb * 32 + 30], in_=stc[0:30, cb * 128:(cb + 1) * 128], identity=identf[0:30, 0:30]),
                     ["kvst0", "cst"], [PSN[b]])
            copy_op("dve", extS[:, :, bt, 0:30], ps[b][:, 0:256].rearrange("p (c t) -> p c t", t=32)[:, :, 0:30], [PSN[b]], ["ext"], stream=False)
        glu_phase(128, lambda cb: extS[:, cb, :, 30:62], "ext", inview=lambda a: a.rearrange("p (b t) -> p b t", t=32))
        for bt in range(4):
            b = nps()
            for cb in range(8):
                P.op("pe", lambda e, cb=cb, b=b, bt=bt: e.transpose(out=ps[b][0:32, (cb % 4) * 128:(cb % 4 + 1) * 128], in_=extS[:, cb, bt, 30:62], identity=identf),
                     ["ext", "cst"], [PSN[b]])
                if cb % 4 == 3:
                    copy_op("dve", stc[0:32, (cb - 3) * 128:(cb + 1) * 128], ps[b][0:32, :], [PSN[b]], ["kvst0"], stream=False)
                    if cb == 3:
                        b = nps()
            P.op("act", lambda e, bt=bt: e.dma_start(out=conv_s[bt, :, :], in_=stc[2:32, :]), reads=["kvst0"], dma="oc")
        conv_phase(lambda cb, k: extS[:, cb, :, k:k + 32], lambda cb: acc[:, cb, 0:128].rearrange("p (b t) -> p b t", t=32), None)
        conv_norm(128)
        sample_attention()
        layer_tail(128)
        store_y(y_s, 0, 1)

    print("ops:", {e: len(v) for e, v in P.ops.items()}, flush=True)
    with nc.Block() as block:
        semcms = P.emit(nc, block)
    return nc


_NC_CACHE = {}


def _rope_tab(pos):
    inv = (500000.0 ** (-np.arange(0, 16, 2, dtype=np.float32) / 16.0)).astype(np.float32)
    ang = pos.astype(np.float32)[:, None] * inv[None, :]
    return np.concatenate([np.cos(ang), np.sin(ang)], axis=1).astype(np.float32)


def prepare(inputs):
    f = lambda k: np.ascontiguousarray(np.asarray(inputs[k], dtype=np.float32))
    xp, xs = f("x_prompt"), f("x_sample")
    ck, cv, scv = f("cache_k")[0], f("cache_v")[0], f("state_conv")[0]
    par = np.zeros((128, NPAR), np.float32)
    colmaj = lambda v, n: np.ascontiguousarray(v.reshape(n, 128).T)
    par[:, P_GPRE:P_GPRE + 16] = colmaj(f("g_pre_mix")[0], 16)
    par[:, P_GPOST:P_GPOST + 16] = colmaj(f("g_post_mix")[0], 16)
    par[:, P_GFFN:P_GFFN + 16] = colmaj(f("g_pre_ffn")[0], 16)
    par[:, P_GPFF:P_GPFF + 16] = colmaj(f("g_post_ffn")[0], 16)
    par[:, P_BCO:P_BCO + 16] = colmaj(f("b_conv_out")[0], 16)
    par[:, P_GSUB] = f("g_subln")[0]
    wdw = f("w_dw")[0]
    par[:, P_WDW:P_WDW + 248] = wdw.reshape(31, 8, 128).transpose(2, 0, 1).reshape(128, 248)
    par[:, P_BDW:P_BDW + 8] = colmaj(f("b_dw")[0], 8)
    par[:, P_GCN:P_GCN + 8] = colmaj(f("g_conv_norm")[0], 8)
    par[:, P_BCN:P_BCN + 8] = colmaj(f("b_conv_norm")[0], 8)
    for n, key in enumerate(["lambda_q1", "lambda_k1", "lambda_q2", "lambda_k2"]):
        par[:, P_LAM + 64 * n:P_LAM + 64 * (n + 1)] = f(key)[0][None, :]
    consts = np.concatenate([np.eye(128, dtype=np.float32), np.ones((128, 128), np.float32)], axis=1)
    w = {"w_in": f("w_in")[0], "w_ao": f("w_attn_out")[0], "w_co": f("w_conv_out")[0], "w_o": f("w_o")[0],
         "w_fg": f("w_ffn_gate")[0], "w_fu": f("w_ffn_up")[0], "w_fd": f("w_ffn_down")[0]}
    in_maps = []
    own_tiles = {}
    for core in range(8):
        b, j = core // 4, core % 4
        own = [4 * i + j for i in range(NOWN)]
        oth = [4 * wd + k for wd in range(NOWN) for k in range(4) if k != j]
        own_tiles[core] = own
        xb = xp[b]
        x_oth = np.concatenate([xb[t * TT:(t + 1) * TT] for t in oth], axis=0)
        x_own = np.concatenate([xb[t * TT:(t + 1) * TT] for t in own], axis=0)
        x_halo = np.zeros((NOWN * 32, D), np.float32)
        for i, t in enumerate(own):
            if t > 0:
                x_halo[i * 32:(i + 1) * 32] = xb[t * TT - 32:t * TT]
        pos = np.concatenate([np.arange(t * TT, (t + 1) * TT) for t in oth + own] + [4096 + (np.arange(128) % 32)])
        rt = _rope_tab(pos).reshape(NTL * NSB + 1, 128, 16).transpose(1, 0, 2)
        flags = np.zeros((128, NOTH), np.float32)
        for o, t in enumerate(oth):
            flags[:, o] = 1.0 if (t % 4) < j else 0.0
        m = {"x_oth": x_oth, "x_own": x_own, "x_halo": x_halo,
             "x_smp": np.ascontiguousarray(xs[4 * core:4 * core + 4].reshape(128, D)),
             "rope": np.ascontiguousarray(rt), "flags": flags, "params": par, "consts": consts,
             "cache_k": np.ascontiguousarray(ck[4 * core:4 * core + 4].reshape(4, 4096, 1024)),
             "cache_v": np.ascontiguousarray(cv[4 * core:4 * core + 4].reshape(4, 4096, 1024)),
             "state_conv": np.ascontiguousarray(scv[4 * core:4 * core + 4])}
        m.update(w)
        in_maps.append(m)
    return in_maps, own_tiles


def kernel(**inputs):
    in_maps, own_tiles = prepare(inputs)
    if "nc" not in _NC_CACHE:
        _NC_CACHE["nc"] = build_nc()
    res = run_bass_kernel_spmd(_NC_CACHE["nc"], in_maps, core_ids=list(range(8)))
    R = res.results
    yp = np.zeros((2, SEQ, D), np.float32)
    kp = np.zeros((1, 2, SEQ, 1024), np.float32)
    vp = np.zeros((1, 2, SEQ, 1024), np.float32)
    cp = np.zeros((1, 2, 30, CC), np.float32)
    ys = np.zeros((32, 32, D), np.float32)
    ks = np.zeros((1, 32, 32, 1024), np.float32)
    vs = np.zeros((1, 32, 32, 1024), np.float32)
    cs = np.zeros((1, 32, 30, CC), np.float32)
    for core in range(8):
        b, j = core // 4, core % 4
        r = R[core]
        for i, t in enumerate(own_tiles[core]):
            yp[b, t * TT:(t + 1) * TT] = r["y_own"][i * TT:(i + 1) * TT]
            kp[0, b, t * TT:(t + 1) * TT] = r["k_own"][i * TT:(i + 1) * TT]
            vp[0, b, t * TT:(t + 1) * TT] = r["v_own"][i * TT:(i + 1) * TT]
        if j == 3:
            cp[0, b] = r["conv_p"]
        ys[4 * core:4 * core + 4] = r["y_s"].reshape(4, 32, D)
        ks[0, 4 * core:4 * core + 4] = r["k_s"].reshape(4, 32, 1024)
        vs[0, 4 * core:4 * core + 4] = r["v_s"].reshape(4, 32, 1024)
        cs[0, 4 * core:4 * core + 4] = r["conv_s"]
    return (yp, ys, kp.reshape(1, 2, SEQ, 8, 2, 64), vp.reshape(1, 2, SEQ, 8, 128), cp,
            ks.reshape(1, 32, 32, 8, 2, 64), vs.reshape(1, 32, 32, 8, 128), cs)
```
